# Optimizing a Trainium2 kernel written in Bass

```python
import math
import jax, jax.numpy as jnp
from jax import lax
import numpy as np

D_MODEL = 1024
BATCH = 16
SEQ = 256
DEPTH = 4
DEC_BATCH = 8
DEC_SEQ = 1024
PAST_LEN = 256

GRID_W = 64
N_MIXERS = 3
N_POOL = (DEPTH + 2) // 3
N_MLA = (DEPTH + 1) // 3
N_LRU = DEPTH // 3
POOL_WINDOWS = (2, 4, 8, 16)
POOL_GROUPS = 4
POOL_GC = D_MODEL // POOL_GROUPS
MLA_HEADS = 8
Q_LORA = 384
KV_LORA = 256
QK_NOPE = 128
QK_ROPE = 64
V_HEAD = 128
ROPE_NF = QK_ROPE // 4
ROPE_THETA = 10000.0
MLA_SCALE = (QK_NOPE + QK_ROPE) ** -0.5
DENSE_KEY_LIMIT = 2048
Q_BLOCK = 128
D_RNN = D_MODEL
LRU_BLOCKS = 8
LRU_BW = D_RNN // LRU_BLOCKS
CONV_W = 4
CONV_LEFT = 1
LRU_C = 8.0
D_FF = ((8 * D_MODEL // 3 + 255) // 256) * 256
ALPHA = (2.0 * DEPTH) ** 0.25
BETA = (8.0 * DEPTH) ** -0.25
EPS = 1e-6

kernel_name = 'hybrid_pool_mla_rglru_diffusion_step'


def layer_norm(x, g, b):
    x32 = x.astype(jnp.float32)
    mu = jnp.mean(x32, axis=-1, keepdims=True)
    var = jnp.mean(jnp.square(x32 - mu), axis=-1, keepdims=True)
    y = (x32 - mu) * lax.rsqrt(var + EPS)
    return (y * g.astype(jnp.float32) + b.astype(jnp.float32)).astype(x.dtype)


def rms_norm(x, g):
    x32 = x.astype(jnp.float32)
    y = x32 * lax.rsqrt(jnp.mean(jnp.square(x32), axis=-1, keepdims=True) + EPS)
    return (y * g.astype(jnp.float32)).astype(x.dtype)


def modulation(cond, w_ada, b_ada):
    m = (jax.nn.silu(cond) @ w_ada + b_ada)[:, None, :]
    return jnp.split(m, 6, axis=-1)


def swiglu(h, w_in, w_out):
    a, b = jnp.split(h @ w_in, 2, axis=-1)
    return (jax.nn.silu(a) * b) @ w_out


def rope_tables(T):
    t = jnp.arange(T)
    rows = (t // GRID_W).astype(jnp.float32)
    cols = (t % GRID_W).astype(jnp.float32)
    inv = ROPE_THETA ** (-jnp.arange(ROPE_NF, dtype=jnp.float32) / ROPE_NF)
    ang = jnp.stack([rows[:, None] * inv, cols[:, None] * inv], axis=1)
    return jnp.cos(ang), jnp.sin(ang)


def apply_rope(x, cos, sin):
    xr = x.reshape(x.shape[:-1] + (2, 2, ROPE_NF)).astype(jnp.float32)
    x1, x2 = xr[..., 0, :], xr[..., 1, :]
    out = jnp.stack([x1 * cos - x2 * sin, x2 * cos + x1 * sin], axis=-2)
    return out.reshape(x.shape).astype(x.dtype)


def pool_mix(x, w_pool, scale):
    B, T, D = x.shape
    x32 = x.astype(jnp.float32)
    csum = jnp.concatenate([jnp.zeros((B, 1, D), jnp.float32), jnp.cumsum(x32, axis=1)], axis=1)
    t = jnp.arange(T)
    groups = []
    for g, w in enumerate(POOL_WINDOWS):
        lo = jnp.clip(t - w // 2, 0, T)
        hi = jnp.clip(t + w - w // 2, 0, T)
        cg = csum[..., g * POOL_GC:(g + 1) * POOL_GC]
        cnt = (hi - lo).astype(jnp.float32)[:, None]
        groups.append((jnp.take(cg, hi, axis=1) - jnp.take(cg, lo, axis=1)) / cnt)
    pooled = jnp.stack(groups, axis=2) - x32.reshape(B, T, POOL_GROUPS, POOL_GC)
    out = jnp.einsum('btgc,gcd->btgd', pooled.astype(x.dtype), w_pool).reshape(B, T, D)
    return out * scale


def mla_project(x, w_dq, g_q, w_uq, w_dkv, g_kv):
    q = jnp.einsum('btr,rhe->bthe', rms_norm(x @ w_dq, g_q), w_uq)
    kv = x @ w_dkv
    c_kv = rms_norm(kv[..., :KV_LORA], g_kv)
    return q[..., :QK_NOPE], q[..., QK_NOPE:], c_kv, kv[..., KV_LORA:]


def mla_attend(q_nope, q_rope, c_kv, k_rope, w_uk, w_uv, w_o):
    B, T = q_nope.shape[:2]
    k_nope = jnp.einsum('bsr,rhe->bshe', c_kv, w_uk)
    v = jnp.einsum('bsr,rhe->bshe', c_kv, w_uv)

    def attend(qn, qr):
        s = jnp.einsum('bthe,bshe->bhts', qn, k_nope) + jnp.einsum('bthe,bse->bhts', qr, k_rope)
        p = jax.nn.softmax(s.astype(jnp.float32) * MLA_SCALE, axis=-1).astype(v.dtype)
        return jnp.einsum('bhts,bshe->bthe', p, v)

    if k_nope.shape[1] >= DENSE_KEY_LIMIT and T % Q_BLOCK == 0:
        nb = T // Q_BLOCK
        blk = lambda a: jnp.moveaxis(a.reshape((B, nb, Q_BLOCK) + a.shape[2:]), 1, 0)
        o = lax.map(lambda qs: attend(qs[0], qs[1]), (blk(q_nope), blk(q_rope)))
        o = jnp.moveaxis(o, 0, 1).reshape(B, T, MLA_HEADS, V_HEAD)
    else:
        o = attend(q_nope, q_rope)
    return o.reshape(B, T, MLA_HEADS * V_HEAD) @ w_o


def linear_scan(a, b, h0, reverse):
    if reverse:
        b = b.at[:, -1].add(a[:, -1] * h0)
    else:
        b = b.at[:, 0].add(a[:, 0] * h0)
    combine = lambda e1, e2: (e1[0] * e2[0], e2[0] * e1[1] + e2[1])
    _, h = lax.associative_scan(combine, (a, b), axis=1, reverse=reverse)
    return h


def lru_mixer(x, w_in, conv_w, conv_b, w_a, b_a, w_i, b_i, lam, w_out, h0_f, h0_b):
    B, T, _ = x.shape
    xy = x @ w_in
    u, y = xy[..., :D_RNN], jax.nn.gelu(xy[..., D_RNN:])
    up = jnp.pad(u, ((0, 0), (CONV_LEFT, CONV_W - 1 - CONV_LEFT), (0, 0)))
    u = sum(up[:, k:k + T] * conv_w[k] for k in range(CONV_W)) + conv_b
    u32 = u.astype(jnp.float32)
    ub = u32.reshape(B, T, LRU_BLOCKS, LRU_BW)
    r = jax.nn.sigmoid(jnp.einsum('btnc,dncm->dbtnm', ub, w_a.astype(jnp.float32)).reshape(2, B, T, D_RNN)
                       + b_a.astype(jnp.float32)[:, None, None])
    ig = jax.nn.sigmoid(jnp.einsum('btnc,dncm->dbtnm', ub, w_i.astype(jnp.float32)).reshape(2, B, T, D_RNN)
                        + b_i.astype(jnp.float32)[:, None, None])
    log_a = -LRU_C * r * jax.nn.softplus(-lam.astype(jnp.float32))[:, None, None]
    a = jnp.exp(log_a)
    bterm = jnp.sqrt(-jnp.expm1(2.0 * log_a)) * (ig * u32[None])
    h_f = linear_scan(a[0], bterm[0], h0_f.astype(jnp.float32), reverse=False)
    h_b = linear_scan(a[1], bterm[1], h0_b.astype(jnp.float32), reverse=True)
    out = ((h_f + h_b).astype(x.dtype) * y) @ w_out
    return out, h_f[:, -1].astype(x.dtype), h_b[:, 0].astype(x.dtype)


def setup_inputs(seed: int = 0) -> dict:
    key = jax.random.key(seed)
    ks = iter(jax.random.split(key, 40))
    nrm = lambda shape, s=1.0: jax.random.normal(next(ks), shape, jnp.float32) * s
    d = D_MODEL
    u = jax.random.uniform(next(ks), (N_LRU, 2, D_RNN), jnp.float32, 0.9, 0.999)
    a0 = u ** (1.0 / LRU_C)
    return {
        'x_prompt': nrm((BATCH, SEQ, d)),
        'x_sample': nrm((DEC_BATCH, DEC_SEQ, d)),
        'cache_mla_ckv': nrm((DEC_BATCH, N_MLA, PAST_LEN, KV_LORA)),
        'cache_mla_krope': nrm((DEC_BATCH, N_MLA, PAST_LEN, QK_ROPE)),
        'state_lru': nrm((DEC_BATCH, N_LRU, 2, D_RNN), 0.5),
        'c': nrm((DEC_BATCH, d)),
        'c_ctx': nrm((d,)),
        'w_ada': nrm((DEPTH, d, 6 * d), d ** -0.5),
        'b_ada': nrm((DEPTH, 6 * d), 0.02),
        'ln_g': 1.0 + nrm((DEPTH, 2, d), 0.02),
        'ln_b': nrm((DEPTH, 2, d), 0.02),
        'w_ffn_in': nrm((DEPTH, d, 2 * D_FF), d ** -0.5),
        'w_ffn_out': nrm((DEPTH, D_FF, d), BETA * D_FF ** -0.5),
        'w_pool': nrm((N_POOL, POOL_GROUPS, POOL_GC, POOL_GC), BETA * POOL_GC ** -0.5),
        'pool_scale': 1.0 + nrm((N_POOL, d), 0.02),
        'w_dq': nrm((N_MLA, d, Q_LORA), d ** -0.5),
        'g_q': 1.0 + nrm((N_MLA, Q_LORA), 0.02),
        'w_uq': nrm((N_MLA, Q_LORA, MLA_HEADS, QK_NOPE + QK_ROPE), Q_LORA ** -0.5),
        'w_dkv': nrm((N_MLA, d, KV_LORA + QK_ROPE), d ** -0.5),
        'g_kv': 1.0 + nrm((N_MLA, KV_LORA), 0.02),
        'w_uk': nrm((N_MLA, KV_LORA, MLA_HEADS, QK_NOPE), KV_LORA ** -0.5),
        'w_uv': nrm((N_MLA, KV_LORA, MLA_HEADS, V_HEAD), KV_LORA ** -0.5),
        'w_mla_o': nrm((N_MLA, MLA_HEADS * V_HEAD, d), BETA * (MLA_HEADS * V_HEAD) ** -0.5),
        'w_lru_in': nrm((N_LRU, d, 2 * D_RNN), d ** -0.5),
        'lru_conv_w': nrm((N_LRU, CONV_W, D_RNN), CONV_W ** -0.5),
        'lru_conv_b': nrm((N_LRU, D_RNN), 0.02),
        'w_lru_a': nrm((N_LRU, 2, LRU_BLOCKS, LRU_BW, LRU_BW), LRU_BW ** -0.5),
        'b_lru_a': nrm((N_LRU, 2, D_RNN), 0.02),
        'w_lru_i': nrm((N_LRU, 2, LRU_BLOCKS, LRU_BW, LRU_BW), LRU_BW ** -0.5),
        'b_lru_i': nrm((N_LRU, 2, D_RNN), 0.02),
        'lru_lambda': jnp.log(a0) - jnp.log1p(-a0),
        'w_lru_out': nrm((N_LRU, D_RNN, d), BETA * D_RNN ** -0.5),
    }


def reference(x_prompt, x_sample, cache_mla_ckv, cache_mla_krope, state_lru, c, c_ctx,
              w_ada, b_ada, ln_g, ln_b, w_ffn_in, w_ffn_out, w_pool, pool_scale,
              w_dq, g_q, w_uq, w_dkv, g_kv, w_uk, w_uv, w_mla_o,
              w_lru_in, lru_conv_w, lru_conv_b, w_lru_a, b_lru_a, w_lru_i, b_lru_i,
              lru_lambda, w_lru_out):
    x = x_prompt
    B0 = x.shape[0]
    ckv_list, krope_list, lru_list = [], [], []
    for i in range(DEPTH):
        sh1, sc1, g1, sh2, sc2, g2 = modulation(c_ctx[None], w_ada[i], b_ada[i])
        h = x * (1.0 + sc1) + sh1
        kind, j = i % N_MIXERS, i // N_MIXERS
        if kind == 0:
            out = pool_mix(h, w_pool[j], pool_scale[j])
        elif kind == 1:
            qn, qr, ckv, kr = mla_project(h, w_dq[j], g_q[j], w_uq[j], w_dkv[j], g_kv[j])
            out = mla_attend(qn, qr, ckv, kr, w_uk[j], w_uv[j], w_mla_o[j])
            ckv_list.append(ckv)
            krope_list.append(kr)
        else:
            h0 = jnp.zeros((B0, D_RNN), jnp.float32)
            out, hf, hb = lru_mixer(h, w_lru_in[j], lru_conv_w[j], lru_conv_b[j], w_lru_a[j], b_lru_a[j],
                                    w_lru_i[j], b_lru_i[j], lru_lambda[j], w_lru_out[j], h0, h0)
            lru_list.append(jnp.stack([hf, hb], axis=1))
        x = layer_norm(ALPHA * x + g1 * out, ln_g[i, 0], ln_b[i, 0])
        h = x * (1.0 + sc2) + sh2
        x = layer_norm(ALPHA * x + g2 * swiglu(h, w_ffn_in[i], w_ffn_out[i]), ln_g[i, 1], ln_b[i, 1])
    y_prompt = x
    new_mla_ckv = jnp.stack(ckv_list, axis=1)
    new_mla_krope = jnp.stack(krope_list, axis=1)
    new_lru_state = jnp.stack(lru_list, axis=1)

    x = x_sample
    T = x.shape[1]
    cos, sin = rope_tables(T)
    for i in range(DEPTH):
        sh1, sc1, g1, sh2, sc2, g2 = modulation(c, w_ada[i], b_ada[i])
        h = x * (1.0 + sc1) + sh1
        kind, j = i % N_MIXERS, i // N_MIXERS
        if kind == 0:
            out = pool_mix(h, w_pool[j], pool_scale[j])
        elif kind == 1:
            qn, qr, ckv, kr = mla_project(h, w_dq[j], g_q[j], w_uq[j], w_dkv[j], g_kv[j])
            qr = apply_rope(qr, cos[:, None], sin[:, None])
            kr = apply_rope(kr, cos, sin)
            ckv_all = jnp.concatenate([cache_mla_ckv[:, j].astype(ckv.dtype), ckv], axis=1)
            kr_all = jnp.concatenate([cache_mla_krope[:, j].astype(kr.dtype), kr], axis=1)
            out = mla_attend(qn, qr, ckv_all, kr_all, w_uk[j], w_uv[j], w_mla_o[j])
        else:
            out, _, _ = lru_mixer(h, w_lru_in[j], lru_conv_w[j], lru_conv_b[j], w_lru_a[j], b_lru_a[j],
                                  w_lru_i[j], b_lru_i[j], lru_lambda[j], w_lru_out[j],
                                  state_lru[:, j, 0], state_lru[:, j, 1])
        x = layer_norm(ALPHA * x + g1 * out, ln_g[i, 0], ln_b[i, 0])
        h = x * (1.0 + sc2) + sh2
        x = layer_norm(ALPHA * x + g2 * swiglu(h, w_ffn_in[i], w_ffn_out[i]), ln_g[i, 1], ln_b[i, 1])
    y_sample = x
    return (y_prompt, y_sample, new_mla_ckv, new_mla_krope, new_lru_state)
```

```python
import contextlib
import math
import numpy as np
import concourse.bass as bass
import concourse.mybir as mybir
from concourse.bass_utils import run_bass_kernel_spmd

F32 = mybir.dt.float32
BF16 = mybir.dt.bfloat16
ALU = mybir.AluOpType
AF = mybir.ActivationFunctionType

PE, ACT, DVE, POOL, SP = "tensor", "scalar", "vector", "gpsimd", "sync"
ENGS = [PE, ACT, DVE, POOL, SP]
NDS = 24

D = 1024
NT = 1536
DFF = 2816
NFC = 22
ALPHA = 8.0 ** 0.25
EPS_LN = 1e-6 / (ALPHA * ALPHA)
EPS = 1e-6
MLA_SCALE = 192.0 ** -0.5
PADW = 16
SEQS = [(0, 256), (256, 256), (512, 1024)]
POFF = [PADW, 288 + PADW, 576 + PADW]
NTP = 1632
TT = [(0, 512), (512, 512), (1024, 512)]
TCOND = [0, 1, 1]


class Op:
    __slots__ = ("eng", "fn", "deps", "is_dma", "dma_sem", "dma_val", "dma_prev", "target", "count", "idx")


class Prog:
    def __init__(self, nc):
        self.nc = nc
        self.ops = {e: [] for e in ENGS}
        self.last_w = {}
        self.readers = {}
        self.ndma = 0
        self.bar_deps = []
        self.bar_pending = set()
        self.r_last = {}
        self.r_dmas = []

    def barrier(self):
        self.bar_deps = list(self.r_last.values()) + list(self.r_dmas)
        self.bar_pending = set(ENGS)
        self.r_dmas = []

    def op(self, eng, name, kwargs, reads=(), writes=(), dma=False, join=False):
        o = Op()
        meth, kw = name, dict(kwargs)
        o.fn = lambda e: getattr(e, meth)(**kw)
        o.eng, o.is_dma, o.target, o.count = eng, dma, False, 0
        o.idx = len(self.ops[eng])
        deps = set()
        isr = False
        for k in reads:
            if isinstance(k, tuple) and k[0][0] == "r":
                isr = True
            for lw in self.last_w.get(k, ()):
                if lw.is_dma or lw.eng != eng or eng != PE:
                    deps.add(lw)
        joined = {}
        for k in writes:
            if isinstance(k, tuple) and k[0][0] == "r":
                isr = True
            lws = self.last_w.get(k, [])
            jn = dma and join and len(lws) > 0 and all(w.is_dma for w in lws) and not self.readers.get(k)
            joined[k] = jn
            if not jn:
                for lw in lws:
                    if lw.is_dma or lw.eng != eng or dma:
                        deps.add(lw)
            for r in self.readers.get(k, {}).values():
                if r.is_dma or r.eng != eng or dma:
                    deps.add(r)
        if isr:
            if eng in self.bar_pending:
                self.bar_pending.discard(eng)
                for d in self.bar_deps:
                    if d.is_dma or d.eng != eng or dma:
                        deps.add(d)
            if dma:
                self.r_dmas.append(o)
            else:
                self.r_last[eng] = o
        deps.discard(o)
        o.deps = deps
        if dma:
            j = self.ndma
            self.ndma += 1
            o.dma_sem = j % NDS
            o.dma_val = 16 * (j // NDS + 1)
            o.dma_prev = 16 * (j // NDS)
        for k in writes:
            self.last_w[k] = (self.last_w.get(k, []) + [o]) if joined[k] else [o]
            self.readers[k] = {}
        for k in reads:
            self.readers.setdefault(k, {})[eng if not dma else ("dma", o.idx, eng)] = o
        self.ops[eng].append(o)
        return o

    def mm(self, out, lhsT, rhs, start, stop, reads, writes):
        return self.op(PE, "matmul", dict(out=out, lhsT=lhsT, rhs=rhs, start=start, stop=stop), reads, writes)

    def tr(self, out, in_, identity, reads, writes):
        return self.op(PE, "transpose", dict(out=out, in_=in_, identity=identity), reads, writes)

    def act(self, out, in_, func, reads, writes, bias=None, scale=None):
        kw = dict(out=out, in_=in_, func=func)
        if bias is not None:
            kw["bias"] = bias
        if scale is not None:
            kw["scale"] = scale
        return self.op(ACT, "activation", kw, reads, writes)

    def tt(self, out, in0, in1, op, reads, writes, eng=DVE):
        return self.op(eng, "tensor_tensor", dict(out=out, in0=in0, in1=in1, op=op), reads, writes)

    def ts(self, out, in0, s1, s2, op0, op1, reads, writes, eng=DVE):
        kw = dict(out=out, in0=in0, scalar1=s1, scalar2=s2, op0=op0)
        if op1 is not None:
            kw["op1"] = op1
        return self.op(eng, "tensor_scalar", kw, reads, writes)

    def stt(self, out, in0, scalar, in1, op0, op1, reads, writes):
        return self.op(DVE, "scalar_tensor_tensor", dict(out=out, in0=in0, scalar=scalar, in1=in1, op0=op0, op1=op1), reads, writes)

    def cp(self, out, in_, reads, writes, eng=DVE):
        return self.op(eng, "tensor_copy", dict(out=out, in_=in_), reads, writes)

    def recip(self, out, in_, reads, writes):
        return self.op(DVE, "reciprocal", dict(out=out, in_=in_), reads, writes)

    def memset(self, ap, val, writes, eng=DVE):
        return self.op(eng, "memset", dict(ap=ap, constant=val), (), writes)

    def scan(self, out, d0, d1, initial, reads, writes):
        return self.op(DVE, "tensor_tensor_scan", dict(out=out, data0=d0, data1=d1, initial=initial, op0=ALU.mult, op1=ALU.add), reads, writes)

    def dma(self, eng, out, in_, reads=(), writes=(), join=False):
        return self.op(eng, "dma_start", dict(out=out, in_=in_), reads, writes, dma=True, join=join)

    def emit(self):
        nc = self.nc
        for e in ENGS:
            for o in self.ops[e]:
                for d in o.deps:
                    if not d.is_dma:
                        d.target = True
        for e in ENGS:
            c = 0
            for o in self.ops[e]:
                if o.target and not o.is_dma:
                    c += 1
                    o.count = c
        with contextlib.ExitStack() as st:
            esem = {e: st.enter_context(nc.semaphore("es_" + e)) for e in ENGS}
            dsem = [st.enter_context(nc.semaphore("ds%d" % i)) for i in range(NDS)]
            block = st.enter_context(nc.Block())
            for e in ENGS:
                ops = self.ops[e]
                if not ops:
                    continue

                def body(eng, ops=ops, e=e):
                    known = {}

                    def wait(key, sem, val):
                        if known.get(key, 0) >= val:
                            return
                        known[key] = val
                        eng.wait_ge(sem, val)

                    for o in ops:
                        for d in sorted(o.deps, key=lambda d: (d.eng, d.idx)):
                            if d.is_dma:
                                wait(("d", d.dma_sem), dsem[d.dma_sem], d.dma_val)
                            else:
                                wait(("e", d.eng), esem[d.eng], d.count)
                        if o.is_dma and o.dma_prev > 0:
                            wait(("d", o.dma_sem), dsem[o.dma_sem], o.dma_prev)
                        ins = o.fn(eng)
                        if o.is_dma:
                            ins.then_inc(dsem[o.dma_sem], 16)
                        elif o.target:
                            ins.then_inc(esem[e], 1)
                    for o in ops:
                        if o.is_dma:
                            wait(("d", o.dma_sem), dsem[o.dma_sem], o.dma_val)

                getattr(block, e)(body)


VSPEC = [("b_ada", (4, 48)), ("ln_g", (4, 2, 8)), ("ln_b", (4, 2, 8)), ("pool_scale", (2, 8)), ("g_q", (3,)), ("g_kv", (2,)),
         ("conv_w", (4, 8)), ("conv_b", (8,)), ("b_a", (2, 8)), ("b_i", (2, 8)), ("lam", (16,)), ("state", (2, 8)),
         ("eps", (1,)), ("epsln", (1,)), ("one", (1,)), ("quarter", (1,))]
VOFF = {}
_o = 0
for _n, _s in VSPEC:
    VOFF[_n] = (_o, _s)
    _o += int(np.prod(_s))
NV = _o


def build_program(dbg=False):
    nc = bass.Bass("TRN2", target_bir_lowering=False)

    def din(name, shape):
        return nc.dram_tensor(name, list(shape), F32, kind="ExternalInput").ap()

    def dout(name, shape):
        return nc.dram_tensor(name, list(shape), F32, kind="ExternalOutput").ap()

    xin = din("xin", [NT, D])
    condT = din("condT", [128, 16])
    vecs_d = din("vecs", [128, NV])
    ident_d = din("ident", [128, 128])
    pband_d = din("pband", [4, 128, 12, 144])
    rope_d = din("rope", [64, 2, 1024])
    cache_ckv = din("cache_ckv", [256, 256])
    cache_kr = din("cache_kr", [256, 64])
    w_ada = din("w_ada", [4, D, 6 * D])
    w_ffn_in = din("w_ffn_in", [4, D, 2 * DFF])
    w_ffn_out = din("w_ffn_out", [4, DFF, D])
    w_pool = din("w_pool", [2, 4, 256, 256])
    w_dq = din("w_dq", [D, 384])
    w_uq = din("w_uq", [384, 1536])
    w_uq_rp = din("w_uq_rp", [384, 512])
    w_dkv = din("w_dkv", [D, 320])
    w_dkv_rp = din("w_dkv_rp", [D, 64])
    w_uk = din("w_uk", [256, 1024])
    w_uv = din("w_uv", [256, 1024])
    w_mla_o = din("w_mla_o", [D, D])
    w_lru_in = din("w_lru_in", [D, 2 * D])
    w_lru_a = din("w_lru_a", [2, 8, 128, 128])
    w_lru_i = din("w_lru_i", [2, 8, 128, 128])
    w_lru_out = din("w_lru_out", [D, D])

    yout = dout("yout", [NT, D])
    o_ckv = dout("o_ckv", [512, 256])
    o_kr = dout("o_kr", [512, 64])
    o_lru = dout("o_lru", [32, 128])
    dbg_out = [dout("dbg%d" % i, [NT, D]) for i in range(4)] if dbg else None

    st = contextlib.ExitStack()
    with st:
        def sb(name, shape, dtp):
            return st.enter_context(nc.sbuf_tensor(name, list(shape), dtp))

        Xt = sb("X", [128, 8 * NT], F32)
        Ht = sb("H", [128, 8 * NT], BF16)
        X = Xt[:, :].rearrange("p (c t) -> p c t", c=8)
        H = Ht[:, :].rearrange("p (c t) -> p c t", c=8)
        NSLOT = 4
        RINGW = 4096
        ring_t = sb("ring", [128, NSLOT * RINGW], BF16)
        vecs = sb("vecs_sb", [128, NV], F32)
        ident = sb("ident_sb", [128, 128], F32)
        ones_b = sb("ones_b", [128, 128], BF16)
        scond = sb("scond", [128, 16], BF16)
        condf = sb("condf", [128, 16], F32)
        MOD = sb("MOD", [128, 4 * 2 * 48], F32)
        DER = sb("DER", [128, 4 * 2 * 7 * 8], F32)
        RBYTES = 96 * 1024
        Rt = sb("R", [128, RBYTES // 4], F32)
        ps = [st.enter_context(nc.psum_tensor("ps%d" % i, [128, 512], F32)) for i in range(8)]

        P = Prog(nc)
        XST_OFF = 66 * 1024

        def view(off, dtp, *dims):
            n = int(np.prod(dims))
            assert off % 4 == 0
            if dtp == F32:
                assert off + 4 * n <= RBYTES, (off, n)
                ap = Rt[:, off // 4: off // 4 + n]
            else:
                assert n % 2 == 0 and off + 2 * n <= RBYTES, (off, n)
                ap = Rt[:, off // 4: off // 4 + n // 2].bitcast(BF16)
            if len(dims) == 2:
                ap = ap.rearrange("p (a b) -> p a b", a=dims[0])
            elif len(dims) == 3:
                ap = ap.rearrange("p (a b c) -> p a b c", a=dims[0], b=dims[1])
            return ap

        class Alloc:
            def __init__(self, off=0):
                self.off = off

            def __call__(self, dtp, *dims):
                n = int(np.prod(dims)) * (4 if dtp == F32 else 2)
                o = self.off
                self.off += (n + 31) // 32 * 32
                return view(o, dtp, *dims)

        ring_n = [0]

        def ring_load(items):
            s_ = ring_n[0] % NSLOT
            ring_n[0] += 1
            slot = ring_t[:, s_ * RINGW:(s_ + 1) * RINGW]
            key = ("wring", s_)
            for i, (dst_fn, src) in enumerate(items):
                P.dma(POOL, dst_fn(slot), src, writes=[key], join=(i > 0))
            return slot, key

        def wload(dst, src, key, join=False):
            P.dma(POOL, dst, src, writes=[key], join=join)

        rot = {}

        def bank(pool):
            pool = tuple(pool)
            i = rot.get(pool, 0)
            rot[pool] = i + 1
            return pool[i % len(pool)]

        def pk(b):
            return ("ps", b)

        def V(name, *idx):
            o, shape = VOFF[name]
            i = 0
            for k, s_ in zip(idx, shape[:len(idx)]):
                i = i * s_ + k
            rest = int(np.prod(shape[len(idx):])) if len(idx) < len(shape) else 1
            return vecs[:, o + i * rest: o + (i + 1) * rest]

        def MODv(L, j, kind):
            o = (L * 2 + j) * 48 + kind * 8
            return MOD[:, o:o + 8]

        def DERv(L, j, kind):
            o = ((L * 2 + j) * 7 + kind) * 8
            return DER[:, o:o + 8]

        P.dma(SP, vecs[:, :], vecs_d, writes=["vecs"])
        P.dma(SP, ident[:, :], ident_d, writes=["ident"])
        P.dma(SP, condf[:, :], condT, writes=["condf"])
        P.memset(ones_b[:, :], 1.0, ["ones"])
        P.act(scond[:, :], condf[:, :], AF.Silu, ["condf"], ["scond"])
        scond3 = scond[:, :].rearrange("p (a b) -> p a b", a=8)

        stage_t = view(XST_OFF, F32, 2 * D)
        for ti in range(12):
            sbuf = stage_t[:, (ti % 2) * D:(ti % 2 + 1) * D]
            sk = ("rxstage", ti % 2)
            P.dma(SP, sbuf, xin[ti * 128:(ti + 1) * 128, :], writes=[sk])
            tt = ti // 4
            for half in range(2):
                b = bank([0, 1, 2, 3])
                for cc in range(4):
                    c = half * 4 + cc
                    P.tr(ps[b][:, cc * 128:(cc + 1) * 128], sbuf[:, c * 128:(c + 1) * 128], ident[:, :], [sk, "ident"], [pk(b)])
                dst = X[:, half * 4:half * 4 + 4, ti * 128:(ti + 1) * 128]
                src = ps[b][:, :].rearrange("p (a b) -> p a b", a=4)
                wk = [("X", half * 4 + cc, tt) for cc in range(4)]
                if half:
                    P.act(dst, src, AF.Copy, [pk(b)], wk)
                else:
                    P.cp(dst, src, [pk(b)], wk)

        def emit_out(dstd, tts=(0, 1, 2)):
            for ti in range(12):
                tt = ti // 4
                if tt not in tts:
                    continue
                sbuf = stage_t[:, (ti % 2) * D:(ti % 2 + 1) * D]
                for half in range(2):
                    b = bank([0, 1, 2, 3])
                    for cc in range(4):
                        c = half * 4 + cc
                        P.tr(ps[b][:, cc * 128:(cc + 1) * 128], X[:, c, ti * 128:(ti + 1) * 128], ident[:, :], [("X", c, tt), "ident"], [pk(b)])
                    sk = ("rxstage", ti % 2, half)
                    if half:
                        P.act(sbuf[:, half * 512:(half + 1) * 512], ps[b][:, :], AF.Copy, [pk(b)], [sk])
                    else:
                        P.cp(sbuf[:, half * 512:(half + 1) * 512], ps[b][:, :], [pk(b)], [sk])
                P.dma(SP, dstd[ti * 128:(ti + 1) * 128, :], sbuf, reads=[("rxstage", ti % 2, 0), ("rxstage", ti % 2, 1)])

        def mod_item(L, it, b=7):
            src = w_ada[L, :, it * 512:(it + 1) * 512].rearrange("(k p) n -> p k n", p=128)
            slot, key = ring_load([(lambda s_: s_.rearrange("p (k n) -> p k n", k=8), src)])
            s3 = slot.rearrange("p (k n) -> p k n", k=8)
            for oc in range(4):
                o = it * 4 + oc
                for k in range(8):
                    P.mm(ps[b][:, 2 * o:2 * o + 2], s3[:, k, oc * 128:(oc + 1) * 128], scond3[:, k, :], k == 0, k == 7, [key, "scond"], [pk(b)])

        def mod_finish(L, b=7, part=None, rng=None):
            lo, hi = (0, 48) if part is None else ((0, 24) if part == 0 else (24, 48))
            if rng is not None:
                lo, hi = rng
            for j in range(2):
                o = (L * 2 + j) * 48
                P.tt(MOD[:, o + lo:o + hi], ps[b][:, 2 * lo + j:2 * hi:2], V("b_ada", L)[:, lo:hi], ALU.add, [pk(b), "vecs"], [("MOD", L, j)])

        def modulation(L):
            for it in range(12):
                mod_item(L, it)
            mod_finish(L)

        def derive(L, part=None):
            for j in range(2):
                rk = [("MOD", L, j), "vecs"]
                wk = [("DER", L, j)]
                if part in (None, 0, "a"):
                    P.ts(DERv(L, j, 0), MODv(L, j, 1), 1.0, None, ALU.add, None, rk, wk)
                    P.cp(DERv(L, j, 1), MODv(L, j, 0), rk, wk)
                if part in (None, 0, "g"):
                    if L % 3 == 0:
                        P.stt(DERv(L, j, 2), MODv(L, j, 2), 1.0 / ALPHA, V("pool_scale", L // 3), ALU.mult, ALU.mult, rk, wk)
                    else:
                        P.ts(DERv(L, j, 2), MODv(L, j, 2), 1.0 / ALPHA, None, ALU.mult, None, rk, wk)
                if part in (None, 1):
                    P.ts(DERv(L, j, 3), MODv(L, j, 4), 1.0, None, ALU.add, None, rk, wk)
                    P.tt(DERv(L, j, 4), V("ln_b", L, 0), DERv(L, j, 3), ALU.mult, rk + wk, wk)
                    P.tt(DERv(L, j, 4), DERv(L, j, 4), MODv(L, j, 3), ALU.add, rk + wk, wk)
                    P.ts(DERv(L, j, 5), MODv(L, j, 5), 1.0 / ALPHA, None, ALU.mult, None, rk, wk)

        def derive_next(L):
            for j in range(2):
                rk = [("DER", L + 1, j), "vecs"]
                wk = [("DERN", L, j)]
                P.tt(DERv(L, j, 6), V("ln_b", L, 1), DERv(L + 1, j, 0), ALU.mult, rk, wk)
                P.tt(DERv(L, j, 6), DERv(L, j, 6), DERv(L + 1, j, 1), ALU.add, rk + wk, wk)

        LNOFF = 36 * 1024 + 3072

        def ln_bufs():
            A = Alloc(LNOFF)
            Zb = A(BF16, 3, 512)
            Sq = A(BF16, 3, 512)
            mean = A(F32, 3, 512)
            rstd = A(F32, 3, 512)
            T1 = A(F32, 2, 512)
            V1 = A(F32, 2, 512)
            assert A.off <= RBYTES
            return Zb, Sq, mean, rstd, T1, V1

        def ln_stats_chunk(tt, c):
            t0, tn = TT[tt]
            Zb, Sq, mean, rstd, T1, V1 = ln_bufs()
            bm, bs = 6, 7
            r = c % 3
            xs = X[:, c, t0:t0 + tn]
            P.act(Zb[:, r, :], xs, AF.Copy, [("X", c, tt)], [("rZb", r)])
            P.act(Sq[:, r, :], xs, AF.Square, [("X", c, tt)], [("rSq", r)])
            P.mm(ps[bm][:, :], ones_b[:, :], Zb[:, r, :], c == 0, c == 7, [("rZb", r), "ones"], [pk(bm)])
            P.mm(ps[bs][:, :], ones_b[:, :], Sq[:, r, :], c == 0, c == 7, [("rSq", r), "ones"], [pk(bs)])

        def ln_stats_fin(tt):
            Zb, Sq, mean, rstd, T1, V1 = ln_bufs()
            mean, rstd = mean[:, tt, :], rstd[:, tt, :]
            bm, bs = 6, 7
            P.ts(mean, ps[bm][:, :], 1.0 / D, None, ALU.mult, None, [pk(bm)], [("rmean", tt)])
            P.tt(rstd, mean, mean, ALU.mult, [("rmean", tt)], [("rrstd", tt)])
            P.stt(rstd, ps[bs][:, :], 1.0 / D, rstd, ALU.mult, ALU.subtract, [pk(bs), ("rrstd", tt)], [("rrstd", tt)])
            P.act(rstd, rstd, AF.Ln, [("rrstd", tt), "vecs"], [("rrstd", tt)], bias=V("epsln"), scale=1.0)
            P.act(rstd, rstd, AF.Exp, [("rrstd", tt)], [("rrstd", tt)], scale=-0.5)

        def ln_stats(tt):
            for c in range(8):
                ln_stats_chunk(tt, c)
            ln_stats_fin(tt)

        def ln_apply_chunk(tt, L, which, hmode, c):
            t0, tn = TT[tt]
            Zb, Sq, mean, rstd, T1, V1 = ln_bufs()
            mean, rstd = mean[:, tt, :], rstd[:, tt, :]
            j = TCOND[tt]
            r = c % 2
            xs = X[:, c, t0:t0 + tn]
            P.tt(T1[:, r, :], xs, mean, ALU.subtract, [("X", c, tt), ("rmean", tt)], [("rT1", r)])
            P.stt(V1[:, r, :], T1[:, r, :], V("ln_g", L, which)[:, c:c + 1], rstd, ALU.mult, ALU.mult, [("rT1", r), ("rrstd", tt), "vecs"], [("rV1", r)])
            P.act(xs, V1[:, r, :], AF.Identity, [("rV1", r), "vecs"], [("X", c, tt)], bias=V("ln_b", L, which)[:, c:c + 1], scale=1.0)
            if hmode == 1:
                P.act(H[:, c, t0:t0 + tn], V1[:, r, :], AF.Identity, [("rV1", r), ("DER", L, j)], [("H", c, tt)],
                      bias=DERv(L, j, 4)[:, c:c + 1], scale=DERv(L, j, 3)[:, c:c + 1])
            elif hmode == 2:
                P.act(H[:, c, t0:t0 + tn], V1[:, r, :], AF.Identity, [("rV1", r), ("DER", L + 1, j), ("DERN", L, j)], [("H", c, tt)],
                      bias=DERv(L, j, 6)[:, c:c + 1], scale=DERv(L + 1, j, 0)[:, c:c + 1])

        def ln_apply(tt, L, which, hmode):
            for c in range(8):
                ln_apply_chunk(tt, L, which, hmode, c)

        def resid(b, c, tt, L, kind):
            j = TCOND[tt]
            t0, tn = TT[tt]
            xs = X[:, c, t0:t0 + tn]
            P.stt(xs, ps[b][:, :tn], DERv(L, j, kind)[:, c:c + 1], xs, ALU.mult, ALU.add, [pk(b), ("X", c, tt), ("DER", L, j)], [("X", c, tt)])

        def ffn(L, pre_hook, mod_next, tail_mod=None, out_hook=None):
            A = Alloc()
            G = A(BF16, 11, NT)
            SA = A(BF16, 3, 512)
            assert A.off <= LNOFF
            state = {"nsa": 0, "mod": 0}

            def load_in(half, jj):
                js = [half * 11 + jj * 2 + s_ for s_ in range(2) if jj * 2 + s_ < 11]
                nj = len(js)
                j0 = js[0]
                items = []
                for ab in range(2):
                    src = w_ffn_in[L, :, ab * DFF + j0 * 128: ab * DFF + (j0 + nj) * 128].rearrange("(k p) n -> p k n", p=128)
                    items.append((lambda s_, ab=ab, nj=nj: s_.rearrange("p (k a n) -> p k a n", k=8, a=2)[:, :, ab, :nj * 128], src))
                slot, key = ring_load(items)
                return js, slot.rearrange("p (k a n) -> p k a n", k=8, a=2), key

            def p1(half, js, s4, key, tts):
                for si, jf in enumerate(js):
                    gi = jf - half * 11
                    for tt in tts:
                        t0, tn = TT[tt]
                        ba = bank([0, 1, 2, 3])
                        bb = bank([0, 1, 2, 3])
                        for ab, b in ((0, ba), (1, bb)):
                            for k in range(8):
                                P.mm(ps[b][:, :tn], s4[:, k, ab, si * 128:(si + 1) * 128], H[:, k, t0:t0 + tn], k == 0, k == 7, [key, ("H", k, tt)], [pk(b)])
                        r = state["nsa"] % 3
                        state["nsa"] += 1
                        P.act(SA[:, r, :], ps[ba][:, :], AF.Silu, [pk(ba)], [("rSA", r)])
                        P.tt(G[:, gi, t0:t0 + tn], SA[:, r, :], ps[bb][:, :], ALU.mult, [("rSA", r), pk(bb)], [("rG", gi, tt)])

            def mod_step():
                if mod_next is not None and state["mod"] < 12:
                    mod_item(mod_next, state["mod"])
                    state["mod"] += 1

            def load_out(half, cp_):
                items = []
                for cs in range(2):
                    c = cp_ * 2 + cs
                    src = w_ffn_out[L, half * 11 * 128:(half + 1) * 11 * 128, c * 128:(c + 1) * 128].rearrange("(g p) n -> p g n", p=128)
                    items.append((lambda s_, cs=cs: s_[:, cs * 1408:(cs + 1) * 1408].rearrange("p (g n) -> p g n", g=11), src))
                return ring_load(items)

            def p2(slot, key, cs, c, tt):
                w3 = slot[:, cs * 1408:(cs + 1) * 1408].rearrange("p (g n) -> p g n", g=11)
                t0, tn = TT[tt]
                b = bank([4, 5])
                for gi in range(11):
                    P.mm(ps[b][:, :tn], w3[:, gi, :], G[:, gi, t0:t0 + tn], gi == 0, gi == 10, [key, ("rG", gi, tt)], [pk(b)])
                resid(b, c, tt, L, 5)

            NPRO = 3
            pro = [load_in(0, jj) for jj in range(NPRO)]
            for tt in range(3):
                for ii, (js, s4, key) in enumerate(pro):
                    pre_hook(tt, ii)
                    p1(0, js, s4, key, [tt])
            for jj in range(NPRO, 6):
                js, s4, key = load_in(0, jj)
                p1(0, js, s4, key, [0, 1, 2])
                mod_step()
                mod_step()
            for cp_ in range(4):
                slot, key = load_out(0, cp_)
                for cs in range(2):
                    for tt in range(3):
                        p2(slot, key, cs, cp_ * 2 + cs, tt)
                if cp_ >= 2:
                    mod_step()
            for jj in range(6):
                js, s4, key = load_in(1, jj)
                p1(1, js, s4, key, [0, 1, 2])
                mod_step()
                if tail_mod is not None:
                    mod_item(tail_mod, jj, 6)
            if tail_mod is not None:
                mod_finish(tail_mod, 6, 0)
            while mod_next is not None and state["mod"] < 12:
                mod_step()
            if mod_next is not None:
                mod_finish(mod_next)
                derive(mod_next)
            if L < 3:
                derive_next(L)
            outs = [load_out(1, cp_) for cp_ in range(4)]
            hm = 2 if L < 3 else 0

            def p2tile(tt, hook=None):
                for cp_ in range(4):
                    slot, key = outs[cp_]
                    for cs in range(2):
                        p2(slot, key, cs, cp_ * 2 + cs, tt)
                        if hook is not None:
                            hook(cp_ * 2 + cs)
            p2tile(0)
            p2tile(1, lambda c: ln_stats_chunk(0, c))
            ln_stats_fin(0)

            def hk2(c):
                ln_stats_chunk(1, c)
                ln_apply_chunk(0, L, 1, hm, c)
            p2tile(2, hk2)
            ln_stats_fin(1)
            if out_hook is not None:
                out_hook(0)
            tm = list(range(6, 12)) if tail_mod is not None else []
            for c in range(8):
                ln_stats_chunk(2, c)
                ln_apply_chunk(1, L, 1, hm, c)
                if tm and c % 2 == 1:
                    mod_item(tail_mod, tm.pop(0), 5)
            ln_stats_fin(2)
            if out_hook is not None:
                out_hook(1)
            while tm:
                mod_item(tail_mod, tm.pop(0), 5)
            ln_apply(2, L, 1, hm)
            if out_hook is not None:
                out_hook(2)
            if tail_mod is not None:
                mod_finish(tail_mod, 5, 1)
                derive(tail_mod)

        def pair(ap2d, base, width, off, n):
            return ap2d[:, base:base + 2 * width].rearrange("p (s t) -> p s t", s=2)[:, :, off:off + n]

        def pool_mixer(L, mid=None):
            jw = L // 3
            P.barrier()
            A = Alloc()
            Wp = A(BF16, 4, 2, 256)
            Bt = A(BF16, 4, 12, 144)
            Zb = A(BF16, 12, 1024)
            assert A.off <= RBYTES, A.off
            for g in range(4):
                wload(Wp[:, g, :, :], w_pool[jw, g].rearrange("(k p) n -> p k n", p=128), ("rWp", g))
            for g in range(4):
                wload(Bt[:, g, :, :], pband_d[g], ("rBt", g))
            if L == 0:
                for tt in range(3):
                    j = TCOND[tt]
                    t0, tn = TT[tt]
                    for c in range(8):
                        P.act(H[:, c, t0:t0 + tn], X[:, c, t0:t0 + tn], AF.Identity, [("X", c, tt), ("DER", L, j)], [("H", c, tt)],
                              bias=DERv(L, j, 1)[:, c:c + 1], scale=DERv(L, j, 0)[:, c:c + 1])
            for i in range(12):
                tt = i // 4
                for half in range(2):
                    b = bank([0, 1, 2, 3])
                    for gg in range(2):
                        g = half * 2 + gg
                        for kc in range(2):
                            P.mm(ps[b][:, gg * 256:(gg + 1) * 256], H[:, 2 * g + kc, i * 128:(i + 1) * 128], Wp[:, g, kc, :], kc == 0, kc == 1,
                                 [("rWp", g), ("H", 2 * g + kc, tt)], [pk(b)])
                    if half:
                        P.act(Zb[:, i, 512:1024], ps[b][:, :], AF.Copy, [pk(b)], [("rZ", i, 1)])
                    else:
                        P.cp(Zb[:, i, 0:512], ps[b][:, :], [pk(b)], [("rZ", i, 0)])
            if mid is not None:
                mid()
            for c in range(8):
                g, dc = c // 2, c % 2
                for tt in range(3):
                    t0, tn = TT[tt]
                    contribs = []
                    for i in range(12):
                        si_ = 0 if i < 2 else (1 if i < 4 else 2)
                        so, T = SEQS[si_]
                        ls = i * 128 - so
                        lo = max(so + max(0, ls - 8), t0)
                        hi = min(so + min(T, ls + 136), t0 + tn)
                        if hi > lo:
                            contribs.append((i, lo - t0, hi - t0, (lo - so) - (ls - 8)))
                    b = bank([4, 5, 6, 7])
                    for idx, (i, clo, chi, bclo) in enumerate(contribs):
                        P.op(PE, "matmul", dict(out=ps[b][:, clo:chi], lhsT=Zb[:, i, g * 256 + dc * 128:g * 256 + (dc + 1) * 128], rhs=Bt[:, g, i, bclo:bclo + (chi - clo)],
                                                start=(idx == 0), stop=(idx == len(contribs) - 1), skip_group_check=True),
                             [("rZ", i, g // 2), ("rBt", g)], [pk(b)])
                    resid(b, c, tt, L, 2)

        def mla_small_weights():
            Aw = Alloc(76 * 1024)
            return Aw(BF16, 8, 384), Aw(BF16, 8, 384), Aw(BF16, 2, 1024), Aw(BF16, 2, 1024)

        def mla_prefetch():
            Wdq, Wdkv, Wuk, Wuv = mla_small_weights()
            kp = "(k p) n -> p k n"
            wload(Wdq, w_dq.rearrange(kp, p=128), ("rWdq",))
            wload(Wdkv[:, :, 0:320], w_dkv.rearrange(kp, p=128), ("rWdkv",))
            wload(Wdkv[:, :, 320:384], w_dkv_rp.rearrange(kp, p=128), ("rWdkv",), join=True)
            wload(Wuk, w_uk.rearrange(kp, p=128), ("rWuk",))
            wload(Wuv, w_uv.rearrange(kp, p=128), ("rWuv",))

        def mla_mixer(L, tile_hook=None):
            P.barrier()
            Wdq, Wdkv, Wuk, Wuv = mla_small_weights()
            A = Alloc()
            Wuq = A(BF16, 3, 2048)
            kp = "(k p) n -> p k n"
            wload(Wuq[:, :, 0:1536], w_uq.rearrange(kp, p=128), ("rWuq",))
            wload(Wuq[:, :, 1536:2048], w_uq_rp.rearrange(kp, p=128), ("rWuq",), join=True)
            NK = 1792
            QLn = A(BF16, 3, NT)
            CK = A(BF16, 2, NK)
            KR = A(BF16, NK)
            ROPE = A(F32, 2, 1024)
            PT = A(BF16, 4, 512)
            rden = A(F32, 2, 512)
            hb0 = [A(BF16, 1280), A(BF16, 10, 128), A(BF16, 1024), A(BF16, 1024), A(F32, 2, 512)]
            alias_off = A.off
            CKf = A(F32, 2, 512)
            KRf = A(F32, 512)
            SQ = A(BF16, 3, 512)
            rs = A(F32, 512)
            cst = A(F32, 2, 256)
            assert A.off <= 76 * 1024, A.off
            A2 = Alloc(alias_off)
            hb1 = [A2(BF16, 1280), A2(BF16, 10, 128), A2(BF16, 1024), A2(BF16, 1024), A2(F32, 2, 512)]
            assert A2.off <= A.off
            RT = hb0[4]
            P.memset(KR[64:128, :], 0.0, [("rKRpad",)])
            P.dma(SP, ROPE[0:64, :, :], rope_d, writes=[("rROPE",)])
            for t2 in range(2):
                P.dma(SP, cst[:, 0, :], cache_ckv[t2 * 128:(t2 + 1) * 128, :], writes=[("rcst", 0)])
                P.dma(SP, cst[:, 1, 0:64], cache_kr[t2 * 128:(t2 + 1) * 128, :], writes=[("rcst", 1)])
                for rc in range(2):
                    P.tr(ps[0][:, rc * 128:(rc + 1) * 128], cst[:, 0, rc * 128:(rc + 1) * 128], ident[:, :], [("rcst", 0), "ident"], [pk(0)])
                P.tr(ps[1][0:64, 0:128], cst[:, 1, 0:64], ident[:, :], [("rcst", 1), "ident"], [pk(1)])
                P.cp(CK[:, :, 512 + t2 * 128:512 + (t2 + 1) * 128], ps[0][:, 0:256].rearrange("p (a b) -> p a b", a=2), [pk(0)], [("rCK", 3 + t2)])
                P.cp(KR[0:64, 512 + t2 * 128:512 + (t2 + 1) * 128], ps[1][0:64, 0:128], [pk(1)], [("rKR", 3 + t2)])
            for tt in range(3):
                t0, tn = TT[tt]
                qb = [0, 1, 2]
                for rc in range(3):
                    for k in range(8):
                        P.mm(ps[qb[rc]][:, :], Wdq[:, k, rc * 128:(rc + 1) * 128], H[:, k, t0:t0 + 512], k == 0, k == 7, [("rWdq",), ("H", k, tt)], [pk(qb[rc])])
                    P.act(SQ[:, rc, :], ps[qb[rc]][:, :], AF.Square, [pk(qb[rc])], [("rSQ", rc)])
                for rc in range(3):
                    P.mm(ps[6][:, :], ones_b[:, :], SQ[:, rc, :], rc == 0, rc == 2, [("rSQ", rc), "ones"], [pk(6)])
                P.act(rs, ps[6][:, :], AF.Ln, [pk(6), "vecs"], [("rrs",)], bias=V("eps"), scale=1.0 / 384)
                P.act(rs, rs, AF.Exp, [("rrs",)], [("rrs",)], scale=-0.5)
                for rc in range(3):
                    P.stt(QLn[:, rc, t0:t0 + 512], ps[qb[rc]][:, :], V("g_q")[:, rc:rc + 1], rs, ALU.mult, ALU.mult, [pk(qb[rc]), ("rrs",), "vecs"], [("rQLn", tt)])
                kb = [3, 4]
                for rc in range(2):
                    for k in range(8):
                        P.mm(ps[kb[rc]][:, :], Wdkv[:, k, rc * 128:(rc + 1) * 128], H[:, k, t0:t0 + 512], k == 0, k == 7, [("rWdkv",), ("H", k, tt)], [pk(kb[rc])])
                    P.act(SQ[:, rc, :], ps[kb[rc]][:, :], AF.Square, [pk(kb[rc])], [("rSQ", rc)])
                for rc in range(2):
                    P.mm(ps[7][:, :], ones_b[:, :], SQ[:, rc, :], rc == 0, rc == 1, [("rSQ", rc), "ones"], [pk(7)])
                P.act(rs, ps[7][:, :], AF.Ln, [pk(7), "vecs"], [("rrs",)], bias=V("eps"), scale=1.0 / 256)
                P.act(rs, rs, AF.Exp, [("rrs",)], [("rrs",)], scale=-0.5)
                koff = 0 if tt == 0 else 768 + (tt - 1) * 512
                for rc in range(2):
                    if tt == 0:
                        P.stt(CKf[:, rc, :], ps[kb[rc]][:, :], V("g_kv")[:, rc:rc + 1], rs, ALU.mult, ALU.mult, [pk(kb[rc]), ("rrs",), "vecs"], [("rCKf",)])
                        P.act(CK[:, rc, 0:512], CKf[:, rc, :], AF.Copy, [("rCKf",)], [("rCK", 0)])
                    else:
                        P.stt(CK[:, rc, koff:koff + 512], ps[kb[rc]][:, :], V("g_kv")[:, rc:rc + 1], rs, ALU.mult, ALU.mult, [pk(kb[rc]), ("rrs",), "vecs"], [("rCK", tt)])
                for k in range(8):
                    P.mm(ps[5][0:64, :], Wdkv[:, k, 256:320], H[:, k, t0:t0 + 512], k == 0, k == 7, [("rWdkv",), ("H", k, tt)], [pk(5)])
                if tt == 0:
                    P.act(KRf[0:64, :], ps[5][0:64, :], AF.Copy, [pk(5)], [("rKRf",)])
                    P.cp(KR[0:64, 0:512], ps[5][0:64, :], [pk(5)], [("rKR", 0)])
                else:
                    s0 = (tt - 1) * 512
                    for k in range(8):
                        P.mm(ps[6][0:64, :], Wdkv[:, k, 320:384], H[:, k, t0:t0 + 512], k == 0, k == 7, [("rWdkv",), ("H", k, tt)], [pk(6)])
                    P.tt(RT[0:64, 0, :], ps[5][0:64, :], ROPE[0:64, 0, s0:s0 + 512], ALU.mult, [pk(5), ("rROPE",)], [("rRT", 0, 0)])
                    P.tt(RT[0:64, 1, :], ps[6][0:64, :], ROPE[0:64, 1, s0:s0 + 512], ALU.mult, [pk(6), ("rROPE",)], [("rRT", 0, 1)])
                    P.tt(KR[0:64, koff:koff + 512], RT[0:64, 0, :], RT[0:64, 1, :], ALU.add, [("rRT", 0, 0), ("rRT", 0, 1)], [("rKR", tt)])
                if tt == 0:
                    for t4 in range(4):
                        for rc in range(2):
                            P.tr(ps[2][:, rc * 128:(rc + 1) * 128], CKf[:, rc, t4 * 128:(t4 + 1) * 128], ident[:, :], [("rCKf",), "ident"], [pk(2)])
                        P.tr(ps[2][:, 256:320], KRf[0:64, t4 * 128:(t4 + 1) * 128], ident[0:64, 0:64], [("rKRf",), "ident"], [pk(2)])
                        P.act(cst[:, 0, :], ps[2][:, 0:256], AF.Copy, [pk(2)], [("rcst", 0)])
                        P.act(cst[:, 1, 0:64], ps[2][:, 256:320], AF.Copy, [pk(2)], [("rcst", 1)])
                        P.dma(SP, o_ckv[t4 * 128:(t4 + 1) * 128, :], cst[:, 0, :], reads=[("rcst", 0)])
                        P.dma(SP, o_kr[t4 * 128:(t4 + 1) * 128, :], cst[:, 1, 0:64], reads=[("rcst", 1)])
            P.barrier()
            for par, hb in enumerate((hb0, hb1)):
                P.memset(hb[3][64:128, :], 0.0, [("rQRpad", par)])
            jobs = [
                (0, 512, 0, 512, [0], False, [("rCK", 0)], [("rKR", 0)], 256),
                (512, 1024, 512, 1280, [1, 2], True, [("rCK", i) for i in (1, 2, 3, 4)], [("rKR", i) for i in (1, 2, 3, 4)], None),
            ]
            heads = [(job, h) for job in jobs for h in range(8)]
            cnt = {"pt": 0, "acc": 0}

            def kvq(i):
                (q0, nq, k0, nk, tts, rope, ckk, krk, blk), h = heads[i]
                par = i % 2
                KTh, Vh, QTh, QRh, RTb = (hb0, hb1)[par]
                nsc = nk // 128
                qtiles = [(q, min(512, nq - q)) for q in range(0, nq, 512)]
                qlk = [("rQLn", t) for t in tts]
                for ks in range(0, nk, 512):
                    kn = min(512, nk - ks)
                    b = bank([0, 1])
                    for rc in range(2):
                        P.mm(ps[b][:, :kn], Wuk[:, rc, h * 128:(h + 1) * 128], CK[:, rc, k0 + ks:k0 + ks + kn], rc == 0, rc == 1, [("rWuk",)] + ckk, [pk(b)])
                    P.act(KTh[:, ks:ks + kn], ps[b][:, :kn], AF.Copy, [pk(b)], [("rKTh", par)])
                for s4 in range(0, nsc, 4):
                    ns = min(4, nsc - s4)
                    b = bank([0, 1])
                    for si in range(ns):
                        sc = s4 + si
                        for rc in range(2):
                            P.mm(ps[b][:, si * 128:(si + 1) * 128], CK[:, rc, k0 + sc * 128:k0 + (sc + 1) * 128], Wuv[:, rc, h * 128:(h + 1) * 128], rc == 0, rc == 1, [("rWuv",)] + ckk, [pk(b)])
                    P.cp(Vh[:, s4:s4 + ns, :], ps[b][:, :ns * 128].rearrange("p (a b) -> p a b", a=ns), [pk(b)], [("rVh", par)])
                for (q, qn) in qtiles:
                    b = bank([0, 1])
                    for rc in range(3):
                        P.mm(ps[b][:, :qn], Wuq[:, rc, h * 192:h * 192 + 128], QLn[:, rc, q0 + q:q0 + q + qn], rc == 0, rc == 2, [("rWuq",)] + qlk, [pk(b)])
                    P.act(QTh[:, q:q + qn], ps[b][:, :qn], AF.Copy, [pk(b)], [("rQTh", par)])
                    b = bank([0, 1])
                    for rc in range(3):
                        P.mm(ps[b][0:64, :qn], Wuq[:, rc, h * 192 + 128:h * 192 + 192], QLn[:, rc, q0 + q:q0 + q + qn], rc == 0, rc == 2, [("rWuq",)] + qlk, [pk(b)])
                    if not rope:
                        P.act(QRh[0:64, q:q + qn], ps[b][0:64, :qn], AF.Copy, [pk(b)], [("rQRh", par)])
                    else:
                        b2 = bank([0, 1])
                        for rc in range(3):
                            P.mm(ps[b2][0:64, :qn], Wuq[:, rc, 1536 + h * 64:1536 + (h + 1) * 64], QLn[:, rc, q0 + q:q0 + q + qn], rc == 0, rc == 2, [("rWuq",)] + qlk, [pk(b2)])
                        P.tt(RTb[0:64, 0, :qn], ps[b][0:64, :qn], ROPE[0:64, 0, q:q + qn], ALU.mult, [pk(b), ("rROPE",)], [("rRT", par, 0)])
                        P.tt(RTb[0:64, 1, :qn], ps[b2][0:64, :qn], ROPE[0:64, 1, q:q + qn], ALU.mult, [pk(b2), ("rROPE",)], [("rRT", par, 1)])
                        P.tt(QRh[0:64, q:q + qn], RTb[0:64, 0, :qn], RTb[0:64, 1, :qn], ALU.add, [("rRT", par, 0), ("rRT", par, 1)], [("rQRh", par)])

            def att(i):
                (q0, nq, k0, nk, tts, rope, ckk, krk, blk), h = heads[i]
                par = i % 2
                KTh, Vh, QTh, QRh, RTb = (hb0, hb1)[par]
                nsc = nk // 128
                if blk is None:
                    qtiles = [(q, min(512, nq - q)) for q in range(0, nq, 512)]
                    seq = [(q, qn, sc, sc == 0, sc == nsc - 1, sc == nsc - 1, (q, qn)) for (q, qn) in qtiles for sc in range(nsc)]
                else:
                    per = blk // 128
                    seq = []
                    for sc in range(nsc):
                        q = (sc // per) * blk
                        seq.append((q, blk, sc, sc % per == 0, sc % per == per - 1, sc == nsc - 1, (0, nq)))

                def score(item):
                    q, qn, sc = item[0], item[1], item[2]
                    b = bank([2, 3])
                    P.mm(ps[b][:, :qn], KTh[:, sc * 128:(sc + 1) * 128], QTh[:, q:q + qn], True, False, [("rKTh", par), ("rQTh", par)], [pk(b)])
                    P.mm(ps[b][:, :qn], KR[:, k0 + sc * 128:k0 + (sc + 1) * 128], QRh[:, q:q + qn], False, True, krk + [("rQRh", par), ("rQRpad", par), ("rKRpad",)], [pk(b)])
                    r = cnt["pt"] % 4
                    cnt["pt"] += 1
                    P.act(PT[:, r, :qn], ps[b][:, :qn], AF.Exp, [pk(b)], [("rPT", r)], scale=MLA_SCALE)
                    return r

                rr = score(seq[0])
                newacc = True
                for idx, (q, qn, sc, first, last, fin, (fq, fqn)) in enumerate(seq):
                    r = rr
                    if idx + 1 < len(seq):
                        rr = score(seq[idx + 1])
                    if newacc:
                        ai = cnt["acc"] % 2
                        cnt["acc"] += 1
                        bo, bd = (4, 5) if ai == 0 else (6, 7)
                        newacc = False
                    cq = q - fq
                    P.mm(ps[bo][:, cq:cq + qn], Vh[:, sc, :], PT[:, r, :qn], first, last, [("rVh", par), ("rPT", r)], [pk(bo)])
                    P.mm(ps[bd][:, cq:cq + qn], ones_b[:, :], PT[:, r, :qn], first, last, ["ones", ("rPT", r)], [pk(bd)])
                    if fin:
                        P.act(rden[:, ai, :fqn], ps[bd][:, :fqn], AF.Ln, [pk(bd)], [("rrden", ai)])
                        P.act(rden[:, ai, :fqn], rden[:, ai, :fqn], AF.Exp, [("rrden", ai)], [("rrden", ai)], scale=-1.0)
                        wk = [("H", h, (q0 + fq) // 512)] + [("Hq", h, q0 + fq + o) for o in range(0, fqn, 256 if blk else 512)]
                        P.tt(H[:, h, q0 + fq:q0 + fq + fqn], ps[bo][:, :fqn], rden[:, ai, :fqn], ALU.mult, [pk(bo), ("rrden", ai)], wk)
                        newacc = True

            kvq(0)
            for i in range(len(heads)):
                if i + 1 < len(heads):
                    kvq(i + 1)
                att(i)
            wo = []
            for half in range(2):
                src = w_mla_o[half * 512:(half + 1) * 512, :].rearrange(kp, p=128)
                wo.append(ring_load([(lambda s_: s_.rearrange("p (k n) -> p k n", k=4), src)]))
            P.barrier()
            for tt in range(3):
                for c in range(8):
                    t0, tn = TT[tt]
                    b = bank([0, 1, 2, 3])
                    for h in range(8):
                        slot, key = wo[h // 4]
                        w3 = slot.rearrange("p (k n) -> p k n", k=4)
                        hq = [("Hq", h, 0), ("Hq", h, 256)] if tt == 0 else [("Hq", h, t0)]
                        P.mm(ps[b][:, :], w3[:, h % 4, c * 128:(c + 1) * 128], H[:, h, t0:t0 + 512], h == 0, h == 7, [key, ("H", h, tt)] + hq, [pk(b)])
                    resid(b, c, tt, L, 2)
                    if tile_hook is not None:
                        tile_hook(tt, c)

        def lru_mixer(L, tile_hook=None):
            P.barrier()
            A = Alloc()
            Wg = A(BF16, 8, 4, 128)
            def load_wg():
                first = True
                for d in range(2):
                    wload(Wg[:, :, d * 2 + 0, :], w_lru_a[d].rearrange("n c m -> c n m"), ("rWg",), join=not first)
                    first = False
                    wload(Wg[:, :, d * 2 + 1, :], w_lru_i[d].rearrange("n c m -> c n m"), ("rWg",), join=True)
            M = view(66 * 1024, BF16, 8, NT)
            UP = A(F32, NTP)
            UC = A(F32, NT)
            Y2s = [A(BF16, NT), A(BF16, NT), A(BF16, NT)]
            Abd = [A(F32, NT), A(F32, NT)]
            IGd = [A(F32, NT), A(F32, NT)]
            Tbd = [A(F32, NT), A(F32, NT)]
            assert A.off <= 66 * 1024, A.off
            A.off = 90 * 1024
            UCb = A(BF16, NT)
            YT = A(F32, 1, 512)
            SPv = A(F32, 2, 16)
            HB = A(F32, 2, 16)
            HS = A(F32, 32)
            assert A.off <= RBYTES, A.off
            P.act(SPv[:, 0, :], V("lam"), AF.Exp, ["vecs"], [("rSP",)], scale=-1.0)
            P.act(SPv[:, 0, :], SPv[:, 0, :], AF.Ln, [("rSP",), "vecs"], [("rSP",)], bias=V("one"), scale=1.0)
            P.ts(SPv[:, 1, :], SPv[:, 0, :], -8.0, None, ALU.mult, None, [("rSP",)], [("rSP2",)])
            P.ts(SPv[:, 0, :], SPv[:, 0, :], -4.0, None, ALU.mult, None, [("rSP",), ("rSP2",)], [("rSP",)])
            P.ts(HB[:, 0, :], vecs[:, VOFF["b_a"][0]:VOFF["b_a"][0] + 16], 0.5, None, ALU.mult, None, ["vecs"], [("rHB",)])
            P.ts(HB[:, 1, :], vecs[:, VOFF["b_i"][0]:VOFF["b_i"][0] + 16], 0.5, None, ALU.mult, None, ["vecs"], [("rHB",)])
            P.memset(UP, 0.0, [("rUP", 0), ("rUP", 1)])
            GRP = [0, 1, 1]
            TSEQ = [(0, 512), (512, 1024)]
            GT = [[0], [1, 2]]
            def FEa(n):
                par = n % 3
                Y2 = Y2s[par]
                items = []
                for uy in range(2):
                    src = w_lru_in[:, uy * D + n * 128: uy * D + (n + 1) * 128].rearrange("(k p) m -> p k m", p=128)
                    items.append((lambda s_, uy=uy: s_[:, 0:2048].rearrange("p (k a m) -> p k a m", k=8, a=2)[:, :, uy, :], src))
                slot, key = ring_load(items)
                w4 = slot[:, 0:2048].rearrange("p (k a m) -> p k a m", k=8, a=2)
                for uy in range(2):
                    for tt in range(3):
                        t0, tn = TT[tt]
                        b = bank([0, 1, 2, 3])
                        for k in range(8):
                            P.mm(ps[b][:, :], w4[:, k, uy, :], H[:, k, t0:t0 + 512], k == 0, k == 7, [key, ("H", k, tt)], [pk(b)])
                        if uy == 0:
                            if tt == 0:
                                P.act(pair(UP, 0, 288, PADW, 256), ps[b][:, :].rearrange("p (s t) -> p s t", s=2), AF.Copy, [pk(b)], [("rUP", 0)])
                            else:
                                o = POFF[2] + (tt - 1) * 512
                                P.act(UP[:, o:o + 512], ps[b][:, :], AF.Copy, [pk(b)], [("rUP", 1)])
                        else:
                            P.act(Y2[:, t0:t0 + 512], ps[b][:, :], AF.Gelu_apprx_tanh, [pk(b)], [("rY", par, tt)])

            def FEb(n):
                segs = [(lambda sh: pair(UP, 0, 288, PADW + sh, 256), pair(UC, 0, 256, 0, 256)),
                        (lambda sh: UP[:, POFF[2] + sh:POFF[2] + sh + 1024], UC[:, 512:1536])]
                for g, (pv, uv) in enumerate(segs):
                    P.ts(uv, pv(0), V("conv_w", 1)[:, n:n + 1], V("conv_b")[:, n:n + 1], ALU.mult, ALU.add, [("rUP", g), "vecs"], [("rUC", g)])
                    for kk in (0, 2, 3):
                        P.stt(uv, pv(kk - 1), V("conv_w", kk)[:, n:n + 1], uv, ALU.mult, ALU.add, [("rUP", g), ("rUC", g), "vecs"], [("rUC", g)])
                    a0, an = TSEQ[g]
                    P.act(UCb[:, a0:a0 + an], UC[:, a0:a0 + an], AF.Copy, [("rUC", g)], [("rUCb", g)])

            def GS(n):
                for d in range(2):
                    dn = d * 8 + n
                    for gate in range(2):
                        for tt in range(3):
                            t0, tn = TT[tt]
                            b = bank([4, 5, 6, 7])
                            P.mm(ps[b][:, :], Wg[:, n, d * 2 + gate, :], UCb[:, t0:t0 + 512], True, True, [("rWg",), ("rUCb", GRP[tt])], [pk(b)])
                            dst, dk = (Abd[d], "rAb") if gate == 0 else (IGd[d], "rIG")
                            P.act(dst[:, t0:t0 + 512], ps[b][:, :], AF.Tanh, [pk(b), ("rHB",)], [(dk, d, tt)], bias=HB[:, gate, dn:dn + 1], scale=0.5)
                for d in range(2):
                    ik = [("rIG", d, t) for t in range(3)]
                    P.stt(IGd[d][:, :], IGd[d][:, :], 1.0, UC[:, :], ALU.add, ALU.mult, ik + [("rUC", 0), ("rUC", 1)], ik)

            def BE_act(n):
                for d in range(2):
                    dn = d * 8 + n
                    ak = [("rAb", d, t) for t in range(3)]
                    tk = [("rTb", d, t) for t in range(3)]
                    P.act(Tbd[d][:, :], Abd[d][:, :], AF.Exp, ak + [("rSP2",)], tk, bias=SPv[:, 1, dn:dn + 1], scale=SPv[:, 1, dn:dn + 1])
                    P.act(Abd[d][:, :], Abd[d][:, :], AF.Exp, ak + [("rSP",)], ak, bias=SPv[:, 0, dn:dn + 1], scale=SPv[:, 0, dn:dn + 1])
                for d in range(2):
                    tk = [("rTb", d, t) for t in range(3)]
                    P.act(Tbd[d][:, :], Tbd[d][:, :], AF.Sqrt, tk + ["vecs"], tk, bias=V("quarter"), scale=-0.25)

            def BE_dve(n):
                for d in range(2):
                    tka = [("rTb", d, t) for t in range(3)]
                    ika = [("rIG", d, t) for t in range(3)]
                    P.tt(Tbd[d][:, :], Tbd[d][:, :], IGd[d][:, :], ALU.mult, tka + ika, tka)
                    for g, (a0, an) in enumerate(TSEQ):
                        ak = [("rAb", d, t) for t in GT[g]]
                        tk = [("rTb", d, t) for t in GT[g]]
                        for si, (so, sl) in enumerate(SEQS):
                            if (0 if si < 2 else 1) != g:
                                continue
                            init = 0.0 if si < 2 else V("state", d)[:, n:n + 1]
                            if d == 0:
                                P.scan(Tbd[d][:, so:so + sl], Abd[d][:, so:so + sl], Tbd[d][:, so:so + sl], init, ak + tk + ["vecs"], tk)
                            else:
                                P.scan(Tbd[d][:, so:so + sl][:, ::-1], Abd[d][:, so:so + sl][:, ::-1], Tbd[d][:, so:so + sl][:, ::-1], init, ak + tk + ["vecs"], tk)

            def BE_fin(n):
                par = n % 3
                Y2 = Y2s[par]
                for d in range(2):
                    tk = [("rTb", d, 0)]
                    c0 = d * 8 + n
                    src_ = Tbd[d][:, 255:512:256] if d == 0 else Tbd[d][:, 0:257:256]
                    P.act(HS[:, c0:c0 + 17:16], src_, AF.Copy, tk, [("rHS",)])
                k0_ = [("rTb", 0, t) for t in range(3)]
                k1_ = [("rTb", 1, t) for t in range(3)]
                P.tt(Tbd[0][:, :], Tbd[0][:, :], Tbd[1][:, :], ALU.add, k0_ + k1_, k0_)
                yk = [("rY", par, t) for t in range(3)]
                P.tt(M[:, n, :], Tbd[0][:, :], Y2[:, :], ALU.mult, k0_ + yk, [("M", n, 0), ("M", n, 1)])

            FEa(0)
            FEb(0)
            FEa(1)
            load_wg()
            GS(0)
            for n in range(8):
                BE_act(n)
                if n + 1 < 8:
                    FEb(n + 1)
                BE_dve(n)
                if n + 2 < 8:
                    FEa(n + 2)
                BE_fin(n)
                if n + 1 < 8:
                    GS(n + 1)
            P.tr(ps[0][0:32, 0:128], HS[:, 0:32], ident[:, :], [("rHS",), "ident"], [pk(0)])
            P.act(UC[0:32, 0:128], ps[0][0:32, 0:128], AF.Copy, [pk(0)], [("rUC", 0)])
            P.dma(SP, o_lru, UC[0:32, 0:128], reads=[("rUC", 0)])
            wo = []
            for half in range(2):
                src = w_lru_out[half * 512:(half + 1) * 512, :].rearrange("(k p) n -> p k n", p=128)
                wo.append(ring_load([(lambda s_: s_.rearrange("p (k n) -> p k n", k=4), src)]))
            P.barrier()
            for tt in range(3):
                for c in range(8):
                    t0, tn = TT[tt]
                    b = bank([0, 1, 2, 3])
                    for k in range(8):
                        slot, key = wo[k // 4]
                        w3 = slot.rearrange("p (k n) -> p k n", k=4)
                        P.mm(ps[b][:, :], w3[:, k % 4, c * 128:(c + 1) * 128], M[:, k, t0:t0 + 512], k == 0, k == 7, [key, ("M", k, GRP[tt])], [pk(b)])
                    resid(b, c, tt, L, 2)
                    if tile_hook is not None:
                        tile_hook(tt, c)

        for it in range(4):
            mod_item(0, it)
        mod_finish(0, 7, rng=(0, 16))
        derive(0, "a")

        def mod0_gate():
            mod_item(0, 4)
            mod_item(0, 5)
            mod_finish(0, 7, rng=(16, 24))
            derive(0, "g")
        for L in range(NLAYERS):
            kind = L % 3

            def tile_hook(tt, c, L=L):
                if tt == 1:
                    ln_stats_chunk(0, c)
                    if c == 7:
                        ln_stats_fin(0)
                elif tt == 2:
                    ln_stats_chunk(1, c)
                    ln_apply_chunk(0, L, 0, 1, c)
                    if c == 7:
                        ln_stats_fin(1)
            if kind == 0:
                pool_mixer(L, mid=(mod0_gate if L == 0 else None))
                P.barrier()
                ln_stats(0)
                ln_stats(1)
                if L == 0:
                    for it in range(6, 12):
                        mod_item(0, it, 5)
                    mod_finish(0, 5, 1)
                    derive(0, 1)
                ln_apply(0, L, 0, 1)
            elif kind == 1:
                mla_mixer(L, tile_hook)
            else:
                lru_mixer(L, tile_hook)

            def pre_hook(tt, ii, L=L):
                if (tt, ii) == (0, 2):
                    ln_apply(1, L, 0, 1)
                elif (tt, ii) == (1, 0):
                    ln_stats(2)
                elif (tt, ii) == (1, 2):
                    ln_apply(2, L, 0, 1)
            if L == 0:
                mla_prefetch()
            ffn(L, pre_hook, 1 if L == 0 else None, tail_mod=(L + 2 if L < 2 else None),
                out_hook=((lambda tt: emit_out(yout, (tt,))) if (L == 3 and not dbg) else None))
            if dbg:
                P.barrier()
                emit_out(dbg_out[L])
                P.barrier()
        if dbg:
            emit_out(yout)
        P.emit()
    return nc


NLAYERS = 4
_CACHE = {}


def _host_consts():
    ident = np.eye(128, dtype=np.float32)
    pband = np.zeros((4, 128, 12, 144), np.float32)
    for g, w in enumerate((2, 4, 8, 16)):
        for i in range(12):
            si_ = 0 if i < 2 else (1 if i < 4 else 2)
            so, T = SEQS[si_]
            ls = i * 128 - so
            tp = ls - 8 + np.arange(144)
            valid = (tp >= 0) & (tp < T)
            lo = np.clip(tp - w // 2, 0, T)
            hi = np.clip(tp + w - w // 2, 0, T)
            cnt = np.maximum(hi - lo, 1).astype(np.float32)
            tr = (ls + np.arange(128))[:, None]
            inw = (tr >= lo[None, :]) & (tr < hi[None, :])
            blk = inw.astype(np.float32) / cnt[None, :] - (tr == tp[None, :]).astype(np.float32)
            pband[g, :, i, :] = np.where(valid[None, :], blk, 0.0).astype(np.float32)
    t = np.arange(1024)
    rows = (t // 64).astype(np.float32)
    cols = (t % 64).astype(np.float32)
    inv = (np.float32(10000.0) ** (-np.arange(16, dtype=np.float32) / np.float32(16))).astype(np.float32)
    ang = np.stack([rows[:, None] * inv, cols[:, None] * inv], axis=1).astype(np.float32)
    cos, sin = np.cos(ang).astype(np.float32), np.sin(ang).astype(np.float32)
    rope = np.zeros((64, 2, 1024), np.float32)
    perm = np.zeros(64, np.int64)
    for a in range(2):
        for j in range(2):
            for f in range(16):
                i = a * 32 + j * 16 + f
                perm[i] = a * 32 + (1 - j) * 16 + f
                rope[i, 0] = cos[:, a, f]
                rope[i, 1] = -sin[:, a, f] if j == 0 else sin[:, a, f]
    return ident, pband, rope, perm


def _fm(a):
    a = np.asarray(a, np.float32)
    lead = a.shape[:-1]
    C = a.shape[-1] // 128
    a = a.reshape(lead + (C, 128))
    a = np.moveaxis(a, -1, 0)
    return np.ascontiguousarray(a.reshape(128, -1))


def kernel(x_prompt, x_sample, cache_mla_ckv, cache_mla_krope, state_lru, c, c_ctx,
           w_ada, b_ada, ln_g, ln_b, w_ffn_in, w_ffn_out, w_pool, pool_scale,
           w_dq, g_q, w_uq, w_dkv, g_kv, w_uk, w_uv, w_mla_o,
           w_lru_in, lru_conv_w, lru_conv_b, w_lru_a, b_lru_a, w_lru_i, b_lru_i,
           lru_lambda, w_lru_out, _dbg=False):
    f = lambda a: np.ascontiguousarray(np.asarray(a, dtype=np.float32))
    ident, pband, rope, perm = _host_consts()
    if ("nc", _dbg) not in _CACHE:
        _CACHE[("nc", _dbg)] = build_program(_dbg)
    nc = _CACHE[("nc", _dbg)]
    x_prompt, x_sample = f(x_prompt), f(x_sample)
    w_uq0 = f(w_uq)[0]
    w_dkv0 = f(w_dkv)[0]
    shared = {
        "ident": ident, "pband": pband, "rope": rope,
        "w_ada": f(w_ada), "w_ffn_in": f(w_ffn_in), "w_ffn_out": f(w_ffn_out), "w_pool": f(w_pool),
        "w_dq": f(w_dq)[0], "w_uq": np.ascontiguousarray(w_uq0.reshape(384, 1536)),
        "w_uq_rp": np.ascontiguousarray(w_uq0[:, :, 128:192][:, :, perm].reshape(384, 512)),
        "w_dkv": w_dkv0, "w_dkv_rp": np.ascontiguousarray(w_dkv0[:, 256:320][:, perm]),
        "w_uk": np.ascontiguousarray(f(w_uk)[0].reshape(256, 1024)), "w_uv": np.ascontiguousarray(f(w_uv)[0].reshape(256, 1024)),
        "w_mla_o": f(w_mla_o)[0], "w_lru_in": f(w_lru_in)[0], "w_lru_a": f(w_lru_a)[0], "w_lru_i": f(w_lru_i)[0],
        "w_lru_out": f(w_lru_out)[0],
    }
    one = np.ones((128, 1), np.float32)
    common = [_fm(f(b_ada)), _fm(f(ln_g)), _fm(f(ln_b)), _fm(f(pool_scale)), _fm(f(g_q)[0]), _fm(f(g_kv)[0]),
              _fm(f(lru_conv_w)[0]), _fm(f(lru_conv_b)[0]), _fm(f(b_lru_a)[0]), _fm(f(b_lru_i)[0]), _fm(f(lru_lambda)[0])]
    tail = [one * np.float32(EPS), one * np.float32(EPS_LN), one, one * np.float32(0.25)]
    in_maps = []
    for i in range(8):
        cond = np.stack([f(c_ctx), f(c)[i]], axis=0)
        condT = np.ascontiguousarray(cond.reshape(2, 8, 128).transpose(2, 1, 0).reshape(128, 16))
        vec = np.concatenate(common + [_fm(f(state_lru)[i, 0])] + tail, axis=1).astype(np.float32)
        assert vec.shape == (128, NV), vec.shape
        m = dict(shared)
        m.update({
            "xin": np.ascontiguousarray(np.concatenate([x_prompt[2 * i], x_prompt[2 * i + 1], x_sample[i]], axis=0)),
            "condT": condT, "vecs": np.ascontiguousarray(vec),
            "cache_ckv": f(cache_mla_ckv)[i, 0], "cache_kr": f(cache_mla_krope)[i, 0],
        })
        in_maps.append(m)
    res = run_bass_kernel_spmd(nc, in_maps, core_ids=list(range(8)))
    y_prompt = np.zeros((16, 256, D), np.float32)
    y_sample = np.zeros((8, 1024, D), np.float32)
    n_ckv = np.zeros((16, 1, 256, 256), np.float32)
    n_kr = np.zeros((16, 1, 256, 64), np.float32)
    n_lru = np.zeros((16, 1, 2, D), np.float32)
    for i in range(8):
        r = res.results[i]
        y_prompt[2 * i] = r["yout"][0:256]
        y_prompt[2 * i + 1] = r["yout"][256:512]
        y_sample[i] = r["yout"][512:]
        for s_ in range(2):
            n_ckv[2 * i + s_, 0] = r["o_ckv"][s_ * 256:(s_ + 1) * 256]
            n_kr[2 * i + s_, 0] = r["o_kr"][s_ * 256:(s_ + 1) * 256]
            n_lru[2 * i + s_, 0] = r["o_lru"][s_ * 16:(s_ + 1) * 16].reshape(2, D)
    if _dbg:
        kernel.dbg = [[res.results[i]["dbg%d" % k] for k in range(4)] for i in range(8)]
    return (y_prompt, y_sample, n_ckv, n_kr, n_lru)
```

```python
import contextlib
import math
import numpy as np
import concourse.bass as bass
import concourse.mybir as mybir
from concourse.bass_utils import run_bass_kernel_spmd

F32 = mybir.dt.float32
BF16 = mybir.dt.bfloat16
ALU = mybir.AluOpType
AF = mybir.ActivationFunctionType

PE, ACT, DVE, POOL, SP = "tensor", "scalar", "vector", "gpsimd", "sync"
ENGS = [PE, ACT, DVE, POOL, SP]
NDS = 24

D = 1024
NT = 1536
DFF = 2816
NFC = 22
ALPHA = 8.0 ** 0.25
EPS_LN = 1e-6 / (ALPHA * ALPHA)
EPS = 1e-6
MLA_SCALE = 192.0 ** -0.5
PADW = 16
SEQS = [(0, 256), (256, 256), (512, 1024)]
POFF = [PADW, 288 + PADW, 576 + PADW]
NTP = 1632
TT = [(0, 512), (512, 512), (1024, 512)]
TCOND = [0, 1, 1]


class Op:
    __slots__ = ("eng", "fn", "deps", "is_dma", "dma_sem", "dma_val", "dma_prev", "target", "count", "idx")


class Prog:
    def __init__(self, nc):
        self.nc = nc
        self.ops = {e: [] for e in ENGS}
        self.last_w = {}
        self.readers = {}
        self.ndma = 0
        self.bar_deps = []
        self.bar_pending = set()
        self.r_last = {}
        self.r_dmas = []

    def barrier(self):
        self.bar_deps = list(self.r_last.values()) + list(self.r_dmas)
        self.bar_pending = set(ENGS)
        self.r_dmas = []

    def op(self, eng, name, kwargs, reads=(), writes=(), dma=False, join=False):
        o = Op()
        meth, kw = name, dict(kwargs)
        o.fn = lambda e: getattr(e, meth)(**kw)
        o.eng, o.is_dma, o.target, o.count = eng, dma, False, 0
        o.idx = len(self.ops[eng])
        deps = set()
        isr = False
        for k in reads:
            if isinstance(k, tuple) and k[0][0] == "r":
                isr = True
            for lw in self.last_w.get(k, ()):
                if lw.is_dma or lw.eng != eng or eng != PE:
                    deps.add(lw)
        joined = {}
        for k in writes:
            if isinstance(k, tuple) and k[0][0] == "r":
                isr = True
            lws = self.last_w.get(k, [])
            jn = dma and join and len(lws) > 0 and all(w.is_dma for w in lws) and not self.readers.get(k)
            joined[k] = jn
            if not jn:
                for lw in lws:
                    if lw.is_dma or lw.eng != eng or dma:
                        deps.add(lw)
            for r in self.readers.get(k, {}).values():
                if r.is_dma or r.eng != eng or dma:
                    deps.add(r)
        if isr:
            if eng in self.bar_pending:
                self.bar_pending.discard(eng)
                for d in self.bar_deps:
                    if d.is_dma or d.eng != eng or dma:
                        deps.add(d)
            if dma:
                self.r_dmas.append(o)
            else:
                self.r_last[eng] = o
        deps.discard(o)
        o.deps = deps
        if dma:
            j = self.ndma
            self.ndma += 1
            o.dma_sem = j % NDS
            o.dma_val = 16 * (j // NDS + 1)
            o.dma_prev = 16 * (j // NDS)
        for k in writes:
            self.last_w[k] = (self.last_w.get(k, []) + [o]) if joined[k] else [o]
            self.readers[k] = {}
        for k in reads:
            self.readers.setdefault(k, {})[eng if not dma else ("dma", o.idx, eng)] = o
        self.ops[eng].append(o)
        return o

    def mm(self, out, lhsT, rhs, start, stop, reads, writes):
        return self.op(PE, "matmul", dict(out=out, lhsT=lhsT, rhs=rhs, start=start, stop=stop), reads, writes)

    def tr(self, out, in_, identity, reads, writes):
        return self.op(PE, "transpose", dict(out=out, in_=in_, identity=identity), reads, writes)

    def act(self, out, in_, func, reads, writes, bias=None, scale=None):
        kw = dict(out=out, in_=in_, func=func)
        if bias is not None:
            kw["bias"] = bias
        if scale is not None:
            kw["scale"] = scale
        return self.op(ACT, "activation", kw, reads, writes)

    def tt(self, out, in0, in1, op, reads, writes, eng=DVE):
        return self.op(eng, "tensor_tensor", dict(out=out, in0=in0, in1=in1, op=op), reads, writes)

    def ts(self, out, in0, s1, s2, op0, op1, reads, writes, eng=DVE):
        kw = dict(out=out, in0=in0, scalar1=s1, scalar2=s2, op0=op0)
        if op1 is not None:
            kw["op1"] = op1
        return self.op(eng, "tensor_scalar", kw, reads, writes)

    def stt(self, out, in0, scalar, in1, op0, op1, reads, writes):
        return self.op(DVE, "scalar_tensor_tensor", dict(out=out, in0=in0, scalar=scalar, in1=in1, op0=op0, op1=op1), reads, writes)

    def cp(self, out, in_, reads, writes, eng=DVE):
        return self.op(eng, "tensor_copy", dict(out=out, in_=in_), reads, writes)

    def recip(self, out, in_, reads, writes):
        return self.op(DVE, "reciprocal", dict(out=out, in_=in_), reads, writes)

    def memset(self, ap, val, writes, eng=DVE):
        return self.op(eng, "memset", dict(ap=ap, constant=val), (), writes)

    def scan(self, out, d0, d1, initial, reads, writes):
        return self.op(DVE, "tensor_tensor_scan", dict(out=out, data0=d0, data1=d1, initial=initial, op0=ALU.mult, op1=ALU.add), reads, writes)

    def dma(self, eng, out, in_, reads=(), writes=(), join=False):
        return self.op(eng, "dma_start", dict(out=out, in_=in_), reads, writes, dma=True, join=join)

    def emit(self):
        nc = self.nc
        for e in ENGS:
            for o in self.ops[e]:
                for d in o.deps:
                    if not d.is_dma:
                        d.target = True
        for e in ENGS:
            c = 0
            for o in self.ops[e]:
                if o.target and not o.is_dma:
                    c += 1
                    o.count = c
        with contextlib.ExitStack() as st:
            esem = {e: st.enter_context(nc.semaphore("es_" + e)) for e in ENGS}
            dsem = [st.enter_context(nc.semaphore("ds%d" % i)) for i in range(NDS)]
            block = st.enter_context(nc.Block())
            for e in ENGS:
                ops = self.ops[e]
                if not ops:
                    continue

                def body(eng, ops=ops, e=e):
                    known = {}

                    def wait(key, sem, val):
                        if known.get(key, 0) >= val:
                            return
                        known[key] = val
                        eng.wait_ge(sem, val)

                    for o in ops:
                        for d in sorted(o.deps, key=lambda d: (d.eng, d.idx)):
                            if d.is_dma:
                                wait(("d", d.dma_sem), dsem[d.dma_sem], d.dma_val)
                            else:
                                wait(("e", d.eng), esem[d.eng], d.count)
                        if o.is_dma and o.dma_prev > 0:
                            wait(("d", o.dma_sem), dsem[o.dma_sem], o.dma_prev)
                        ins = o.fn(eng)
                        if o.is_dma:
                            ins.then_inc(dsem[o.dma_sem], 16)
                        elif o.target:
                            ins.then_inc(esem[e], 1)
                    for o in ops:
                        if o.is_dma:
                            wait(("d", o.dma_sem), dsem[o.dma_sem], o.dma_val)

                getattr(block, e)(body)


VSPEC = [("b_ada", (4, 48)), ("ln_g", (4, 2, 8)), ("ln_b", (4, 2, 8)), ("pool_scale", (2, 8)), ("g_q", (3,)), ("g_kv", (2,)),
         ("conv_w", (4, 8)), ("conv_b", (8,)), ("b_a", (2, 8)), ("b_i", (2, 8)), ("lam", (16,)), ("state", (2, 8)),
         ("eps", (1,)), ("epsln", (1,)), ("one", (1,)), ("quarter", (1,))]
VOFF = {}
_o = 0
for _n, _s in VSPEC:
    VOFF[_n] = (_o, _s)
    _o += int(np.prod(_s))
NV = _o


def build_program(dbg=False):
    nc = bass.Bass("TRN2", target_bir_lowering=False)

    def din(name, shape):
        return nc.dram_tensor(name, list(shape), F32, kind="ExternalInput").ap()

    def dout(name, shape):
        return nc.dram_tensor(name, list(shape), F32, kind="ExternalOutput").ap()

    xin = din("xin", [NT, D])
    condT = din("condT", [128, 16])
    vecs_d = din("vecs", [128, NV])
    ident_d = din("ident", [128, 128])
    pband_d = din("pband", [4, 128, 12, 144])
    rope_d = din("rope", [64, 2, 1024])
    cache_ckv = din("cache_ckv", [256, 256])
    cache_kr = din("cache_kr", [256, 64])
    w_ada = din("w_ada", [4, D, 6 * D])
    w_ffn_in = din("w_ffn_in", [4, D, 2 * DFF])
    w_ffn_out = din("w_ffn_out", [4, DFF, D])
    w_pool = din("w_pool", [2, 4, 256, 256])
    w_dq = din("w_dq", [D, 384])
    w_uq = din("w_uq", [384, 1536])
    w_uq_rp = din("w_uq_rp", [384, 512])
    w_dkv = din("w_dkv", [D, 320])
    w_dkv_rp = din("w_dkv_rp", [D, 64])
    w_uk = din("w_uk", [256, 1024])
    w_uv = din("w_uv", [256, 1024])
    w_mla_o = din("w_mla_o", [D, D])
    w_lru_in = din("w_lru_in", [D, 2 * D])
    w_lru_a = din("w_lru_a", [2, 8, 128, 128])
    w_lru_i = din("w_lru_i", [2, 8, 128, 128])
    w_lru_out = din("w_lru_out", [D, D])

    yout = dout("yout", [NT, D])
    o_ckv = dout("o_ckv", [512, 256])
    o_kr = dout("o_kr", [512, 64])
    o_lru = dout("o_lru", [32, 128])
    dbg_out = [dout("dbg%d" % i, [NT, D]) for i in range(4)] if dbg else None

    st = contextlib.ExitStack()
    with st:
        def sb(name, shape, dtp):
            return st.enter_context(nc.sbuf_tensor(name, list(shape), dtp))

        Xt = sb("X", [128, 8 * NT], F32)
        Ht = sb("H", [128, 8 * NT], BF16)
        X = Xt[:, :].rearrange("p (c t) -> p c t", c=8)
        H = Ht[:, :].rearrange("p (c t) -> p c t", c=8)
        NSLOT = 4
        RINGW = 4096
        ring_t = sb("ring", [128, NSLOT * RINGW], BF16)
        vecs = sb("vecs_sb", [128, NV], F32)
        ident = sb("ident_sb", [128, 128], F32)
        ones_b = sb("ones_b", [128, 128], BF16)
        scond = sb("scond", [128, 16], BF16)
        condf = sb("condf", [128, 16], F32)
        MOD = sb("MOD", [128, 4 * 2 * 48], F32)
        DER = sb("DER", [128, 4 * 2 * 7 * 8], F32)
        RBYTES = 96 * 1024
        Rt = sb("R", [128, RBYTES // 4], F32)
        ps = [st.enter_context(nc.psum_tensor("ps%d" % i, [128, 512], F32)) for i in range(8)]

        P = Prog(nc)
        XST_OFF = 68 * 1024

        def view(off, dtp, *dims):
            n = int(np.prod(dims))
            assert off % 4 == 0
            if dtp == F32:
                assert off + 4 * n <= RBYTES, (off, n)
                ap = Rt[:, off // 4: off // 4 + n]
            else:
                assert n % 2 == 0 and off + 2 * n <= RBYTES, (off, n)
                ap = Rt[:, off // 4: off // 4 + n // 2].bitcast(BF16)
            if len(dims) == 2:
                ap = ap.rearrange("p (a b) -> p a b", a=dims[0])
            elif len(dims) == 3:
                ap = ap.rearrange("p (a b c) -> p a b c", a=dims[0], b=dims[1])
            return ap

        class Alloc:
            def __init__(self, off=0):
                self.off = off

            def __call__(self, dtp, *dims):
                n = int(np.prod(dims)) * (4 if dtp == F32 else 2)
                o = self.off
                self.off += (n + 31) // 32 * 32
                return view(o, dtp, *dims)

        ring_n = [0]

        def ring_load(items):
            s_ = ring_n[0] % NSLOT
            ring_n[0] += 1
            slot = ring_t[:, s_ * RINGW:(s_ + 1) * RINGW]
            key = ("wring", s_)
            for i, (dst_fn, src) in enumerate(items):
                P.dma(POOL, dst_fn(slot), src, writes=[key], join=(i > 0))
            return slot, key

        def wload(dst, src, key, join=False):
            P.dma(POOL, dst, src, writes=[key], join=join)

        rot = {}

        def bank(pool):
            pool = tuple(pool)
            i = rot.get(pool, 0)
            rot[pool] = i + 1
            return pool[i % len(pool)]

        def pk(b):
            return ("ps", b)

        def V(name, *idx):
            o, shape = VOFF[name]
            i = 0
            for k, s_ in zip(idx, shape[:len(idx)]):
                i = i * s_ + k
            rest = int(np.prod(shape[len(idx):])) if len(idx) < len(shape) else 1
            return vecs[:, o + i * rest: o + (i + 1) * rest]

        def MODv(L, j, kind):
            o = (L * 2 + j) * 48 + kind * 8
            return MOD[:, o:o + 8]

        def DERv(L, j, kind):
            o = ((L * 2 + j) * 7 + kind) * 8
            return DER[:, o:o + 8]

        P.dma(SP, vecs[:, :], vecs_d, writes=["vecs"])
        P.dma(SP, ident[:, :], ident_d, writes=["ident"])
        P.dma(SP, condf[:, :], condT, writes=["condf"])
        P.memset(ones_b[:, :], 1.0, ["ones"])
        P.act(scond[:, :], condf[:, :], AF.Silu, ["condf"], ["scond"])
        scond3 = scond[:, :].rearrange("p (a b) -> p a b", a=8)

        stage_t = view(XST_OFF, F32, 2 * D)
        for ti in range(12):
            sbuf = stage_t[:, (ti % 2) * D:(ti % 2 + 1) * D]
            sk = ("rxstage", ti % 2)
            P.dma(SP, sbuf, xin[ti * 128:(ti + 1) * 128, :], writes=[sk])
            tt = ti // 4
            for half in range(2):
                b = bank([0, 1, 2, 3])
                for cc in range(4):
                    c = half * 4 + cc
                    P.tr(ps[b][:, cc * 128:(cc + 1) * 128], sbuf[:, c * 128:(c + 1) * 128], ident[:, :], [sk, "ident"], [pk(b)])
                dst = X[:, half * 4:half * 4 + 4, ti * 128:(ti + 1) * 128]
                src = ps[b][:, :].rearrange("p (a b) -> p a b", a=4)
                wk = [("X", half * 4 + cc, tt) for cc in range(4)]
                if half:
                    P.act(dst, src, AF.Copy, [pk(b)], wk)
                else:
                    P.cp(dst, src, [pk(b)], wk)

        def emit_out(dstd, tts=(0, 1, 2)):
            for ti in range(12):
                tt = ti // 4
                if tt not in tts:
                    continue
                sbuf = stage_t[:, (ti % 2) * D:(ti % 2 + 1) * D]
                for half in range(2):
                    b = bank([0, 1, 2, 3])
                    for cc in range(4):
                        c = half * 4 + cc
                        P.tr(ps[b][:, cc * 128:(cc + 1) * 128], X[:, c, ti * 128:(ti + 1) * 128], ident[:, :], [("X", c, tt), "ident"], [pk(b)])
                    sk = ("rxstage", ti % 2, half)
                    if half:
                        P.act(sbuf[:, half * 512:(half + 1) * 512], ps[b][:, :], AF.Copy, [pk(b)], [sk])
                    else:
                        P.cp(sbuf[:, half * 512:(half + 1) * 512], ps[b][:, :], [pk(b)], [sk])
                P.dma(SP, dstd[ti * 128:(ti + 1) * 128, :], sbuf, reads=[("rxstage", ti % 2, 0), ("rxstage", ti % 2, 1)])

        def mod_item(L, it, b=7):
            src = w_ada[L, :, it * 512:(it + 1) * 512].rearrange("(k p) n -> p k n", p=128)
            slot, key = ring_load([(lambda s_: s_.rearrange("p (k n) -> p k n", k=8), src)])
            s3 = slot.rearrange("p (k n) -> p k n", k=8)
            for oc in range(4):
                o = it * 4 + oc
                for k in range(8):
                    P.mm(ps[b][:, 2 * o:2 * o + 2], s3[:, k, oc * 128:(oc + 1) * 128], scond3[:, k, :], k == 0, k == 7, [key, "scond"], [pk(b)])

        def mod_finish(L, b=7, part=None, rng=None):
            lo, hi = (0, 48) if part is None else ((0, 24) if part == 0 else (24, 48))
            if rng is not None:
                lo, hi = rng
            for j in range(2):
                o = (L * 2 + j) * 48
                P.tt(MOD[:, o + lo:o + hi], ps[b][:, 2 * lo + j:2 * hi:2], V("b_ada", L)[:, lo:hi], ALU.add, [pk(b), "vecs"], [("MOD", L, j)])

        def modulation(L):
            for it in range(12):
                mod_item(L, it)
            mod_finish(L)

        def derive(L, part=None):
            for j in range(2):
                rk = [("MOD", L, j), "vecs"]
                wk = [("DER", L, j)]
                if part in (None, 0, "a"):
                    P.ts(DERv(L, j, 0), MODv(L, j, 1), 1.0, None, ALU.add, None, rk, wk)
                    P.cp(DERv(L, j, 1), MODv(L, j, 0), rk, wk)
                if part in (None, 0, "g"):
                    if L % 3 == 0:
                        P.stt(DERv(L, j, 2), MODv(L, j, 2), 1.0 / ALPHA, V("pool_scale", L // 3), ALU.mult, ALU.mult, rk, wk)
                    else:
                        P.ts(DERv(L, j, 2), MODv(L, j, 2), 1.0 / ALPHA, None, ALU.mult, None, rk, wk)
                if part in (None, 1):
                    P.ts(DERv(L, j, 3), MODv(L, j, 4), 1.0, None, ALU.add, None, rk, wk)
                    P.tt(DERv(L, j, 4), V("ln_b", L, 0), DERv(L, j, 3), ALU.mult, rk + wk, wk)
                    P.tt(DERv(L, j, 4), DERv(L, j, 4), MODv(L, j, 3), ALU.add, rk + wk, wk)
                    P.ts(DERv(L, j, 5), MODv(L, j, 5), 1.0 / ALPHA, None, ALU.mult, None, rk, wk)

        def derive_next(L):
            for j in range(2):
                rk = [("DER", L + 1, j), "vecs"]
                wk = [("DERN", L, j)]
                P.tt(DERv(L, j, 6), V("ln_b", L, 1), DERv(L + 1, j, 0), ALU.mult, rk, wk)
                P.tt(DERv(L, j, 6), DERv(L, j, 6), DERv(L + 1, j, 1), ALU.add, rk + wk, wk)

        LNOFF = 36 * 1024 + 3072

        def ln_bufs():
            A = Alloc(LNOFF)
            Zb = A(BF16, 4, 512)
            Sq = A(BF16, 4, 512)
            mean = A(F32, 3, 512)
            rstd = A(F32, 3, 512)
            T1 = A(F32, 2, 512)
            V1 = A(F32, 2, 512)
            assert A.off <= RBYTES
            return Zb, Sq, mean, rstd, T1, V1

        def ln_stats_chunk(tt, c):
            t0, tn = TT[tt]
            Zb, Sq, mean, rstd, T1, V1 = ln_bufs()
            bm, bs = 6, 7
            r = c % 4
            if c % 2 == 0:
                xs2 = X[:, c:c + 2, t0:t0 + tn]
                rk2 = [("X", c, tt), ("X", c + 1, tt)]
                P.act(Zb[:, r:r + 2, :], xs2, AF.Copy, rk2, [("rZb", r), ("rZb", r + 1)])
                P.act(Sq[:, r:r + 2, :], xs2, AF.Square, rk2, [("rSq", r), ("rSq", r + 1)])
            P.mm(ps[bm][:, :], ones_b[:, :], Zb[:, r, :], c == 0, c == 7, [("rZb", r), "ones"], [pk(bm)])
            P.mm(ps[bs][:, :], ones_b[:, :], Sq[:, r, :], c == 0, c == 7, [("rSq", r), "ones"], [pk(bs)])

        def ln_stats_fin(tt):
            Zb, Sq, mean, rstd, T1, V1 = ln_bufs()
            mean, rstd = mean[:, tt, :], rstd[:, tt, :]
            bm, bs = 6, 7
            P.ts(mean, ps[bm][:, :], 1.0 / D, None, ALU.mult, None, [pk(bm)], [("rmean", tt)])
            P.tt(rstd, mean, mean, ALU.mult, [("rmean", tt)], [("rrstd", tt)])
            P.stt(rstd, ps[bs][:, :], 1.0 / D, rstd, ALU.mult, ALU.subtract, [pk(bs), ("rrstd", tt)], [("rrstd", tt)])
            P.act(rstd, rstd, AF.Ln, [("rrstd", tt), "vecs"], [("rrstd", tt)], bias=V("epsln"), scale=1.0)
            P.act(rstd, rstd, AF.Exp, [("rrstd", tt)], [("rrstd", tt)], scale=-0.5)

        def ln_stats(tt):
            for c in range(8):
                ln_stats_chunk(tt, c)
            ln_stats_fin(tt)

        def ln_apply_chunk(tt, L, which, hmode, c):
            t0, tn = TT[tt]
            Zb, Sq, mean, rstd, T1, V1 = ln_bufs()
            mean, rstd = mean[:, tt, :], rstd[:, tt, :]
            j = TCOND[tt]
            r = c % 2
            xs = X[:, c, t0:t0 + tn]
            P.tt(T1[:, r, :], xs, mean, ALU.subtract, [("X", c, tt), ("rmean", tt)], [("rT1", r)])
            P.stt(V1[:, r, :], T1[:, r, :], V("ln_g", L, which)[:, c:c + 1], rstd, ALU.mult, ALU.mult, [("rT1", r), ("rrstd", tt), "vecs"], [("rV1", r)])
            P.act(xs, V1[:, r, :], AF.Identity, [("rV1", r), "vecs"], [("X", c, tt)], bias=V("ln_b", L, which)[:, c:c + 1], scale=1.0)
            if hmode == 1:
                P.act(H[:, c, t0:t0 + tn], V1[:, r, :], AF.Identity, [("rV1", r), ("DER", L, j)], [("H", c, tt)],
                      bias=DERv(L, j, 4)[:, c:c + 1], scale=DERv(L, j, 3)[:, c:c + 1])
            elif hmode == 2:
                P.act(H[:, c, t0:t0 + tn], V1[:, r, :], AF.Identity, [("rV1", r), ("DER", L + 1, j), ("DERN", L, j)], [("H", c, tt)],
                      bias=DERv(L, j, 6)[:, c:c + 1], scale=DERv(L + 1, j, 0)[:, c:c + 1])

        def ln_apply(tt, L, which, hmode):
            for c in range(8):
                ln_apply_chunk(tt, L, which, hmode, c)

        def resid(b, c, tt, L, kind):
            j = TCOND[tt]
            t0, tn = TT[tt]
            xs = X[:, c, t0:t0 + tn]
            P.stt(xs, ps[b][:, :tn], DERv(L, j, kind)[:, c:c + 1], xs, ALU.mult, ALU.add, [pk(b), ("X", c, tt), ("DER", L, j)], [("X", c, tt)])

        def ffn(L, pre_hook, mod_next, tail_mod=None, out_hook=None):
            A = Alloc()
            G = A(BF16, 11, NT)
            SA = A(BF16, 3, 512)
            assert A.off <= LNOFF
            state = {"nsa": 0, "mod": 0}

            def load_in(half, jj):
                js = [half * 11 + jj * 2 + s_ for s_ in range(2) if jj * 2 + s_ < 11]
                nj = len(js)
                j0 = js[0]
                items = []
                for ab in range(2):
                    src = w_ffn_in[L, :, ab * DFF + j0 * 128: ab * DFF + (j0 + nj) * 128].rearrange("(k p) n -> p k n", p=128)
                    items.append((lambda s_, ab=ab, nj=nj: s_.rearrange("p (k a n) -> p k a n", k=8, a=2)[:, :, ab, :nj * 128], src))
                slot, key = ring_load(items)
                return js, slot.rearrange("p (k a n) -> p k a n", k=8, a=2), key

            def p1(half, js, s4, key, tts):
                for si, jf in enumerate(js):
                    gi = jf - half * 11
                    for tt in tts:
                        t0, tn = TT[tt]
                        ba = bank([0, 1, 2, 3])
                        bb = bank([0, 1, 2, 3])
                        for ab, b in ((0, ba), (1, bb)):
                            for k in range(8):
                                P.mm(ps[b][:, :tn], s4[:, k, ab, si * 128:(si + 1) * 128], H[:, k, t0:t0 + tn], k == 0, k == 7, [key, ("H", k, tt)], [pk(b)])
                        r = state["nsa"] % 3
                        state["nsa"] += 1
                        P.act(SA[:, r, :], ps[ba][:, :], AF.Silu, [pk(ba)], [("rSA", r)])
                        P.tt(G[:, gi, t0:t0 + tn], SA[:, r, :], ps[bb][:, :], ALU.mult, [("rSA", r), pk(bb)], [("rG", gi, tt)])

            def mod_step():
                if mod_next is not None and state["mod"] < 12:
                    mod_item(mod_next, state["mod"])
                    state["mod"] += 1

            def load_out(half, cp_):
                items = []
                for cs in range(2):
                    c = cp_ * 2 + cs
                    src = w_ffn_out[L, half * 11 * 128:(half + 1) * 11 * 128, c * 128:(c + 1) * 128].rearrange("(g p) n -> p g n", p=128)
                    items.append((lambda s_, cs=cs: s_[:, cs * 1408:(cs + 1) * 1408].rearrange("p (g n) -> p g n", g=11), src))
                return ring_load(items)

            def p2(slot, key, cs, c, tt):
                w3 = slot[:, cs * 1408:(cs + 1) * 1408].rearrange("p (g n) -> p g n", g=11)
                t0, tn = TT[tt]
                b = bank([4, 5])
                for gi in range(11):
                    P.mm(ps[b][:, :tn], w3[:, gi, :], G[:, gi, t0:t0 + tn], gi == 0, gi == 10, [key, ("rG", gi, tt)], [pk(b)])
                resid(b, c, tt, L, 5)

            NPRO = 3
            pro = [load_in(0, jj) for jj in range(NPRO)]
            for tt in range(3):
                for ii, (js, s4, key) in enumerate(pro):
                    pre_hook(tt, ii)
                    p1(0, js, s4, key, [tt])
            for jj in range(NPRO, 6):
                js, s4, key = load_in(0, jj)
                p1(0, js, s4, key, [0, 1, 2])
                mod_step()
                mod_step()
            for cp_ in range(4):
                slot, key = load_out(0, cp_)
                for cs in range(2):
                    for tt in range(3):
                        p2(slot, key, cs, cp_ * 2 + cs, tt)
                if cp_ >= 2:
                    mod_step()
            for jj in range(6):
                js, s4, key = load_in(1, jj)
                p1(1, js, s4, key, [0, 1, 2])
                mod_step()
                if tail_mod is not None:
                    mod_item(tail_mod, jj, 6)
            if tail_mod is not None:
                mod_finish(tail_mod, 6, 0)
            while mod_next is not None and state["mod"] < 12:
                mod_step()
            if mod_next is not None:
                mod_finish(mod_next)
                derive(mod_next)
            if L < 3:
                derive_next(L)
            outs = [load_out(1, cp_) for cp_ in range(4)]
            hm = 2 if L < 3 else 0

            def p2tile(tt, hook=None):
                for cp_ in range(4):
                    slot, key = outs[cp_]
                    for cs in range(2):
                        p2(slot, key, cs, cp_ * 2 + cs, tt)
                        if hook is not None:
                            hook(cp_ * 2 + cs)
            p2tile(0)
            p2tile(1, lambda c: ln_stats_chunk(0, c))
            ln_stats_fin(0)

            def hk2(c):
                ln_stats_chunk(1, c)
                ln_apply_chunk(0, L, 1, hm, c)
            p2tile(2, hk2)
            ln_stats_fin(1)
            if out_hook is not None:
                out_hook(0)
            tm = list(range(6, 12)) if tail_mod is not None else []
            for c in range(8):
                ln_stats_chunk(2, c)
                ln_apply_chunk(1, L, 1, hm, c)
                if tm and c % 2 == 1:
                    mod_item(tail_mod, tm.pop(0), 5)
            ln_stats_fin(2)
            if out_hook is not None:
                out_hook(1)
            while tm:
                mod_item(tail_mod, tm.pop(0), 5)
            ln_apply(2, L, 1, hm)
            if out_hook is not None:
                out_hook(2)
            if tail_mod is not None:
                mod_finish(tail_mod, 5, 1)
                derive(tail_mod)

        def pair(ap2d, base, width, off, n):
            return ap2d[:, base:base + 2 * width].rearrange("p (s t) -> p s t", s=2)[:, :, off:off + n]

        def pool_mixer(L, mid=None):
            jw = L // 3
            P.barrier()
            A = Alloc()
            Wp = A(BF16, 4, 2, 256)
            Bt = A(BF16, 4, 12, 144)
            Zb = A(BF16, 12, 1024)
            assert A.off <= RBYTES, A.off
            for g in range(4):
                wload(Wp[:, g, :, :], w_pool[jw, g].rearrange("(k p) n -> p k n", p=128), ("rWp", g))
            for g in range(4):
                wload(Bt[:, g, :, :], pband_d[g], ("rBt", g))
            if L == 0:
                for tt in range(3):
                    j = TCOND[tt]
                    t0, tn = TT[tt]
                    for c in range(8):
                        P.act(H[:, c, t0:t0 + tn], X[:, c, t0:t0 + tn], AF.Identity, [("X", c, tt), ("DER", L, j)], [("H", c, tt)],
                              bias=DERv(L, j, 1)[:, c:c + 1], scale=DERv(L, j, 0)[:, c:c + 1])
            for i in range(12):
                tt = i // 4
                for half in range(2):
                    b = bank([0, 1, 2, 3])
                    for gg in range(2):
                        g = half * 2 + gg
                        for kc in range(2):
                            P.mm(ps[b][:, gg * 256:(gg + 1) * 256], H[:, 2 * g + kc, i * 128:(i + 1) * 128], Wp[:, g, kc, :], kc == 0, kc == 1,
                                 [("rWp", g), ("H", 2 * g + kc, tt)], [pk(b)])
                    if half:
                        P.act(Zb[:, i, 512:1024], ps[b][:, :], AF.Copy, [pk(b)], [("rZ", i, 1)])
                    else:
                        P.cp(Zb[:, i, 0:512], ps[b][:, :], [pk(b)], [("rZ", i, 0)])
            if mid is not None:
                mid()
            for c in range(8):
                g, dc = c // 2, c % 2
                for tt in range(3):
                    t0, tn = TT[tt]
                    contribs = []
                    for i in range(12):
                        si_ = 0 if i < 2 else (1 if i < 4 else 2)
                        so, T = SEQS[si_]
                        ls = i * 128 - so
                        lo = max(so + max(0, ls - 8), t0)
                        hi = min(so + min(T, ls + 136), t0 + tn)
                        if hi > lo:
                            contribs.append((i, lo - t0, hi - t0, (lo - so) - (ls - 8)))
                    b = bank([4, 5, 6, 7])
                    for idx, (i, clo, chi, bclo) in enumerate(contribs):
                        P.op(PE, "matmul", dict(out=ps[b][:, clo:chi], lhsT=Zb[:, i, g * 256 + dc * 128:g * 256 + (dc + 1) * 128], rhs=Bt[:, g, i, bclo:bclo + (chi - clo)],
                                                start=(idx == 0), stop=(idx == len(contribs) - 1), skip_group_check=True),
                             [("rZ", i, g // 2), ("rBt", g)], [pk(b)])
                    resid(b, c, tt, L, 2)

        def mla_small_weights():
            Aw = Alloc(76 * 1024)
            return Aw(BF16, 8, 384), Aw(BF16, 8, 384), Aw(BF16, 2, 1024), Aw(BF16, 2, 1024)

        def mla_prefetch():
            Wdq, Wdkv, Wuk, Wuv = mla_small_weights()
            kp = "(k p) n -> p k n"
            wload(Wdq, w_dq.rearrange(kp, p=128), ("rWdq",))
            wload(Wdkv[:, :, 0:320], w_dkv.rearrange(kp, p=128), ("rWdkv",))
            wload(Wdkv[:, :, 320:384], w_dkv_rp.rearrange(kp, p=128), ("rWdkv",), join=True)
            wload(Wuk, w_uk.rearrange(kp, p=128), ("rWuk",))
            wload(Wuv, w_uv.rearrange(kp, p=128), ("rWuv",))

        def mla_mixer(L, tile_hook=None):
            P.barrier()
            Wdq, Wdkv, Wuk, Wuv = mla_small_weights()
            A = Alloc()
            Wuq = A(BF16, 3, 2048)
            kp = "(k p) n -> p k n"
            wload(Wuq[:, :, 0:1536], w_uq.rearrange(kp, p=128), ("rWuq",))
            wload(Wuq[:, :, 1536:2048], w_uq_rp.rearrange(kp, p=128), ("rWuq",), join=True)
            NK = 1792
            QLn = A(BF16, 3, NT)
            CK = A(BF16, 2, NK)
            KR = A(BF16, NK)
            ROPE = A(F32, 2, 1024)
            PT = A(BF16, 4, 512)
            rden = A(F32, 2, 512)
            hb0 = [A(BF16, 1280), A(BF16, 10, 128), A(BF16, 1024), A(BF16, 1024), A(F32, 2, 512)]
            alias_off = A.off
            CKf = A(F32, 2, 512)
            KRf = A(F32, 512)
            SQ = A(BF16, 3, 512)
            rs = A(F32, 512)
            cst = A(F32, 2, 256)
            assert A.off <= 76 * 1024, A.off
            A2 = Alloc(alias_off)
            hb1 = [A2(BF16, 1280), A2(BF16, 10, 128), A2(BF16, 1024), A2(BF16, 1024), A2(F32, 2, 512)]
            assert A2.off <= A.off
            RT = hb0[4]
            P.memset(KR[64:128, :], 0.0, [("rKRpad",)])
            P.dma(SP, ROPE[0:64, :, :], rope_d, writes=[("rROPE",)])
            for t2 in range(2):
                P.dma(SP, cst[:, 0, :], cache_ckv[t2 * 128:(t2 + 1) * 128, :], writes=[("rcst", 0)])
                P.dma(SP, cst[:, 1, 0:64], cache_kr[t2 * 128:(t2 + 1) * 128, :], writes=[("rcst", 1)])
                for rc in range(2):
                    P.tr(ps[0][:, rc * 128:(rc + 1) * 128], cst[:, 0, rc * 128:(rc + 1) * 128], ident[:, :], [("rcst", 0), "ident"], [pk(0)])
                P.tr(ps[1][0:64, 0:128], cst[:, 1, 0:64], ident[:, :], [("rcst", 1), "ident"], [pk(1)])
                P.cp(CK[:, :, 512 + t2 * 128:512 + (t2 + 1) * 128], ps[0][:, 0:256].rearrange("p (a b) -> p a b", a=2), [pk(0)], [("rCK", 3 + t2)])
                P.cp(KR[0:64, 512 + t2 * 128:512 + (t2 + 1) * 128], ps[1][0:64, 0:128], [pk(1)], [("rKR", 3 + t2)])
            for tt in range(3):
                t0, tn = TT[tt]
                qb = [0, 1, 2]
                for rc in range(3):
                    for k in range(8):
                        P.mm(ps[qb[rc]][:, :], Wdq[:, k, rc * 128:(rc + 1) * 128], H[:, k, t0:t0 + 512], k == 0, k == 7, [("rWdq",), ("H", k, tt)], [pk(qb[rc])])
                    P.act(SQ[:, rc, :], ps[qb[rc]][:, :], AF.Square, [pk(qb[rc])], [("rSQ", rc)])
                for rc in range(3):
                    P.mm(ps[6][:, :], ones_b[:, :], SQ[:, rc, :], rc == 0, rc == 2, [("rSQ", rc), "ones"], [pk(6)])
                P.act(rs, ps[6][:, :], AF.Ln, [pk(6), "vecs"], [("rrs",)], bias=V("eps"), scale=1.0 / 384)
                P.act(rs, rs, AF.Exp, [("rrs",)], [("rrs",)], scale=-0.5)
                for rc in range(3):
                    P.stt(QLn[:, rc, t0:t0 + 512], ps[qb[rc]][:, :], V("g_q")[:, rc:rc + 1], rs, ALU.mult, ALU.mult, [pk(qb[rc]), ("rrs",), "vecs"], [("rQLn", tt)])
                kb = [3, 4]
                for rc in range(2):
                    for k in range(8):
                        P.mm(ps[kb[rc]][:, :], Wdkv[:, k, rc * 128:(rc + 1) * 128], H[:, k, t0:t0 + 512], k == 0, k == 7, [("rWdkv",), ("H", k, tt)], [pk(kb[rc])])
                    P.act(SQ[:, rc, :], ps[kb[rc]][:, :], AF.Square, [pk(kb[rc])], [("rSQ", rc)])
                for rc in range(2):
                    P.mm(ps[7][:, :], ones_b[:, :], SQ[:, rc, :], rc == 0, rc == 1, [("rSQ", rc), "ones"], [pk(7)])
                P.act(rs, ps[7][:, :], AF.Ln, [pk(7), "vecs"], [("rrs",)], bias=V("eps"), scale=1.0 / 256)
                P.act(rs, rs, AF.Exp, [("rrs",)], [("rrs",)], scale=-0.5)
                koff = 0 if tt == 0 else 768 + (tt - 1) * 512
                for rc in range(2):
                    if tt == 0:
                        P.stt(CKf[:, rc, :], ps[kb[rc]][:, :], V("g_kv")[:, rc:rc + 1], rs, ALU.mult, ALU.mult, [pk(kb[rc]), ("rrs",), "vecs"], [("rCKf",)])
                        P.act(CK[:, rc, 0:512], CKf[:, rc, :], AF.Copy, [("rCKf",)], [("rCK", 0)])
                    else:
                        P.stt(CK[:, rc, koff:koff + 512], ps[kb[rc]][:, :], V("g_kv")[:, rc:rc + 1], rs, ALU.mult, ALU.mult, [pk(kb[rc]), ("rrs",), "vecs"], [("rCK", tt)])
                for k in range(8):
                    P.mm(ps[5][0:64, :], Wdkv[:, k, 256:320], H[:, k, t0:t0 + 512], k == 0, k == 7, [("rWdkv",), ("H", k, tt)], [pk(5)])
                if tt == 0:
                    P.act(KRf[0:64, :], ps[5][0:64, :], AF.Copy, [pk(5)], [("rKRf",)])
                    P.cp(KR[0:64, 0:512], ps[5][0:64, :], [pk(5)], [("rKR", 0)])
                else:
                    s0 = (tt - 1) * 512
                    for k in range(8):
                        P.mm(ps[6][0:64, :], Wdkv[:, k, 320:384], H[:, k, t0:t0 + 512], k == 0, k == 7, [("rWdkv",), ("H", k, tt)], [pk(6)])
                    P.tt(RT[0:64, 0, :], ps[5][0:64, :], ROPE[0:64, 0, s0:s0 + 512], ALU.mult, [pk(5), ("rROPE",)], [("rRT", 0, 0)])
                    P.tt(RT[0:64, 1, :], ps[6][0:64, :], ROPE[0:64, 1, s0:s0 + 512], ALU.mult, [pk(6), ("rROPE",)], [("rRT", 0, 1)])
                    P.tt(KR[0:64, koff:koff + 512], RT[0:64, 0, :], RT[0:64, 1, :], ALU.add, [("rRT", 0, 0), ("rRT", 0, 1)], [("rKR", tt)])
                if tt == 0:
                    for t4 in range(4):
                        for rc in range(2):
                            P.tr(ps[2][:, rc * 128:(rc + 1) * 128], CKf[:, rc, t4 * 128:(t4 + 1) * 128], ident[:, :], [("rCKf",), "ident"], [pk(2)])
                        P.tr(ps[2][:, 256:320], KRf[0:64, t4 * 128:(t4 + 1) * 128], ident[0:64, 0:64], [("rKRf",), "ident"], [pk(2)])
                        P.act(cst[:, 0, :], ps[2][:, 0:256], AF.Copy, [pk(2)], [("rcst", 0)])
                        P.act(cst[:, 1, 0:64], ps[2][:, 256:320], AF.Copy, [pk(2)], [("rcst", 1)])
                        P.dma(SP, o_ckv[t4 * 128:(t4 + 1) * 128, :], cst[:, 0, :], reads=[("rcst", 0)])
                        P.dma(SP, o_kr[t4 * 128:(t4 + 1) * 128, :], cst[:, 1, 0:64], reads=[("rcst", 1)])
            P.barrier()
            for par, hb in enumerate((hb0, hb1)):
                P.memset(hb[3][64:128, :], 0.0, [("rQRpad", par)])
            jobs = [
                (0, 512, 0, 512, [0], False, [("rCK", 0)], [("rKR", 0)], 256),
                (512, 1024, 512, 1280, [1, 2], True, [("rCK", i) for i in (1, 2, 3, 4)], [("rKR", i) for i in (1, 2, 3, 4)], None),
            ]
            heads = [(job, h) for job in jobs for h in range(8)]
            cnt = {"pt": 0, "acc": 0}

            def kvq(i):
                (q0, nq, k0, nk, tts, rope, ckk, krk, blk), h = heads[i]
                par = i % 2
                KTh, Vh, QTh, QRh, RTb = (hb0, hb1)[par]
                nsc = nk // 128
                qtiles = [(q, min(512, nq - q)) for q in range(0, nq, 512)]
                qlk = [("rQLn", t) for t in tts]
                for ks in range(0, nk, 512):
                    kn = min(512, nk - ks)
                    b = bank([0, 1])
                    for rc in range(2):
                        P.mm(ps[b][:, :kn], Wuk[:, rc, h * 128:(h + 1) * 128], CK[:, rc, k0 + ks:k0 + ks + kn], rc == 0, rc == 1, [("rWuk",)] + ckk, [pk(b)])
                    P.act(KTh[:, ks:ks + kn], ps[b][:, :kn], AF.Copy, [pk(b)], [("rKTh", par)])
                for s4 in range(0, nsc, 4):
                    ns = min(4, nsc - s4)
                    b = bank([0, 1])
                    for si in range(ns):
                        sc = s4 + si
                        for rc in range(2):
                            P.mm(ps[b][:, si * 128:(si + 1) * 128], CK[:, rc, k0 + sc * 128:k0 + (sc + 1) * 128], Wuv[:, rc, h * 128:(h + 1) * 128], rc == 0, rc == 1, [("rWuv",)] + ckk, [pk(b)])
                    P.cp(Vh[:, s4:s4 + ns, :], ps[b][:, :ns * 128].rearrange("p (a b) -> p a b", a=ns), [pk(b)], [("rVh", par)])
                for (q, qn) in qtiles:
                    b = bank([0, 1])
                    for rc in range(3):
                        P.mm(ps[b][:, :qn], Wuq[:, rc, h * 192:h * 192 + 128], QLn[:, rc, q0 + q:q0 + q + qn], rc == 0, rc == 2, [("rWuq",)] + qlk, [pk(b)])
                    P.act(QTh[:, q:q + qn], ps[b][:, :qn], AF.Copy, [pk(b)], [("rQTh", par)])
                    b = bank([0, 1])
                    for rc in range(3):
                        P.mm(ps[b][0:64, :qn], Wuq[:, rc, h * 192 + 128:h * 192 + 192], QLn[:, rc, q0 + q:q0 + q + qn], rc == 0, rc == 2, [("rWuq",)] + qlk, [pk(b)])
                    if not rope:
                        P.act(QRh[0:64, q:q + qn], ps[b][0:64, :qn], AF.Copy, [pk(b)], [("rQRh", par)])
                    else:
                        b2 = bank([0, 1])
                        for rc in range(3):
                            P.mm(ps[b2][0:64, :qn], Wuq[:, rc, 1536 + h * 64:1536 + (h + 1) * 64], QLn[:, rc, q0 + q:q0 + q + qn], rc == 0, rc == 2, [("rWuq",)] + qlk, [pk(b2)])
                        P.tt(RTb[0:64, 0, :qn], ps[b][0:64, :qn], ROPE[0:64, 0, q:q + qn], ALU.mult, [pk(b), ("rROPE",)], [("rRT", par, 0)])
                        P.tt(RTb[0:64, 1, :qn], ps[b2][0:64, :qn], ROPE[0:64, 1, q:q + qn], ALU.mult, [pk(b2), ("rROPE",)], [("rRT", par, 1)])
                        P.tt(QRh[0:64, q:q + qn], RTb[0:64, 0, :qn], RTb[0:64, 1, :qn], ALU.add, [("rRT", par, 0), ("rRT", par, 1)], [("rQRh", par)])

            def att(i):
                (q0, nq, k0, nk, tts, rope, ckk, krk, blk), h = heads[i]
                par = i % 2
                KTh, Vh, QTh, QRh, RTb = (hb0, hb1)[par]
                nsc = nk // 128
                if blk is None:
                    qtiles = [(q, min(512, nq - q)) for q in range(0, nq, 512)]
                    seq = [(q, qn, sc, sc == 0, sc == nsc - 1, sc == nsc - 1, (q, qn)) for (q, qn) in qtiles for sc in range(nsc)]
                else:
                    per = blk // 128
                    seq = []
                    for sc in range(nsc):
                        q = (sc // per) * blk
                        seq.append((q, blk, sc, sc % per == 0, sc % per == per - 1, sc == nsc - 1, (0, nq)))

                def score(item):
                    q, qn, sc = item[0], item[1], item[2]
                    b = bank([2, 3])
                    P.mm(ps[b][:, :qn], KTh[:, sc * 128:(sc + 1) * 128], QTh[:, q:q + qn], True, False, [("rKTh", par), ("rQTh", par)], [pk(b)])
                    P.mm(ps[b][:, :qn], KR[:, k0 + sc * 128:k0 + (sc + 1) * 128], QRh[:, q:q + qn], False, True, krk + [("rQRh", par), ("rQRpad", par), ("rKRpad",)], [pk(b)])
                    r = cnt["pt"] % 4
                    cnt["pt"] += 1
                    P.act(PT[:, r, :qn], ps[b][:, :qn], AF.Exp, [pk(b)], [("rPT", r)], scale=MLA_SCALE)
                    return r

                rr = score(seq[0])
                newacc = True
                for idx, (q, qn, sc, first, last, fin, (fq, fqn)) in enumerate(seq):
                    r = rr
                    if idx + 1 < len(seq):
                        rr = score(seq[idx + 1])
                    if newacc:
                        ai = cnt["acc"] % 2
                        cnt["acc"] += 1
                        bo, bd = (4, 5) if ai == 0 else (6, 7)
                        newacc = False
                    cq = q - fq
                    P.mm(ps[bo][:, cq:cq + qn], Vh[:, sc, :], PT[:, r, :qn], first, last, [("rVh", par), ("rPT", r)], [pk(bo)])
                    P.mm(ps[bd][:, cq:cq + qn], ones_b[:, :], PT[:, r, :qn], first, last, ["ones", ("rPT", r)], [pk(bd)])
                    if fin:
                        P.act(rden[:, ai, :fqn], ps[bd][:, :fqn], AF.Ln, [pk(bd)], [("rrden", ai)])
                        P.act(rden[:, ai, :fqn], rden[:, ai, :fqn], AF.Exp, [("rrden", ai)], [("rrden", ai)], scale=-1.0)
                        wk = [("H", h, (q0 + fq) // 512)] + [("Hq", h, q0 + fq + o) for o in range(0, fqn, 256 if blk else 512)]
                        P.tt(H[:, h, q0 + fq:q0 + fq + fqn], ps[bo][:, :fqn], rden[:, ai, :fqn], ALU.mult, [pk(bo), ("rrden", ai)], wk)
                        newacc = True

            kvq(0)
            for i in range(len(heads)):
                if i + 1 < len(heads):
                    kvq(i + 1)
                att(i)
            wo = []
            for half in range(2):
                src = w_mla_o[half * 512:(half + 1) * 512, :].rearrange(kp, p=128)
                wo.append(ring_load([(lambda s_: s_.rearrange("p (k n) -> p k n", k=4), src)]))
            P.barrier()
            for tt in range(3):
                for c in range(8):
                    t0, tn = TT[tt]
                    b = bank([0, 1, 2, 3])
                    for h in range(8):
                        slot, key = wo[h // 4]
                        w3 = slot.rearrange("p (k n) -> p k n", k=4)
                        hq = [("Hq", h, 0), ("Hq", h, 256)] if tt == 0 else [("Hq", h, t0)]
                        P.mm(ps[b][:, :], w3[:, h % 4, c * 128:(c + 1) * 128], H[:, h, t0:t0 + 512], h == 0, h == 7, [key, ("H", h, tt)] + hq, [pk(b)])
                    resid(b, c, tt, L, 2)
                    if tile_hook is not None:
                        tile_hook(tt, c)

        def lru_mixer(L, tile_hook=None):
            P.barrier()
            A = Alloc()
            Wg = A(BF16, 8, 4, 128)
            def load_wg():
                first = True
                for d in range(2):
                    wload(Wg[:, :, d * 2 + 0, :], w_lru_a[d].rearrange("n c m -> c n m"), ("rWg",), join=not first)
                    first = False
                    wload(Wg[:, :, d * 2 + 1, :], w_lru_i[d].rearrange("n c m -> c n m"), ("rWg",), join=True)
            M = view(66 * 1024, BF16, 8, NT)
            UP = A(F32, NTP)
            UC = A(F32, NT)
            Y2s = [A(BF16, NT), A(BF16, NT), A(BF16, NT)]
            Abd = [A(F32, NT), A(F32, NT)]
            IGd = [A(F32, NT), A(F32, NT)]
            Tbd = [A(F32, NT), A(F32, NT)]
            assert A.off <= 66 * 1024, A.off
            A.off = 90 * 1024
            UCb = A(BF16, NT)
            YT = A(F32, 1, 512)
            SPv = A(F32, 2, 16)
            HB = A(F32, 2, 16)
            HS = A(F32, 32)
            assert A.off <= RBYTES, A.off
            P.act(SPv[:, 0, :], V("lam"), AF.Exp, ["vecs"], [("rSP",)], scale=-1.0)
            P.act(SPv[:, 0, :], SPv[:, 0, :], AF.Ln, [("rSP",), "vecs"], [("rSP",)], bias=V("one"), scale=1.0)
            P.ts(SPv[:, 1, :], SPv[:, 0, :], -8.0, None, ALU.mult, None, [("rSP",)], [("rSP2",)])
            P.ts(SPv[:, 0, :], SPv[:, 0, :], -4.0, None, ALU.mult, None, [("rSP",), ("rSP2",)], [("rSP",)])
            P.ts(HB[:, 0, :], vecs[:, VOFF["b_a"][0]:VOFF["b_a"][0] + 16], 0.5, None, ALU.mult, None, ["vecs"], [("rHB",)])
            P.ts(HB[:, 1, :], vecs[:, VOFF["b_i"][0]:VOFF["b_i"][0] + 16], 0.5, None, ALU.mult, None, ["vecs"], [("rHB",)])
            P.memset(UP, 0.0, [("rUP", 0), ("rUP", 1)])
            GRP = [0, 1, 1]
            TSEQ = [(0, 512), (512, 1024)]
            GT = [[0], [1, 2]]
            def FEa(n):
                par = n % 3
                Y2 = Y2s[par]
                items = []
                for uy in range(2):
                    src = w_lru_in[:, uy * D + n * 128: uy * D + (n + 1) * 128].rearrange("(k p) m -> p k m", p=128)
                    items.append((lambda s_, uy=uy: s_[:, 0:2048].rearrange("p (k a m) -> p k a m", k=8, a=2)[:, :, uy, :], src))
                slot, key = ring_load(items)
                w4 = slot[:, 0:2048].rearrange("p (k a m) -> p k a m", k=8, a=2)
                for uy in range(2):
                    for tt in range(3):
                        t0, tn = TT[tt]
                        b = bank([0, 1, 2, 3])
                        for k in range(8):
                            P.mm(ps[b][:, :], w4[:, k, uy, :], H[:, k, t0:t0 + 512], k == 0, k == 7, [key, ("H", k, tt)], [pk(b)])
                        if uy == 0:
                            if tt == 0:
                                P.act(pair(UP, 0, 288, PADW, 256), ps[b][:, :].rearrange("p (s t) -> p s t", s=2), AF.Copy, [pk(b)], [("rUP", 0)])
                            else:
                                o = POFF[2] + (tt - 1) * 512
                                P.act(UP[:, o:o + 512], ps[b][:, :], AF.Copy, [pk(b)], [("rUP", 1)])
                        else:
                            P.act(Y2[:, t0:t0 + 512], ps[b][:, :], AF.Gelu_apprx_tanh, [pk(b)], [("rY", par, tt)])

            def FEb(n):
                segs = [(lambda sh: pair(UP, 0, 288, PADW + sh, 256), pair(UC, 0, 256, 0, 256)),
                        (lambda sh: UP[:, POFF[2] + sh:POFF[2] + sh + 1024], UC[:, 512:1536])]
                for g, (pv, uv) in enumerate(segs):
                    P.ts(uv, pv(0), V("conv_w", 1)[:, n:n + 1], V("conv_b")[:, n:n + 1], ALU.mult, ALU.add, [("rUP", g), "vecs"], [("rUC", g)])
                    for kk in (0, 2, 3):
                        P.stt(uv, pv(kk - 1), V("conv_w", kk)[:, n:n + 1], uv, ALU.mult, ALU.add, [("rUP", g), ("rUC", g), "vecs"], [("rUC", g)])
                    a0, an = TSEQ[g]
                    P.act(UCb[:, a0:a0 + an], UC[:, a0:a0 + an], AF.Copy, [("rUC", g)], [("rUCb", g)])

            def GS(n):
                for d in range(2):
                    dn = d * 8 + n
                    for gate in range(2):
                        for tt in range(3):
                            t0, tn = TT[tt]
                            b = bank([4, 5, 6, 7])
                            P.mm(ps[b][:, :], Wg[:, n, d * 2 + gate, :], UCb[:, t0:t0 + 512], True, True, [("rWg",), ("rUCb", GRP[tt])], [pk(b)])
                            dst, dk = (Abd[d], "rAb") if gate == 0 else (IGd[d], "rIG")
                            P.act(dst[:, t0:t0 + 512], ps[b][:, :], AF.Tanh, [pk(b), ("rHB",)], [(dk, d, tt)], bias=HB[:, gate, dn:dn + 1], scale=0.5)
                for d in range(2):
                    ik = [("rIG", d, t) for t in range(3)]
                    P.stt(IGd[d][:, :], IGd[d][:, :], 1.0, UC[:, :], ALU.add, ALU.mult, ik + [("rUC", 0), ("rUC", 1)], ik)

            def BE_act(n):
                for d in range(2):
                    dn = d * 8 + n
                    ak = [("rAb", d, t) for t in range(3)]
                    tk = [("rTb", d, t) for t in range(3)]
                    P.act(Tbd[d][:, :], Abd[d][:, :], AF.Exp, ak + [("rSP2",)], tk, bias=SPv[:, 1, dn:dn + 1], scale=SPv[:, 1, dn:dn + 1])
                    P.act(Abd[d][:, :], Abd[d][:, :], AF.Exp, ak + [("rSP",)], ak, bias=SPv[:, 0, dn:dn + 1], scale=SPv[:, 0, dn:dn + 1])
                for d in range(2):
                    tk = [("rTb", d, t) for t in range(3)]
                    P.act(Tbd[d][:, :], Tbd[d][:, :], AF.Sqrt, tk + ["vecs"], tk, bias=V("quarter"), scale=-0.25)

            def BE_dve(n):
                for d in range(2):
                    tka = [("rTb", d, t) for t in range(3)]
                    ika = [("rIG", d, t) for t in range(3)]
                    P.tt(Tbd[d][:, :], Tbd[d][:, :], IGd[d][:, :], ALU.mult, tka + ika, tka)
                    for g, (a0, an) in enumerate(TSEQ):
                        ak = [("rAb", d, t) for t in GT[g]]
                        tk = [("rTb", d, t) for t in GT[g]]
                        for si, (so, sl) in enumerate(SEQS):
                            if (0 if si < 2 else 1) != g:
                                continue
                            init = 0.0 if si < 2 else V("state", d)[:, n:n + 1]
                            if d == 0:
                                P.scan(Tbd[d][:, so:so + sl], Abd[d][:, so:so + sl], Tbd[d][:, so:so + sl], init, ak + tk + ["vecs"], tk)
                            else:
                                P.scan(Tbd[d][:, so:so + sl][:, ::-1], Abd[d][:, so:so + sl][:, ::-1], Tbd[d][:, so:so + sl][:, ::-1], init, ak + tk + ["vecs"], tk)

            def BE_fin(n):
                par = n % 3
                Y2 = Y2s[par]
                for d in range(2):
                    tk = [("rTb", d, 0)]
                    c0 = d * 8 + n
                    src_ = Tbd[d][:, 255:512:256] if d == 0 else Tbd[d][:, 0:257:256]
                    P.act(HS[:, c0:c0 + 17:16], src_, AF.Copy, tk, [("rHS",)])
                k0_ = [("rTb", 0, t) for t in range(3)]
                k1_ = [("rTb", 1, t) for t in range(3)]
                P.tt(Tbd[0][:, :], Tbd[0][:, :], Tbd[1][:, :], ALU.add, k0_ + k1_, k0_)
                yk = [("rY", par, t) for t in range(3)]
                P.tt(M[:, n, :], Tbd[0][:, :], Y2[:, :], ALU.mult, k0_ + yk, [("M", n, 0), ("M", n, 1)])

            FEa(0)
            FEb(0)
            FEa(1)
            load_wg()
            GS(0)
            for n in range(8):
                BE_act(n)
                if n + 1 < 8:
                    FEb(n + 1)
                BE_dve(n)
                if n + 2 < 8:
                    FEa(n + 2)
                BE_fin(n)
                if n + 1 < 8:
                    GS(n + 1)
            P.tr(ps[0][0:32, 0:128], HS[:, 0:32], ident[:, :], [("rHS",), "ident"], [pk(0)])
            P.act(UC[0:32, 0:128], ps[0][0:32, 0:128], AF.Copy, [pk(0)], [("rUC", 0)])
            P.dma(SP, o_lru, UC[0:32, 0:128], reads=[("rUC", 0)])
            wo = []
            for half in range(2):
                src = w_lru_out[half * 512:(half + 1) * 512, :].rearrange("(k p) n -> p k n", p=128)
                wo.append(ring_load([(lambda s_: s_.rearrange("p (k n) -> p k n", k=4), src)]))
            P.barrier()
            for tt in range(3):
                for c in range(8):
                    t0, tn = TT[tt]
                    b = bank([0, 1, 2, 3])
                    for k in range(8):
                        slot, key = wo[k // 4]
                        w3 = slot.rearrange("p (k n) -> p k n", k=4)
                        P.mm(ps[b][:, :], w3[:, k % 4, c * 128:(c + 1) * 128], M[:, k, t0:t0 + 512], k == 0, k == 7, [key, ("M", k, GRP[tt])], [pk(b)])
                    resid(b, c, tt, L, 2)
                    if tile_hook is not None:
                        tile_hook(tt, c)

        for it in range(4):
            mod_item(0, it)
        mod_finish(0, 7, rng=(0, 16))
        derive(0, "a")

        def mod0_gate():
            mod_item(0, 4)
            mod_item(0, 5)
            mod_finish(0, 7, rng=(16, 24))
            derive(0, "g")
        for L in range(NLAYERS):
            kind = L % 3

            def tile_hook(tt, c, L=L):
                if tt == 1:
                    ln_stats_chunk(0, c)
                    if c == 7:
                        ln_stats_fin(0)
                elif tt == 2:
                    ln_stats_chunk(1, c)
                    ln_apply_chunk(0, L, 0, 1, c)
                    if c == 7:
                        ln_stats_fin(1)
            if kind == 0:
                pool_mixer(L, mid=(mod0_gate if L == 0 else None))
                P.barrier()
                ln_stats(0)
                ln_stats(1)
                if L == 0:
                    for it in range(6, 12):
                        mod_item(0, it, 5)
                    mod_finish(0, 5, 1)
                    derive(0, 1)
                ln_apply(0, L, 0, 1)
            elif kind == 1:
                mla_mixer(L, tile_hook)
            else:
                lru_mixer(L, tile_hook)

            def pre_hook(tt, ii, L=L):
                if (tt, ii) == (0, 2):
                    ln_apply(1, L, 0, 1)
                elif (tt, ii) == (1, 0):
                    ln_stats(2)
                elif (tt, ii) == (1, 2):
                    ln_apply(2, L, 0, 1)
            if L == 0:
                mla_prefetch()
            ffn(L, pre_hook, 1 if L == 0 else None, tail_mod=(L + 2 if L < 2 else None),
                out_hook=((lambda tt: emit_out(yout, (tt,))) if (L == 3 and not dbg) else None))
            if dbg:
                P.barrier()
                emit_out(dbg_out[L])
                P.barrier()
        if dbg:
            emit_out(yout)
        P.emit()
    return nc


NLAYERS = 4
_CACHE = {}


def _host_consts():
    ident = np.eye(128, dtype=np.float32)
    pband = np.zeros((4, 128, 12, 144), np.float32)
    for g, w in enumerate((2, 4, 8, 16)):
        for i in range(12):
            si_ = 0 if i < 2 else (1 if i < 4 else 2)
            so, T = SEQS[si_]
            ls = i * 128 - so
            tp = ls - 8 + np.arange(144)
            valid = (tp >= 0) & (tp < T)
            lo = np.clip(tp - w // 2, 0, T)
            hi = np.clip(tp + w - w // 2, 0, T)
            cnt = np.maximum(hi - lo, 1).astype(np.float32)
            tr = (ls + np.arange(128))[:, None]
            inw = (tr >= lo[None, :]) & (tr < hi[None, :])
            blk = inw.astype(np.float32) / cnt[None, :] - (tr == tp[None, :]).astype(np.float32)
            pband[g, :, i, :] = np.where(valid[None, :], blk, 0.0).astype(np.float32)
    t = np.arange(1024)
    rows = (t // 64).astype(np.float32)
    cols = (t % 64).astype(np.float32)
    inv = (np.float32(10000.0) ** (-np.arange(16, dtype=np.float32) / np.float32(16))).astype(np.float32)
    ang = np.stack([rows[:, None] * inv, cols[:, None] * inv], axis=1).astype(np.float32)
    cos, sin = np.cos(ang).astype(np.float32), np.sin(ang).astype(np.float32)
    rope = np.zeros((64, 2, 1024), np.float32)
    perm = np.zeros(64, np.int64)
    for a in range(2):
        for j in range(2):
            for f in range(16):
                i = a * 32 + j * 16 + f
                perm[i] = a * 32 + (1 - j) * 16 + f
                rope[i, 0] = cos[:, a, f]
                rope[i, 1] = -sin[:, a, f] if j == 0 else sin[:, a, f]
    return ident, pband, rope, perm


def _fm(a):
    a = np.asarray(a, np.float32)
    lead = a.shape[:-1]
    C = a.shape[-1] // 128
    a = a.reshape(lead + (C, 128))
    a = np.moveaxis(a, -1, 0)
    return np.ascontiguousarray(a.reshape(128, -1))


def kernel(x_prompt, x_sample, cache_mla_ckv, cache_mla_krope, state_lru, c, c_ctx,
           w_ada, b_ada, ln_g, ln_b, w_ffn_in, w_ffn_out, w_pool, pool_scale,
           w_dq, g_q, w_uq, w_dkv, g_kv, w_uk, w_uv, w_mla_o,
           w_lru_in, lru_conv_w, lru_conv_b, w_lru_a, b_lru_a, w_lru_i, b_lru_i,
           lru_lambda, w_lru_out, _dbg=False):
    f = lambda a: np.ascontiguousarray(np.asarray(a, dtype=np.float32))
    ident, pband, rope, perm = _host_consts()
    if ("nc", _dbg) not in _CACHE:
        _CACHE[("nc", _dbg)] = build_program(_dbg)
    nc = _CACHE[("nc", _dbg)]
    x_prompt, x_sample = f(x_prompt), f(x_sample)
    w_uq0 = f(w_uq)[0]
    w_dkv0 = f(w_dkv)[0]
    shared = {
        "ident": ident, "pband": pband, "rope": rope,
        "w_ada": f(w_ada), "w_ffn_in": f(w_ffn_in), "w_ffn_out": f(w_ffn_out), "w_pool": f(w_pool),
        "w_dq": f(w_dq)[0], "w_uq": np.ascontiguousarray(w_uq0.reshape(384, 1536)),
        "w_uq_rp": np.ascontiguousarray(w_uq0[:, :, 128:192][:, :, perm].reshape(384, 512)),
        "w_dkv": w_dkv0, "w_dkv_rp": np.ascontiguousarray(w_dkv0[:, 256:320][:, perm]),
        "w_uk": np.ascontiguousarray(f(w_uk)[0].reshape(256, 1024)), "w_uv": np.ascontiguousarray(f(w_uv)[0].reshape(256, 1024)),
        "w_mla_o": f(w_mla_o)[0], "w_lru_in": f(w_lru_in)[0], "w_lru_a": f(w_lru_a)[0], "w_lru_i": f(w_lru_i)[0],
        "w_lru_out": f(w_lru_out)[0],
    }
    one = np.ones((128, 1), np.float32)
    common = [_fm(f(b_ada)), _fm(f(ln_g)), _fm(f(ln_b)), _fm(f(pool_scale)), _fm(f(g_q)[0]), _fm(f(g_kv)[0]),
              _fm(f(lru_conv_w)[0]), _fm(f(lru_conv_b)[0]), _fm(f(b_lru_a)[0]), _fm(f(b_lru_i)[0]), _fm(f(lru_lambda)[0])]
    tail = [one * np.float32(EPS), one * np.float32(EPS_LN), one, one * np.float32(0.25)]
    in_maps = []
    for i in range(8):
        cond = np.stack([f(c_ctx), f(c)[i]], axis=0)
        condT = np.ascontiguousarray(cond.reshape(2, 8, 128).transpose(2, 1, 0).reshape(128, 16))
        vec = np.concatenate(common + [_fm(f(state_lru)[i, 0])] + tail, axis=1).astype(np.float32)
        assert vec.shape == (128, NV), vec.shape
        m = dict(shared)
        m.update({
            "xin": np.ascontiguousarray(np.concatenate([x_prompt[2 * i], x_prompt[2 * i + 1], x_sample[i]], axis=0)),
            "condT": condT, "vecs": np.ascontiguousarray(vec),
            "cache_ckv": f(cache_mla_ckv)[i, 0], "cache_kr": f(cache_mla_krope)[i, 0],
        })
        in_maps.append(m)
    res = run_bass_kernel_spmd(nc, in_maps, core_ids=list(range(8)))
    y_prompt = np.zeros((16, 256, D), np.float32)
    y_sample = np.zeros((8, 1024, D), np.float32)
    n_ckv = np.zeros((16, 1, 256, 256), np.float32)
    n_kr = np.zeros((16, 1, 256, 64), np.float32)
    n_lru = np.zeros((16, 1, 2, D), np.float32)
    for i in range(8):
        r = res.results[i]
        y_prompt[2 * i] = r["yout"][0:256]
        y_prompt[2 * i + 1] = r["yout"][256:512]
        y_sample[i] = r["yout"][512:]
        for s_ in range(2):
            n_ckv[2 * i + s_, 0] = r["o_ckv"][s_ * 256:(s_ + 1) * 256]
            n_kr[2 * i + s_, 0] = r["o_kr"][s_ * 256:(s_ + 1) * 256]
            n_lru[2 * i + s_, 0] = r["o_lru"][s_ * 16:(s_ + 1) * 16].reshape(2, D)
    if _dbg:
        kernel.dbg = [[res.results[i]["dbg%d" % k] for k in range(4)] for i in range(8)]
    return (y_prompt, y_sample, n_ckv, n_kr, n_lru)
```

```python
import contextlib
import math
import numpy as np
import concourse.bass as bass
import concourse.mybir as mybir
from concourse.bass_utils import run_bass_kernel_spmd

F32 = mybir.dt.float32
BF16 = mybir.dt.bfloat16
ALU = mybir.AluOpType
AF = mybir.ActivationFunctionType

PE, ACT, DVE, POOL, SP = "tensor", "scalar", "vector", "gpsimd", "sync"
ENGS = [PE, ACT, DVE, POOL, SP]
NDS = 24

D = 1024
NT = 1536
DFF = 2816
NFC = 22
ALPHA = 8.0 ** 0.25
EPS_LN = 1e-6 / (ALPHA * ALPHA)
EPS = 1e-6
MLA_SCALE = 192.0 ** -0.5
PADW = 16
SEQS = [(0, 256), (256, 256), (512, 1024)]
POFF = [PADW, 288 + PADW, 576 + PADW]
NTP = 1632
TT = [(0, 512), (512, 512), (1024, 512)]
TCOND = [0, 1, 1]


class Op:
    __slots__ = ("eng", "fn", "deps", "is_dma", "dma_sem", "dma_val", "dma_prev", "target", "count", "idx")


class Prog:
    def __init__(self, nc):
        self.nc = nc
        self.ops = {e: [] for e in ENGS}
        self.last_w = {}
        self.readers = {}
        self.ndma = 0
        self.bar_deps = []
        self.bar_pending = set()
        self.r_last = {}
        self.r_dmas = []

    def barrier(self):
        self.bar_deps = list(self.r_last.values()) + list(self.r_dmas)
        self.bar_pending = set(ENGS)
        self.r_dmas = []

    def op(self, eng, name, kwargs, reads=(), writes=(), dma=False, join=False):
        o = Op()
        meth, kw = name, dict(kwargs)
        o.fn = lambda e: getattr(e, meth)(**kw)
        o.eng, o.is_dma, o.target, o.count = eng, dma, False, 0
        o.idx = len(self.ops[eng])
        deps = set()
        isr = False
        for k in reads:
            if isinstance(k, tuple) and k[0][0] == "r":
                isr = True
            for lw in self.last_w.get(k, ()):
                if lw.is_dma or lw.eng != eng or eng != PE:
                    deps.add(lw)
        joined = {}
        for k in writes:
            if isinstance(k, tuple) and k[0][0] == "r":
                isr = True
            lws = self.last_w.get(k, [])
            jn = dma and join and len(lws) > 0 and all(w.is_dma for w in lws) and not self.readers.get(k)
            joined[k] = jn
            if not jn:
                for lw in lws:
                    if lw.is_dma or lw.eng != eng or dma:
                        deps.add(lw)
            for r in self.readers.get(k, {}).values():
                if r.is_dma or r.eng != eng or dma:
                    deps.add(r)
        if isr:
            if eng in self.bar_pending:
                self.bar_pending.discard(eng)
                for d in self.bar_deps:
                    if d.is_dma or d.eng != eng or dma:
                        deps.add(d)
            if dma:
                self.r_dmas.append(o)
            else:
                self.r_last[eng] = o
        deps.discard(o)
        o.deps = deps
        if dma:
            j = self.ndma
            self.ndma += 1
            o.dma_sem = j % NDS
            o.dma_val = 16 * (j // NDS + 1)
            o.dma_prev = 16 * (j // NDS)
        for k in writes:
            self.last_w[k] = (self.last_w.get(k, []) + [o]) if joined[k] else [o]
            self.readers[k] = {}
        for k in reads:
            self.readers.setdefault(k, {})[eng if not dma else ("dma", o.idx, eng)] = o
        self.ops[eng].append(o)
        return o

    def mm(self, out, lhsT, rhs, start, stop, reads, writes):
        return self.op(PE, "matmul", dict(out=out, lhsT=lhsT, rhs=rhs, start=start, stop=stop), reads, writes)

    def tr(self, out, in_, identity, reads, writes):
        return self.op(PE, "transpose", dict(out=out, in_=in_, identity=identity), reads, writes)

    def act(self, out, in_, func, reads, writes, bias=None, scale=None):
        kw = dict(out=out, in_=in_, func=func)
        if bias is not None:
            kw["bias"] = bias
        if scale is not None:
            kw["scale"] = scale
        return self.op(ACT, "activation", kw, reads, writes)

    def tt(self, out, in0, in1, op, reads, writes, eng=DVE):
        return self.op(eng, "tensor_tensor", dict(out=out, in0=in0, in1=in1, op=op), reads, writes)

    def ts(self, out, in0, s1, s2, op0, op1, reads, writes, eng=DVE):
        kw = dict(out=out, in0=in0, scalar1=s1, scalar2=s2, op0=op0)
        if op1 is not None:
            kw["op1"] = op1
        return self.op(eng, "tensor_scalar", kw, reads, writes)

    def stt(self, out, in0, scalar, in1, op0, op1, reads, writes):
        return self.op(DVE, "scalar_tensor_tensor", dict(out=out, in0=in0, scalar=scalar, in1=in1, op0=op0, op1=op1), reads, writes)

    def cp(self, out, in_, reads, writes, eng=DVE):
        return self.op(eng, "tensor_copy", dict(out=out, in_=in_), reads, writes)

    def recip(self, out, in_, reads, writes):
        return self.op(DVE, "reciprocal", dict(out=out, in_=in_), reads, writes)

    def memset(self, ap, val, writes, eng=DVE):
        return self.op(eng, "memset", dict(ap=ap, constant=val), (), writes)

    def scan(self, out, d0, d1, initial, reads, writes):
        return self.op(DVE, "tensor_tensor_scan", dict(out=out, data0=d0, data1=d1, initial=initial, op0=ALU.mult, op1=ALU.add), reads, writes)

    def dma(self, eng, out, in_, reads=(), writes=(), join=False):
        return self.op(eng, "dma_start", dict(out=out, in_=in_), reads, writes, dma=True, join=join)

    def emit(self):
        nc = self.nc
        for e in ENGS:
            for o in self.ops[e]:
                for d in o.deps:
                    if not d.is_dma:
                        d.target = True
        for e in ENGS:
            c = 0
            for o in self.ops[e]:
                if o.target and not o.is_dma:
                    c += 1
                    o.count = c
        with contextlib.ExitStack() as st:
            esem = {e: st.enter_context(nc.semaphore("es_" + e)) for e in ENGS}
            dsem = [st.enter_context(nc.semaphore("ds%d" % i)) for i in range(NDS)]
            block = st.enter_context(nc.Block())
            for e in ENGS:
                ops = self.ops[e]
                if not ops:
                    continue

                def body(eng, ops=ops, e=e):
                    known = {}

                    def wait(key, sem, val):
                        if known.get(key, 0) >= val:
                            return
                        known[key] = val
                        eng.wait_ge(sem, val)

                    for o in ops:
                        for d in sorted(o.deps, key=lambda d: (d.eng, d.idx)):
                            if d.is_dma:
                                wait(("d", d.dma_sem), dsem[d.dma_sem], d.dma_val)
                            else:
                                wait(("e", d.eng), esem[d.eng], d.count)
                        if o.is_dma and o.dma_prev > 0:
                            wait(("d", o.dma_sem), dsem[o.dma_sem], o.dma_prev)
                        ins = o.fn(eng)
                        if o.is_dma:
                            ins.then_inc(dsem[o.dma_sem], 16)
                        elif o.target:
                            ins.then_inc(esem[e], 1)
                    for o in ops:
                        if o.is_dma:
                            wait(("d", o.dma_sem), dsem[o.dma_sem], o.dma_val)

                getattr(block, e)(body)


VSPEC = [("b_ada", (4, 48)), ("ln_g", (4, 2, 8)), ("ln_b", (4, 2, 8)), ("pool_scale", (2, 8)), ("g_q", (3,)), ("g_kv", (2,)),
         ("conv_w", (4, 8)), ("conv_b", (8,)), ("b_a", (2, 8)), ("b_i", (2, 8)), ("lam", (16,)), ("state", (2, 8)),
         ("eps", (1,)), ("epsln", (1,)), ("one", (1,)), ("quarter", (1,))]
VOFF = {}
_o = 0
for _n, _s in VSPEC:
    VOFF[_n] = (_o, _s)
    _o += int(np.prod(_s))
NV = _o


def build_program(dbg=False):
    nc = bass.Bass("TRN2", target_bir_lowering=False)

    def din(name, shape):
        return nc.dram_tensor(name, list(shape), F32, kind="ExternalInput").ap()

    def dout(name, shape):
        return nc.dram_tensor(name, list(shape), F32, kind="ExternalOutput").ap()

    xin = din("xin", [NT, D])
    condT = din("condT", [128, 16])
    vecs_d = din("vecs", [128, NV])
    ident_d = din("ident", [128, 128])
    pband_d = din("pband", [4, 128, 12, 144])
    rope_d = din("rope", [64, 2, 1024])
    cache_ckv = din("cache_ckv", [256, 256])
    cache_kr = din("cache_kr", [256, 64])
    w_ada = din("w_ada", [4, D, 6 * D])
    w_ffn_in = din("w_ffn_in", [4, D, 2 * DFF])
    w_ffn_out = din("w_ffn_out", [4, DFF, D])
    w_pool = din("w_pool", [2, 4, 256, 256])
    w_dq = din("w_dq", [D, 384])
    w_uq = din("w_uq", [384, 1536])
    w_uq_rp = din("w_uq_rp", [384, 512])
    w_dkv = din("w_dkv", [D, 320])
    w_dkv_rp = din("w_dkv_rp", [D, 64])
    w_uk = din("w_uk", [256, 1024])
    w_uv = din("w_uv", [256, 1024])
    w_mla_o = din("w_mla_o", [D, D])
    w_lru_in = din("w_lru_in", [D, 2 * D])
    w_lru_a = din("w_lru_a", [2, 8, 128, 128])
    w_lru_i = din("w_lru_i", [2, 8, 128, 128])
    w_lru_out = din("w_lru_out", [D, D])

    yout = dout("yout", [NT, D])
    o_ckv = dout("o_ckv", [512, 256])
    o_kr = dout("o_kr", [512, 64])
    o_lru = dout("o_lru", [32, 128])
    dbg_out = [dout("dbg%d" % i, [NT, D]) for i in range(4)] if dbg else None

    st = contextlib.ExitStack()
    with st:
        def sb(name, shape, dtp):
            return st.enter_context(nc.sbuf_tensor(name, list(shape), dtp))

        Xt = sb("X", [128, 8 * NT], F32)
        Ht = sb("H", [128, 8 * NT], BF16)
        X = Xt[:, :].rearrange("p (c t) -> p c t", c=8)
        H = Ht[:, :].rearrange("p (c t) -> p c t", c=8)
        NSLOT = 4
        RINGW = 4096
        ring_t = sb("ring", [128, NSLOT * RINGW], BF16)
        vecs = sb("vecs_sb", [128, NV], F32)
        ident = sb("ident_sb", [128, 128], F32)
        ones_b = sb("ones_b", [128, 128], BF16)
        scond = sb("scond", [128, 16], BF16)
        condf = sb("condf", [128, 16], F32)
        MOD = sb("MOD", [128, 4 * 2 * 48], F32)
        DER = sb("DER", [128, 4 * 2 * 7 * 8], F32)
        RBYTES = 96 * 1024
        Rt = sb("R", [128, RBYTES // 4], F32)
        ps = [st.enter_context(nc.psum_tensor("ps%d" % i, [128, 512], F32)) for i in range(8)]

        P = Prog(nc)
        XST_OFF = 66 * 1024

        def view(off, dtp, *dims):
            n = int(np.prod(dims))
            assert off % 4 == 0
            if dtp == F32:
                assert off + 4 * n <= RBYTES, (off, n)
                ap = Rt[:, off // 4: off // 4 + n]
            else:
                assert n % 2 == 0 and off + 2 * n <= RBYTES, (off, n)
                ap = Rt[:, off // 4: off // 4 + n // 2].bitcast(BF16)
            if len(dims) == 2:
                ap = ap.rearrange("p (a b) -> p a b", a=dims[0])
            elif len(dims) == 3:
                ap = ap.rearrange("p (a b c) -> p a b c", a=dims[0], b=dims[1])
            return ap

        class Alloc:
            def __init__(self, off=0):
                self.off = off

            def __call__(self, dtp, *dims):
                n = int(np.prod(dims)) * (4 if dtp == F32 else 2)
                o = self.off
                self.off += (n + 31) // 32 * 32
                return view(o, dtp, *dims)

        ring_n = [0]

        def ring_load(items):
            s_ = ring_n[0] % NSLOT
            ring_n[0] += 1
            slot = ring_t[:, s_ * RINGW:(s_ + 1) * RINGW]
            key = ("wring", s_)
            for i, (dst_fn, src) in enumerate(items):
                P.dma(POOL, dst_fn(slot), src, writes=[key], join=(i > 0))
            return slot, key

        def wload(dst, src, key, join=False):
            P.dma(POOL, dst, src, writes=[key], join=join)

        rot = {}

        def bank(pool):
            pool = tuple(pool)
            i = rot.get(pool, 0)
            rot[pool] = i + 1
            return pool[i % len(pool)]

        def pk(b):
            return ("ps", b)

        def V(name, *idx):
            o, shape = VOFF[name]
            i = 0
            for k, s_ in zip(idx, shape[:len(idx)]):
                i = i * s_ + k
            rest = int(np.prod(shape[len(idx):])) if len(idx) < len(shape) else 1
            return vecs[:, o + i * rest: o + (i + 1) * rest]

        def MODv(L, j, kind):
            o = (L * 2 + j) * 48 + kind * 8
            return MOD[:, o:o + 8]

        def DERv(L, j, kind):
            o = ((L * 2 + j) * 7 + kind) * 8
            return DER[:, o:o + 8]

        P.dma(SP, vecs[:, :], vecs_d, writes=["vecs"])
        P.dma(SP, ident[:, :], ident_d, writes=["ident"])
        P.dma(SP, condf[:, :], condT, writes=["condf"])
        P.memset(ones_b[:, :], 1.0, ["ones"])
        P.act(scond[:, :], condf[:, :], AF.Silu, ["condf"], ["scond"])
        scond3 = scond[:, :].rearrange("p (a b) -> p a b", a=8)

        stage_t = view(XST_OFF, F32, 2 * D)
        for ti in range(12):
            sbuf = stage_t[:, (ti % 2) * D:(ti % 2 + 1) * D]
            sk = ("rxstage", ti % 2)
            P.dma(SP, sbuf, xin[ti * 128:(ti + 1) * 128, :], writes=[sk])
            tt = ti // 4
            for half in range(2):
                b = bank([0, 1, 2, 3])
                for cc in range(4):
                    c = half * 4 + cc
                    P.tr(ps[b][:, cc * 128:(cc + 1) * 128], sbuf[:, c * 128:(c + 1) * 128], ident[:, :], [sk, "ident"], [pk(b)])
                dst = X[:, half * 4:half * 4 + 4, ti * 128:(ti + 1) * 128]
                src = ps[b][:, :].rearrange("p (a b) -> p a b", a=4)
                wk = [("X", half * 4 + cc, tt) for cc in range(4)]
                if half:
                    P.act(dst, src, AF.Copy, [pk(b)], wk)
                else:
                    P.cp(dst, src, [pk(b)], wk)

        def emit_out(dstd, tts=(0, 1, 2)):
            for ti in range(12):
                tt = ti // 4
                if tt not in tts:
                    continue
                sbuf = stage_t[:, (ti % 2) * D:(ti % 2 + 1) * D]
                for half in range(2):
                    b = bank([0, 1, 2, 3])
                    for cc in range(4):
                        c = half * 4 + cc
                        P.tr(ps[b][:, cc * 128:(cc + 1) * 128], X[:, c, ti * 128:(ti + 1) * 128], ident[:, :], [("X", c, tt), "ident"], [pk(b)])
                    sk = ("rxstage", ti % 2, half)
                    if half:
                        P.act(sbuf[:, half * 512:(half + 1) * 512], ps[b][:, :], AF.Copy, [pk(b)], [sk])
                    else:
                        P.cp(sbuf[:, half * 512:(half + 1) * 512], ps[b][:, :], [pk(b)], [sk])
                P.dma(SP, dstd[ti * 128:(ti + 1) * 128, :], sbuf, reads=[("rxstage", ti % 2, 0), ("rxstage", ti % 2, 1)])

        def mod_item(L, it, b=7):
            src = w_ada[L, :, it * 512:(it + 1) * 512].rearrange("(k p) n -> p k n", p=128)
            slot, key = ring_load([(lambda s_: s_.rearrange("p (k n) -> p k n", k=8), src)])
            s3 = slot.rearrange("p (k n) -> p k n", k=8)
            for oc in range(4):
                o = it * 4 + oc
                for k in range(8):
                    P.mm(ps[b][:, 2 * o:2 * o + 2], s3[:, k, oc * 128:(oc + 1) * 128], scond3[:, k, :], k == 0, k == 7, [key, "scond"], [pk(b)])

        def mod_finish(L, b=7, part=None, rng=None):
            lo, hi = (0, 48) if part is None else ((0, 24) if part == 0 else (24, 48))
            if rng is not None:
                lo, hi = rng
            for j in range(2):
                o = (L * 2 + j) * 48
                P.tt(MOD[:, o + lo:o + hi], ps[b][:, 2 * lo + j:2 * hi:2], V("b_ada", L)[:, lo:hi], ALU.add, [pk(b), "vecs"], [("MOD", L, j)])

        def modulation(L):
            for it in range(12):
                mod_item(L, it)
            mod_finish(L)

        def derive(L, part=None):
            for j in range(2):
                rk = [("MOD", L, j), "vecs"]
                wk = [("DER", L, j)]
                if part in (None, 0, "a"):
                    P.ts(DERv(L, j, 0), MODv(L, j, 1), 1.0, None, ALU.add, None, rk, wk)
                    P.cp(DERv(L, j, 1), MODv(L, j, 0), rk, wk)
                if part in (None, 0, "g"):
                    if L % 3 == 0:
                        P.stt(DERv(L, j, 2), MODv(L, j, 2), 1.0 / ALPHA, V("pool_scale", L // 3), ALU.mult, ALU.mult, rk, wk)
                    else:
                        P.ts(DERv(L, j, 2), MODv(L, j, 2), 1.0 / ALPHA, None, ALU.mult, None, rk, wk)
                if part in (None, 1):
                    P.ts(DERv(L, j, 3), MODv(L, j, 4), 1.0, None, ALU.add, None, rk, wk)
                    P.tt(DERv(L, j, 4), V("ln_b", L, 0), DERv(L, j, 3), ALU.mult, rk + wk, wk)
                    P.tt(DERv(L, j, 4), DERv(L, j, 4), MODv(L, j, 3), ALU.add, rk + wk, wk)
                    P.ts(DERv(L, j, 5), MODv(L, j, 5), 1.0 / ALPHA, None, ALU.mult, None, rk, wk)

        def derive_next(L):
            for j in range(2):
                rk = [("DER", L + 1, j), "vecs"]
                wk = [("DERN", L, j)]
                P.tt(DERv(L, j, 6), V("ln_b", L, 1), DERv(L + 1, j, 0), ALU.mult, rk, wk)
                P.tt(DERv(L, j, 6), DERv(L, j, 6), DERv(L + 1, j, 1), ALU.add, rk + wk, wk)

        LNOFF = 36 * 1024 + 3072

        def ln_bufs():
            A = Alloc(LNOFF)
            Zb = A(BF16, 3, 512)
            Sq = A(BF16, 3, 512)
            mean = A(F32, 3, 512)
            rstd = A(F32, 3, 512)
            T1 = A(F32, 2, 512)
            V1 = A(F32, 2, 512)
            assert A.off <= RBYTES
            return Zb, Sq, mean, rstd, T1, V1

        def ln_stats_chunk(tt, c):
            t0, tn = TT[tt]
            Zb, Sq, mean, rstd, T1, V1 = ln_bufs()
            bm, bs = 6, 7
            r = c % 3
            xs = X[:, c, t0:t0 + tn]
            P.act(Zb[:, r, :], xs, AF.Copy, [("X", c, tt)], [("rZb", r)])
            P.act(Sq[:, r, :], xs, AF.Square, [("X", c, tt)], [("rSq", r)])
            P.mm(ps[bm][:, :], ones_b[:, :], Zb[:, r, :], c == 0, c == 7, [("rZb", r), "ones"], [pk(bm)])
            P.mm(ps[bs][:, :], ones_b[:, :], Sq[:, r, :], c == 0, c == 7, [("rSq", r), "ones"], [pk(bs)])

        def ln_stats_fin(tt):
            Zb, Sq, mean, rstd, T1, V1 = ln_bufs()
            mean, rstd = mean[:, tt, :], rstd[:, tt, :]
            bm, bs = 6, 7
            P.ts(mean, ps[bm][:, :], 1.0 / D, None, ALU.mult, None, [pk(bm)], [("rmean", tt)])
            P.tt(rstd, mean, mean, ALU.mult, [("rmean", tt)], [("rrstd", tt)])
            P.stt(rstd, ps[bs][:, :], 1.0 / D, rstd, ALU.mult, ALU.subtract, [pk(bs), ("rrstd", tt)], [("rrstd", tt)])
            P.act(rstd, rstd, AF.Ln, [("rrstd", tt), "vecs"], [("rrstd", tt)], bias=V("epsln"), scale=1.0)
            P.act(rstd, rstd, AF.Exp, [("rrstd", tt)], [("rrstd", tt)], scale=-0.5)

        def ln_stats(tt):
            for c in range(8):
                ln_stats_chunk(tt, c)
            ln_stats_fin(tt)

        def ln_apply_chunk(tt, L, which, hmode, c):
            t0, tn = TT[tt]
            Zb, Sq, mean, rstd, T1, V1 = ln_bufs()
            mean, rstd = mean[:, tt, :], rstd[:, tt, :]
            j = TCOND[tt]
            r = c % 2
            xs = X[:, c, t0:t0 + tn]
            P.tt(T1[:, r, :], xs, mean, ALU.subtract, [("X", c, tt), ("rmean", tt)], [("rT1", r)])
            P.stt(V1[:, r, :], T1[:, r, :], V("ln_g", L, which)[:, c:c + 1], rstd, ALU.mult, ALU.mult, [("rT1", r), ("rrstd", tt), "vecs"], [("rV1", r)])
            P.act(xs, V1[:, r, :], AF.Identity, [("rV1", r), "vecs"], [("X", c, tt)], bias=V("ln_b", L, which)[:, c:c + 1], scale=1.0)
            if hmode == 1:
                P.act(H[:, c, t0:t0 + tn], V1[:, r, :], AF.Identity, [("rV1", r), ("DER", L, j)], [("H", c, tt)],
                      bias=DERv(L, j, 4)[:, c:c + 1], scale=DERv(L, j, 3)[:, c:c + 1])
            elif hmode == 2:
                P.act(H[:, c, t0:t0 + tn], V1[:, r, :], AF.Identity, [("rV1", r), ("DER", L + 1, j), ("DERN", L, j)], [("H", c, tt)],
                      bias=DERv(L, j, 6)[:, c:c + 1], scale=DERv(L + 1, j, 0)[:, c:c + 1])

        def ln_apply(tt, L, which, hmode):
            for c in range(8):
                ln_apply_chunk(tt, L, which, hmode, c)

        def resid(b, c, tt, L, kind):
            j = TCOND[tt]
            t0, tn = TT[tt]
            xs = X[:, c, t0:t0 + tn]
            P.stt(xs, ps[b][:, :tn], DERv(L, j, kind)[:, c:c + 1], xs, ALU.mult, ALU.add, [pk(b), ("X", c, tt), ("DER", L, j)], [("X", c, tt)])

        def ffn(L, pre_hook, mod_next, tail_mod=None, out_hook=None):
            A = Alloc()
            G = A(BF16, 11, NT)
            SA = A(BF16, 3, 512)
            assert A.off <= LNOFF
            state = {"nsa": 0, "mod": 0}

            def load_in(half, jj):
                js = [half * 11 + jj * 2 + s_ for s_ in range(2) if jj * 2 + s_ < 11]
                nj = len(js)
                j0 = js[0]
                items = []
                for ab in range(2):
                    src = w_ffn_in[L, :, ab * DFF + j0 * 128: ab * DFF + (j0 + nj) * 128].rearrange("(k p) n -> p k n", p=128)
                    items.append((lambda s_, ab=ab, nj=nj: s_.rearrange("p (k a n) -> p k a n", k=8, a=2)[:, :, ab, :nj * 128], src))
                slot, key = ring_load(items)
                return js, slot.rearrange("p (k a n) -> p k a n", k=8, a=2), key

            def p1(half, js, s4, key, tts):
                for si, jf in enumerate(js):
                    gi = jf - half * 11
                    for tt in tts:
                        t0, tn = TT[tt]
                        ba = bank([0, 1, 2, 3])
                        bb = bank([0, 1, 2, 3])
                        for ab, b in ((0, ba), (1, bb)):
                            for k in range(8):
                                P.mm(ps[b][:, :tn], s4[:, k, ab, si * 128:(si + 1) * 128], H[:, k, t0:t0 + tn], k == 0, k == 7, [key, ("H", k, tt)], [pk(b)])
                        r = state["nsa"] % 3
                        state["nsa"] += 1
                        P.act(SA[:, r, :], ps[ba][:, :], AF.Silu, [pk(ba)], [("rSA", r)])
                        P.tt(G[:, gi, t0:t0 + tn], SA[:, r, :], ps[bb][:, :], ALU.mult, [("rSA", r), pk(bb)], [("rG", gi, tt)])

            def mod_step():
                if mod_next is not None and state["mod"] < 12:
                    mod_item(mod_next, state["mod"])
                    state["mod"] += 1

            def load_out(half, cp_):
                items = []
                for cs in range(2):
                    c = cp_ * 2 + cs
                    src = w_ffn_out[L, half * 11 * 128:(half + 1) * 11 * 128, c * 128:(c + 1) * 128].rearrange("(g p) n -> p g n", p=128)
                    items.append((lambda s_, cs=cs: s_[:, cs * 1408:(cs + 1) * 1408].rearrange("p (g n) -> p g n", g=11), src))
                return ring_load(items)

            def p2(slot, key, cs, c, tt):
                w3 = slot[:, cs * 1408:(cs + 1) * 1408].rearrange("p (g n) -> p g n", g=11)
                t0, tn = TT[tt]
                b = bank([4, 5])
                for gi in range(11):
                    P.mm(ps[b][:, :tn], w3[:, gi, :], G[:, gi, t0:t0 + tn], gi == 0, gi == 10, [key, ("rG", gi, tt)], [pk(b)])
                resid(b, c, tt, L, 5)

            NPRO = 3
            pro = [load_in(0, jj) for jj in range(NPRO)]
            for tt in range(3):
                for ii, (js, s4, key) in enumerate(pro):
                    pre_hook(tt, ii)
                    p1(0, js, s4, key, [tt])
            for jj in range(NPRO, 6):
                js, s4, key = load_in(0, jj)
                p1(0, js, s4, key, [0, 1, 2])
                mod_step()
                mod_step()
            for cp_ in range(4):
                slot, key = load_out(0, cp_)
                for cs in range(2):
                    for tt in range(3):
                        p2(slot, key, cs, cp_ * 2 + cs, tt)
                if cp_ >= 2:
                    mod_step()
            for jj in range(6):
                js, s4, key = load_in(1, jj)
                p1(1, js, s4, key, [0, 1, 2])
                mod_step()
                if tail_mod is not None:
                    mod_item(tail_mod, jj, 6)
            if tail_mod is not None:
                mod_finish(tail_mod, 6, 0)
            while mod_next is not None and state["mod"] < 12:
                mod_step()
            if mod_next is not None:
                mod_finish(mod_next)
                derive(mod_next)
            if L < 3:
                derive_next(L)
            outs = [load_out(1, cp_) for cp_ in range(4)]
            hm = 2 if L < 3 else 0

            def p2tile(tt, hook=None):
                for cp_ in range(4):
                    slot, key = outs[cp_]
                    for cs in range(2):
                        p2(slot, key, cs, cp_ * 2 + cs, tt)
                        if hook is not None:
                            hook(cp_ * 2 + cs)
            p2tile(0)
            p2tile(1, lambda c: ln_stats_chunk(0, c))
            ln_stats_fin(0)

            def hk2(c):
                ln_stats_chunk(1, c)
                ln_apply_chunk(0, L, 1, hm, c)
            p2tile(2, hk2)
            ln_stats_fin(1)
            if out_hook is not None:
                out_hook(0)
            tm = list(range(6, 12)) if tail_mod is not None else []
            for c in range(8):
                ln_stats_chunk(2, c)
                ln_apply_chunk(1, L, 1, hm, c)
                if tm and c % 2 == 1:
                    mod_item(tail_mod, tm.pop(0), 5)
            ln_stats_fin(2)
            if out_hook is not None:
                out_hook(1)
            while tm:
                mod_item(tail_mod, tm.pop(0), 5)
            ln_apply(2, L, 1, hm)
            if out_hook is not None:
                out_hook(2)
            if tail_mod is not None:
                mod_finish(tail_mod, 5, 1)
                derive(tail_mod)

        def pair(ap2d, base, width, off, n):
            return ap2d[:, base:base + 2 * width].rearrange("p (s t) -> p s t", s=2)[:, :, off:off + n]

        def pool_mixer(L, mid=None):
            jw = L // 3
            P.barrier()
            A = Alloc()
            Wp = A(BF16, 4, 2, 256)
            Bt = A(BF16, 4, 12, 144)
            Zb = A(BF16, 12, 1024)
            assert A.off <= RBYTES, A.off
            for g in range(4):
                wload(Wp[:, g, :, :], w_pool[jw, g].rearrange("(k p) n -> p k n", p=128), ("rWp", g))
            for g in range(4):
                wload(Bt[:, g, :, :], pband_d[g], ("rBt", g))
            if L == 0:
                for tt in range(3):
                    j = TCOND[tt]
                    t0, tn = TT[tt]
                    for c in range(8):
                        P.act(H[:, c, t0:t0 + tn], X[:, c, t0:t0 + tn], AF.Identity, [("X", c, tt), ("DER", L, j)], [("H", c, tt)],
                              bias=DERv(L, j, 1)[:, c:c + 1], scale=DERv(L, j, 0)[:, c:c + 1])
            for i in range(12):
                tt = i // 4
                for half in range(2):
                    b = bank([0, 1, 2, 3])
                    for gg in range(2):
                        g = half * 2 + gg
                        for kc in range(2):
                            P.mm(ps[b][:, gg * 256:(gg + 1) * 256], H[:, 2 * g + kc, i * 128:(i + 1) * 128], Wp[:, g, kc, :], kc == 0, kc == 1,
                                 [("rWp", g), ("H", 2 * g + kc, tt)], [pk(b)])
                    if half:
                        P.act(Zb[:, i, 512:1024], ps[b][:, :], AF.Copy, [pk(b)], [("rZ", i, 1)])
                    else:
                        P.cp(Zb[:, i, 0:512], ps[b][:, :], [pk(b)], [("rZ", i, 0)])
            if mid is not None:
                mid()
            for c in range(8):
                g, dc = c // 2, c % 2
                for tt in range(3):
                    t0, tn = TT[tt]
                    contribs = []
                    for i in range(12):
                        si_ = 0 if i < 2 else (1 if i < 4 else 2)
                        so, T = SEQS[si_]
                        ls = i * 128 - so
                        lo = max(so + max(0, ls - 8), t0)
                        hi = min(so + min(T, ls + 136), t0 + tn)
                        if hi > lo:
                            contribs.append((i, lo - t0, hi - t0, (lo - so) - (ls - 8)))
                    b = bank([4, 5, 6, 7])
                    for idx, (i, clo, chi, bclo) in enumerate(contribs):
                        P.op(PE, "matmul", dict(out=ps[b][:, clo:chi], lhsT=Zb[:, i, g * 256 + dc * 128:g * 256 + (dc + 1) * 128], rhs=Bt[:, g, i, bclo:bclo + (chi - clo)],
                                                start=(idx == 0), stop=(idx == len(contribs) - 1), skip_group_check=True),
                             [("rZ", i, g // 2), ("rBt", g)], [pk(b)])
                    resid(b, c, tt, L, 2)

        def mla_small_weights():
            Aw = Alloc(76 * 1024)
            return Aw(BF16, 8, 384), Aw(BF16, 8, 384), Aw(BF16, 2, 1024), Aw(BF16, 2, 1024)

        def mla_prefetch():
            Wdq, Wdkv, Wuk, Wuv = mla_small_weights()
            kp = "(k p) n -> p k n"
            wload(Wdq, w_dq.rearrange(kp, p=128), ("rWdq",))
            wload(Wdkv[:, :, 0:320], w_dkv.rearrange(kp, p=128), ("rWdkv",))
            wload(Wdkv[:, :, 320:384], w_dkv_rp.rearrange(kp, p=128), ("rWdkv",), join=True)
            wload(Wuk, w_uk.rearrange(kp, p=128), ("rWuk",))
            wload(Wuv, w_uv.rearrange(kp, p=128), ("rWuv",))

        def mla_mixer(L, tile_hook=None):
            P.barrier()
            Wdq, Wdkv, Wuk, Wuv = mla_small_weights()
            A = Alloc()
            Wuq = A(BF16, 3, 2048)
            kp = "(k p) n -> p k n"
            wload(Wuq[:, :, 0:1536], w_uq.rearrange(kp, p=128), ("rWuq",))
            wload(Wuq[:, :, 1536:2048], w_uq_rp.rearrange(kp, p=128), ("rWuq",), join=True)
            NK = 1792
            QLn = A(BF16, 3, NT)
            CK = A(BF16, 2, NK)
            KR = A(BF16, NK)
            ROPE = A(F32, 2, 1024)
            PT = A(BF16, 4, 512)
            rden = A(F32, 2, 512)
            hb0 = [A(BF16, 1280), A(BF16, 10, 128), A(BF16, 1024), A(BF16, 1024), A(F32, 2, 512)]
            alias_off = A.off
            CKf = A(F32, 2, 512)
            KRf = A(F32, 512)
            SQ = A(BF16, 3, 512)
            rs = A(F32, 512)
            cst = A(F32, 2, 256)
            assert A.off <= 76 * 1024, A.off
            A2 = Alloc(alias_off)
            hb1 = [A2(BF16, 1280), A2(BF16, 10, 128), A2(BF16, 1024), A2(BF16, 1024), A2(F32, 2, 512)]
            assert A2.off <= A.off
            RT = hb0[4]
            P.memset(KR[64:128, :], 0.0, [("rKRpad",)])
            P.dma(SP, ROPE[0:64, :, :], rope_d, writes=[("rROPE",)])
            for t2 in range(2):
                P.dma(SP, cst[:, 0, :], cache_ckv[t2 * 128:(t2 + 1) * 128, :], writes=[("rcst", 0)])
                P.dma(SP, cst[:, 1, 0:64], cache_kr[t2 * 128:(t2 + 1) * 128, :], writes=[("rcst", 1)])
                for rc in range(2):
                    P.tr(ps[0][:, rc * 128:(rc + 1) * 128], cst[:, 0, rc * 128:(rc + 1) * 128], ident[:, :], [("rcst", 0), "ident"], [pk(0)])
                P.tr(ps[1][0:64, 0:128], cst[:, 1, 0:64], ident[:, :], [("rcst", 1), "ident"], [pk(1)])
                P.cp(CK[:, :, 512 + t2 * 128:512 + (t2 + 1) * 128], ps[0][:, 0:256].rearrange("p (a b) -> p a b", a=2), [pk(0)], [("rCK", 3 + t2)])
                P.cp(KR[0:64, 512 + t2 * 128:512 + (t2 + 1) * 128], ps[1][0:64, 0:128], [pk(1)], [("rKR", 3 + t2)])
            for tt in range(3):
                t0, tn = TT[tt]
                qb = [0, 1, 2]
                for rc in range(3):
                    for k in range(8):
                        P.mm(ps[qb[rc]][:, :], Wdq[:, k, rc * 128:(rc + 1) * 128], H[:, k, t0:t0 + 512], k == 0, k == 7, [("rWdq",), ("H", k, tt)], [pk(qb[rc])])
                    P.act(SQ[:, rc, :], ps[qb[rc]][:, :], AF.Square, [pk(qb[rc])], [("rSQ", rc)])
                for rc in range(3):
                    P.mm(ps[6][:, :], ones_b[:, :], SQ[:, rc, :], rc == 0, rc == 2, [("rSQ", rc), "ones"], [pk(6)])
                P.act(rs, ps[6][:, :], AF.Ln, [pk(6), "vecs"], [("rrs",)], bias=V("eps"), scale=1.0 / 384)
                P.act(rs, rs, AF.Exp, [("rrs",)], [("rrs",)], scale=-0.5)
                for rc in range(3):
                    P.stt(QLn[:, rc, t0:t0 + 512], ps[qb[rc]][:, :], V("g_q")[:, rc:rc + 1], rs, ALU.mult, ALU.mult, [pk(qb[rc]), ("rrs",), "vecs"], [("rQLn", tt)])
                kb = [3, 4]
                for rc in range(2):
                    for k in range(8):
                        P.mm(ps[kb[rc]][:, :], Wdkv[:, k, rc * 128:(rc + 1) * 128], H[:, k, t0:t0 + 512], k == 0, k == 7, [("rWdkv",), ("H", k, tt)], [pk(kb[rc])])
                    P.act(SQ[:, rc, :], ps[kb[rc]][:, :], AF.Square, [pk(kb[rc])], [("rSQ", rc)])
                for rc in range(2):
                    P.mm(ps[7][:, :], ones_b[:, :], SQ[:, rc, :], rc == 0, rc == 1, [("rSQ", rc), "ones"], [pk(7)])
                P.act(rs, ps[7][:, :], AF.Ln, [pk(7), "vecs"], [("rrs",)], bias=V("eps"), scale=1.0 / 256)
                P.act(rs, rs, AF.Exp, [("rrs",)], [("rrs",)], scale=-0.5)
                koff = 0 if tt == 0 else 768 + (tt - 1) * 512
                for rc in range(2):
                    if tt == 0:
                        P.stt(CKf[:, rc, :], ps[kb[rc]][:, :], V("g_kv")[:, rc:rc + 1], rs, ALU.mult, ALU.mult, [pk(kb[rc]), ("rrs",), "vecs"], [("rCKf",)])
                        P.act(CK[:, rc, 0:512], CKf[:, rc, :], AF.Copy, [("rCKf",)], [("rCK", 0)])
                    else:
                        P.stt(CK[:, rc, koff:koff + 512], ps[kb[rc]][:, :], V("g_kv")[:, rc:rc + 1], rs, ALU.mult, ALU.mult, [pk(kb[rc]), ("rrs",), "vecs"], [("rCK", tt)])
                for k in range(8):
                    P.mm(ps[5][0:64, :], Wdkv[:, k, 256:320], H[:, k, t0:t0 + 512], k == 0, k == 7, [("rWdkv",), ("H", k, tt)], [pk(5)])
                if tt == 0:
                    P.act(KRf[0:64, :], ps[5][0:64, :], AF.Copy, [pk(5)], [("rKRf",)])
                    P.cp(KR[0:64, 0:512], ps[5][0:64, :], [pk(5)], [("rKR", 0)])
                else:
                    s0 = (tt - 1) * 512
                    for k in range(8):
                        P.mm(ps[6][0:64, :], Wdkv[:, k, 320:384], H[:, k, t0:t0 + 512], k == 0, k == 7, [("rWdkv",), ("H", k, tt)], [pk(6)])
                    P.tt(RT[0:64, 0, :], ps[5][0:64, :], ROPE[0:64, 0, s0:s0 + 512], ALU.mult, [pk(5), ("rROPE",)], [("rRT", 0, 0)])
                    P.tt(RT[0:64, 1, :], ps[6][0:64, :], ROPE[0:64, 1, s0:s0 + 512], ALU.mult, [pk(6), ("rROPE",)], [("rRT", 0, 1)])
                    P.tt(KR[0:64, koff:koff + 512], RT[0:64, 0, :], RT[0:64, 1, :], ALU.add, [("rRT", 0, 0), ("rRT", 0, 1)], [("rKR", tt)])
                if tt == 0:
                    for t4 in range(4):
                        for rc in range(2):
                            P.tr(ps[2][:, rc * 128:(rc + 1) * 128], CKf[:, rc, t4 * 128:(t4 + 1) * 128], ident[:, :], [("rCKf",), "ident"], [pk(2)])
                        P.tr(ps[2][:, 256:320], KRf[0:64, t4 * 128:(t4 + 1) * 128], ident[0:64, 0:64], [("rKRf",), "ident"], [pk(2)])
                        P.act(cst[:, 0, :], ps[2][:, 0:256], AF.Copy, [pk(2)], [("rcst", 0)])
                        P.act(cst[:, 1, 0:64], ps[2][:, 256:320], AF.Copy, [pk(2)], [("rcst", 1)])
                        P.dma(SP, o_ckv[t4 * 128:(t4 + 1) * 128, :], cst[:, 0, :], reads=[("rcst", 0)])
                        P.dma(SP, o_kr[t4 * 128:(t4 + 1) * 128, :], cst[:, 1, 0:64], reads=[("rcst", 1)])
            P.barrier()
            for par, hb in enumerate((hb0, hb1)):
                P.memset(hb[3][64:128, :], 0.0, [("rQRpad", par)])
            jobs = [
                (0, 512, 0, 512, [0], False, [("rCK", 0)], [("rKR", 0)], 256),
                (512, 1024, 512, 1280, [1, 2], True, [("rCK", i) for i in (1, 2, 3, 4)], [("rKR", i) for i in (1, 2, 3, 4)], None),
            ]
            heads = [(job, h) for job in jobs for h in range(8)]
            cnt = {"pt": 0, "acc": 0}

            def kvq(i):
                (q0, nq, k0, nk, tts, rope, ckk, krk, blk), h = heads[i]
                par = i % 2
                KTh, Vh, QTh, QRh, RTb = (hb0, hb1)[par]
                nsc = nk // 128
                qtiles = [(q, min(512, nq - q)) for q in range(0, nq, 512)]
                qlk = [("rQLn", t) for t in tts]
                for ks in range(0, nk, 512):
                    kn = min(512, nk - ks)
                    b = bank([0, 1])
                    for rc in range(2):
                        P.mm(ps[b][:, :kn], Wuk[:, rc, h * 128:(h + 1) * 128], CK[:, rc, k0 + ks:k0 + ks + kn], rc == 0, rc == 1, [("rWuk",)] + ckk, [pk(b)])
                    P.act(KTh[:, ks:ks + kn], ps[b][:, :kn], AF.Copy, [pk(b)], [("rKTh", par)])
                for s4 in range(0, nsc, 4):
                    ns = min(4, nsc - s4)
                    b = bank([0, 1])
                    for si in range(ns):
                        sc = s4 + si
                        for rc in range(2):
                            P.mm(ps[b][:, si * 128:(si + 1) * 128], CK[:, rc, k0 + sc * 128:k0 + (sc + 1) * 128], Wuv[:, rc, h * 128:(h + 1) * 128], rc == 0, rc == 1, [("rWuv",)] + ckk, [pk(b)])
                    P.cp(Vh[:, s4:s4 + ns, :], ps[b][:, :ns * 128].rearrange("p (a b) -> p a b", a=ns), [pk(b)], [("rVh", par)])
                for (q, qn) in qtiles:
                    b = bank([0, 1])
                    for rc in range(3):
                        P.mm(ps[b][:, :qn], Wuq[:, rc, h * 192:h * 192 + 128], QLn[:, rc, q0 + q:q0 + q + qn], rc == 0, rc == 2, [("rWuq",)] + qlk, [pk(b)])
                    P.act(QTh[:, q:q + qn], ps[b][:, :qn], AF.Copy, [pk(b)], [("rQTh", par)])
                    b = bank([0, 1])
                    for rc in range(3):
                        P.mm(ps[b][0:64, :qn], Wuq[:, rc, h * 192 + 128:h * 192 + 192], QLn[:, rc, q0 + q:q0 + q + qn], rc == 0, rc == 2, [("rWuq",)] + qlk, [pk(b)])
                    if not rope:
                        P.act(QRh[0:64, q:q + qn], ps[b][0:64, :qn], AF.Copy, [pk(b)], [("rQRh", par)])
                    else:
                        b2 = bank([0, 1])
                        for rc in range(3):
                            P.mm(ps[b2][0:64, :qn], Wuq[:, rc, 1536 + h * 64:1536 + (h + 1) * 64], QLn[:, rc, q0 + q:q0 + q + qn], rc == 0, rc == 2, [("rWuq",)] + qlk, [pk(b2)])
                        P.tt(RTb[0:64, 0, :qn], ps[b][0:64, :qn], ROPE[0:64, 0, q:q + qn], ALU.mult, [pk(b), ("rROPE",)], [("rRT", par, 0)])
                        P.tt(RTb[0:64, 1, :qn], ps[b2][0:64, :qn], ROPE[0:64, 1, q:q + qn], ALU.mult, [pk(b2), ("rROPE",)], [("rRT", par, 1)])
                        P.tt(QRh[0:64, q:q + qn], RTb[0:64, 0, :qn], RTb[0:64, 1, :qn], ALU.add, [("rRT", par, 0), ("rRT", par, 1)], [("rQRh", par)])

            def att(i):
                (q0, nq, k0, nk, tts, rope, ckk, krk, blk), h = heads[i]
                par = i % 2
                KTh, Vh, QTh, QRh, RTb = (hb0, hb1)[par]
                nsc = nk // 128
                if blk is None:
                    qtiles = [(q, min(512, nq - q)) for q in range(0, nq, 512)]
                    seq = [(q, qn, sc, sc == 0, sc == nsc - 1, sc == nsc - 1, (q, qn)) for (q, qn) in qtiles for sc in range(nsc)]
                else:
                    per = blk // 128
                    seq = []
                    for sc in range(nsc):
                        q = (sc // per) * blk
                        seq.append((q, blk, sc, sc % per == 0, sc % per == per - 1, sc == nsc - 1, (0, nq)))

                def score(item):
                    q, qn, sc = item[0], item[1], item[2]
                    b = bank([2, 3])
                    P.mm(ps[b][:, :qn], KTh[:, sc * 128:(sc + 1) * 128], QTh[:, q:q + qn], True, False, [("rKTh", par), ("rQTh", par)], [pk(b)])
                    P.mm(ps[b][:, :qn], KR[:, k0 + sc * 128:k0 + (sc + 1) * 128], QRh[:, q:q + qn], False, True, krk + [("rQRh", par), ("rQRpad", par), ("rKRpad",)], [pk(b)])
                    r = cnt["pt"] % 4
                    cnt["pt"] += 1
                    P.act(PT[:, r, :qn], ps[b][:, :qn], AF.Exp, [pk(b)], [("rPT", r)], scale=MLA_SCALE)
                    return r

                rr = score(seq[0])
                newacc = True
                for idx, (q, qn, sc, first, last, fin, (fq, fqn)) in enumerate(seq):
                    r = rr
                    if idx + 1 < len(seq):
                        rr = score(seq[idx + 1])
                    if newacc:
                        ai = cnt["acc"] % 2
                        cnt["acc"] += 1
                        bo, bd = (4, 5) if ai == 0 else (6, 7)
                        newacc = False
                    cq = q - fq
                    P.mm(ps[bo][:, cq:cq + qn], Vh[:, sc, :], PT[:, r, :qn], first, last, [("rVh", par), ("rPT", r)], [pk(bo)])
                    P.mm(ps[bd][:, cq:cq + qn], ones_b[:, :], PT[:, r, :qn], first, last, ["ones", ("rPT", r)], [pk(bd)])
                    if fin:
                        P.act(rden[:, ai, :fqn], ps[bd][:, :fqn], AF.Ln, [pk(bd)], [("rrden", ai)])
                        P.act(rden[:, ai, :fqn], rden[:, ai, :fqn], AF.Exp, [("rrden", ai)], [("rrden", ai)], scale=-1.0)
                        wk = [("H", h, (q0 + fq) // 512)] + [("Hq", h, q0 + fq + o) for o in range(0, fqn, 256 if blk else 512)]
                        P.tt(H[:, h, q0 + fq:q0 + fq + fqn], ps[bo][:, :fqn], rden[:, ai, :fqn], ALU.mult, [pk(bo), ("rrden", ai)], wk)
                        newacc = True

            kvq(0)
            for i in range(len(heads)):
                if i + 1 < len(heads):
                    kvq(i + 1)
                att(i)
            wo = []
            for half in range(2):
                src = w_mla_o[half * 512:(half + 1) * 512, :].rearrange(kp, p=128)
                wo.append(ring_load([(lambda s_: s_.rearrange("p (k n) -> p k n", k=4), src)]))
            P.barrier()
            for tt in range(3):
                for c in range(8):
                    t0, tn = TT[tt]
                    b = bank([0, 1, 2, 3])
                    for h in range(8):
                        slot, key = wo[h // 4]
                        w3 = slot.rearrange("p (k n) -> p k n", k=4)
                        hq = [("Hq", h, 0), ("Hq", h, 256)] if tt == 0 else [("Hq", h, t0)]
                        P.mm(ps[b][:, :], w3[:, h % 4, c * 128:(c + 1) * 128], H[:, h, t0:t0 + 512], h == 0, h == 7, [key, ("H", h, tt)] + hq, [pk(b)])
                    resid(b, c, tt, L, 2)
                    if tile_hook is not None:
                        tile_hook(tt, c)

        def lru_mixer(L, tile_hook=None):
            P.barrier()
            A = Alloc()
            Wg = A(BF16, 8, 4, 128)
            def load_wg():
                first = True
                for d in range(2):
                    wload(Wg[:, :, d * 2 + 0, :], w_lru_a[d].rearrange("n c m -> c n m"), ("rWg",), join=not first)
                    first = False
                    wload(Wg[:, :, d * 2 + 1, :], w_lru_i[d].rearrange("n c m -> c n m"), ("rWg",), join=True)
            M = view(66 * 1024, BF16, 8, NT)
            UP = A(F32, NTP)
            UC = A(F32, NT)
            Y2s = [A(BF16, NT), A(BF16, NT), A(BF16, NT)]
            Abd = [A(F32, NT), A(F32, NT)]
            IGd = [A(F32, NT), A(F32, NT)]
            Tbd = [A(F32, NT), A(F32, NT)]
            assert A.off <= 66 * 1024, A.off
            A.off = 90 * 1024
            UCb = A(BF16, NT)
            YT = A(F32, 1, 512)
            SPv = A(F32, 2, 16)
            HB = A(F32, 2, 16)
            HS = A(F32, 32)
            assert A.off <= RBYTES, A.off
            P.act(SPv[:, 0, :], V("lam"), AF.Exp, ["vecs"], [("rSP",)], scale=-1.0)
            P.act(SPv[:, 0, :], SPv[:, 0, :], AF.Ln, [("rSP",), "vecs"], [("rSP",)], bias=V("one"), scale=1.0)
            P.ts(SPv[:, 1, :], SPv[:, 0, :], -8.0, None, ALU.mult, None, [("rSP",)], [("rSP2",)])
            P.ts(SPv[:, 0, :], SPv[:, 0, :], -4.0, None, ALU.mult, None, [("rSP",), ("rSP2",)], [("rSP",)])
            P.ts(HB[:, 0, :], vecs[:, VOFF["b_a"][0]:VOFF["b_a"][0] + 16], 0.5, None, ALU.mult, None, ["vecs"], [("rHB",)])
            P.ts(HB[:, 1, :], vecs[:, VOFF["b_i"][0]:VOFF["b_i"][0] + 16], 0.5, None, ALU.mult, None, ["vecs"], [("rHB",)])
            P.memset(UP, 0.0, [("rUP", 0), ("rUP", 1)])
            GRP = [0, 1, 1]
            TSEQ = [(0, 512), (512, 1024)]
            GT = [[0], [1, 2]]
            def FEa(n):
                par = n % 3
                Y2 = Y2s[par]
                items = []
                for uy in range(2):
                    src = w_lru_in[:, uy * D + n * 128: uy * D + (n + 1) * 128].rearrange("(k p) m -> p k m", p=128)
                    items.append((lambda s_, uy=uy: s_[:, 0:2048].rearrange("p (k a m) -> p k a m", k=8, a=2)[:, :, uy, :], src))
                slot, key = ring_load(items)
                w4 = slot[:, 0:2048].rearrange("p (k a m) -> p k a m", k=8, a=2)
                for uy in range(2):
                    for tt in range(3):
                        t0, tn = TT[tt]
                        b = bank([0, 1, 2, 3])
                        for k in range(8):
                            P.mm(ps[b][:, :], w4[:, k, uy, :], H[:, k, t0:t0 + 512], k == 0, k == 7, [key, ("H", k, tt)], [pk(b)])
                        if uy == 0:
                            if tt == 0:
                                P.act(pair(UP, 0, 288, PADW, 256), ps[b][:, :].rearrange("p (s t) -> p s t", s=2), AF.Copy, [pk(b)], [("rUP", 0)])
                            else:
                                o = POFF[2] + (tt - 1) * 512
                                P.act(UP[:, o:o + 512], ps[b][:, :], AF.Copy, [pk(b)], [("rUP", 1)])
                        else:
                            P.act(Y2[:, t0:t0 + 512], ps[b][:, :], AF.Gelu_apprx_tanh, [pk(b)], [("rY", par, tt)])

            def FEb(n):
                segs = [(lambda sh: pair(UP, 0, 288, PADW + sh, 256), pair(UC, 0, 256, 0, 256)),
                        (lambda sh: UP[:, POFF[2] + sh:POFF[2] + sh + 1024], UC[:, 512:1536])]
                for g, (pv, uv) in enumerate(segs):
                    P.ts(uv, pv(0), V("conv_w", 1)[:, n:n + 1], V("conv_b")[:, n:n + 1], ALU.mult, ALU.add, [("rUP", g), "vecs"], [("rUC", g)])
                    for kk in (0, 2, 3):
                        P.stt(uv, pv(kk - 1), V("conv_w", kk)[:, n:n + 1], uv, ALU.mult, ALU.add, [("rUP", g), ("rUC", g), "vecs"], [("rUC", g)])
                    if g == 1:
                        P.act(UCb[:, :], UC[:, :], AF.Copy, [("rUC", 0), ("rUC", 1)], [("rUCb", 0), ("rUCb", 1)])

            def GS(n):
                for d in range(2):
                    dn = d * 8 + n
                    for gate in range(2):
                        for tt in range(3):
                            t0, tn = TT[tt]
                            b = bank([4, 5, 6, 7])
                            P.mm(ps[b][:, :], Wg[:, n, d * 2 + gate, :], UCb[:, t0:t0 + 512], True, True, [("rWg",), ("rUCb", GRP[tt])], [pk(b)])
                            dst, dk = (Abd[d], "rAb") if gate == 0 else (IGd[d], "rIG")
                            P.act(dst[:, t0:t0 + 512], ps[b][:, :], AF.Tanh, [pk(b), ("rHB",)], [(dk, d, tt)], bias=HB[:, gate, dn:dn + 1], scale=0.5)
                for d in range(2):
                    ik = [("rIG", d, t) for t in range(3)]
                    P.stt(IGd[d][:, :], IGd[d][:, :], 1.0, UC[:, :], ALU.add, ALU.mult, ik + [("rUC", 0), ("rUC", 1)], ik)

            def BE_act(n):
                for d in range(2):
                    dn = d * 8 + n
                    ak = [("rAb", d, t) for t in range(3)]
                    tk = [("rTb", d, t) for t in range(3)]
                    P.act(Tbd[d][:, :], Abd[d][:, :], AF.Exp, ak + [("rSP2",)], tk, bias=SPv[:, 1, dn:dn + 1], scale=SPv[:, 1, dn:dn + 1])
                    P.act(Abd[d][:, :], Abd[d][:, :], AF.Exp, ak + [("rSP",)], ak, bias=SPv[:, 0, dn:dn + 1], scale=SPv[:, 0, dn:dn + 1])
                for d in range(2):
                    tk = [("rTb", d, t) for t in range(3)]
                    P.act(Tbd[d][:, :], Tbd[d][:, :], AF.Sqrt, tk + ["vecs"], tk, bias=V("quarter"), scale=-0.25)

            def BE_dve(n):
                for d in range(2):
                    tka = [("rTb", d, t) for t in range(3)]
                    ika = [("rIG", d, t) for t in range(3)]
                    P.tt(Tbd[d][:, :], Tbd[d][:, :], IGd[d][:, :], ALU.mult, tka + ika, tka)
                    for g, (a0, an) in enumerate(TSEQ):
                        ak = [("rAb", d, t) for t in GT[g]]
                        tk = [("rTb", d, t) for t in GT[g]]
                        for si, (so, sl) in enumerate(SEQS):
                            if (0 if si < 2 else 1) != g:
                                continue
                            init = 0.0 if si < 2 else V("state", d)[:, n:n + 1]
                            if d == 0:
                                P.scan(Tbd[d][:, so:so + sl], Abd[d][:, so:so + sl], Tbd[d][:, so:so + sl], init, ak + tk + ["vecs"], tk)
                            else:
                                P.scan(Tbd[d][:, so:so + sl][:, ::-1], Abd[d][:, so:so + sl][:, ::-1], Tbd[d][:, so:so + sl][:, ::-1], init, ak + tk + ["vecs"], tk)

            def BE_fin(n):
                par = n % 3
                Y2 = Y2s[par]
                for d in range(2):
                    tk = [("rTb", d, 0)]
                    c0 = d * 8 + n
                    src_ = Tbd[d][:, 255:512:256] if d == 0 else Tbd[d][:, 0:257:256]
                    P.act(HS[:, c0:c0 + 17:16], src_, AF.Copy, tk, [("rHS",)])
                k0_ = [("rTb", 0, t) for t in range(3)]
                k1_ = [("rTb", 1, t) for t in range(3)]
                P.tt(Tbd[0][:, :], Tbd[0][:, :], Tbd[1][:, :], ALU.add, k0_ + k1_, k0_)
                yk = [("rY", par, t) for t in range(3)]
                P.tt(M[:, n, :], Tbd[0][:, :], Y2[:, :], ALU.mult, k0_ + yk, [("M", n, 0), ("M", n, 1)])

            FEa(0)
            FEb(0)
            FEa(1)
            load_wg()
            GS(0)
            for n in range(8):
                BE_act(n)
                if n + 1 < 8:
                    FEb(n + 1)
                BE_dve(n)
                if n + 2 < 8:
                    FEa(n + 2)
                BE_fin(n)
                if n + 1 < 8:
                    GS(n + 1)
            P.tr(ps[0][0:32, 0:128], HS[:, 0:32], ident[:, :], [("rHS",), "ident"], [pk(0)])
            P.act(UC[0:32, 0:128], ps[0][0:32, 0:128], AF.Copy, [pk(0)], [("rUC", 0)])
            P.dma(SP, o_lru, UC[0:32, 0:128], reads=[("rUC", 0)])
            wo = []
            for half in range(2):
                src = w_lru_out[half * 512:(half + 1) * 512, :].rearrange("(k p) n -> p k n", p=128)
                wo.append(ring_load([(lambda s_: s_.rearrange("p (k n) -> p k n", k=4), src)]))
            P.barrier()
            for tt in range(3):
                for c in range(8):
                    t0, tn = TT[tt]
                    b = bank([0, 1, 2, 3])
                    for k in range(8):
                        slot, key = wo[k // 4]
                        w3 = slot.rearrange("p (k n) -> p k n", k=4)
                        P.mm(ps[b][:, :], w3[:, k % 4, c * 128:(c + 1) * 128], M[:, k, t0:t0 + 512], k == 0, k == 7, [key, ("M", k, GRP[tt])], [pk(b)])
                    resid(b, c, tt, L, 2)
                    if tile_hook is not None:
                        tile_hook(tt, c)

        for it in range(4):
            mod_item(0, it)
        mod_finish(0, 7, rng=(0, 16))
        derive(0, "a")

        def mod0_gate():
            mod_item(0, 4)
            mod_item(0, 5)
            mod_finish(0, 7, rng=(16, 24))
            derive(0, "g")
        for L in range(NLAYERS):
            kind = L % 3

            def tile_hook(tt, c, L=L):
                if tt == 1:
                    ln_stats_chunk(0, c)
                    if c == 7:
                        ln_stats_fin(0)
                elif tt == 2:
                    ln_stats_chunk(1, c)
                    ln_apply_chunk(0, L, 0, 1, c)
                    if c == 7:
                        ln_stats_fin(1)
            if kind == 0:
                pool_mixer(L, mid=(mod0_gate if L == 0 else None))
                P.barrier()
                ln_stats(0)
                ln_stats(1)
                if L == 0:
                    for it in range(6, 12):
                        mod_item(0, it, 5)
                    mod_finish(0, 5, 1)
                    derive(0, 1)
                ln_apply(0, L, 0, 1)
            elif kind == 1:
                mla_mixer(L, tile_hook)
            else:
                lru_mixer(L, tile_hook)

            def pre_hook(tt, ii, L=L):
                if (tt, ii) == (0, 2):
                    ln_apply(1, L, 0, 1)
                elif (tt, ii) == (1, 0):
                    ln_stats(2)
                elif (tt, ii) == (1, 2):
                    ln_apply(2, L, 0, 1)
            if L == 0:
                mla_prefetch()
            ffn(L, pre_hook, 1 if L == 0 else None, tail_mod=(L + 2 if L < 2 else None),
                out_hook=((lambda tt: emit_out(yout, (tt,))) if (L == 3 and not dbg) else None))
            if dbg:
                P.barrier()
                emit_out(dbg_out[L])
                P.barrier()
        if dbg:
            emit_out(yout)
        P.emit()
    return nc


NLAYERS = 4
_CACHE = {}


def _host_consts():
    ident = np.eye(128, dtype=np.float32)
    pband = np.zeros((4, 128, 12, 144), np.float32)
    for g, w in enumerate((2, 4, 8, 16)):
        for i in range(12):
            si_ = 0 if i < 2 else (1 if i < 4 else 2)
            so, T = SEQS[si_]
            ls = i * 128 - so
            tp = ls - 8 + np.arange(144)
            valid = (tp >= 0) & (tp < T)
            lo = np.clip(tp - w // 2, 0, T)
            hi = np.clip(tp + w - w // 2, 0, T)
            cnt = np.maximum(hi - lo, 1).astype(np.float32)
            tr = (ls + np.arange(128))[:, None]
            inw = (tr >= lo[None, :]) & (tr < hi[None, :])
            blk = inw.astype(np.float32) / cnt[None, :] - (tr == tp[None, :]).astype(np.float32)
            pband[g, :, i, :] = np.where(valid[None, :], blk, 0.0).astype(np.float32)
    t = np.arange(1024)
    rows = (t // 64).astype(np.float32)
    cols = (t % 64).astype(np.float32)
    inv = (np.float32(10000.0) ** (-np.arange(16, dtype=np.float32) / np.float32(16))).astype(np.float32)
    ang = np.stack([rows[:, None] * inv, cols[:, None] * inv], axis=1).astype(np.float32)
    cos, sin = np.cos(ang).astype(np.float32), np.sin(ang).astype(np.float32)
    rope = np.zeros((64, 2, 1024), np.float32)
    perm = np.zeros(64, np.int64)
    for a in range(2):
        for j in range(2):
            for f in range(16):
                i = a * 32 + j * 16 + f
                perm[i] = a * 32 + (1 - j) * 16 + f
                rope[i, 0] = cos[:, a, f]
                rope[i, 1] = -sin[:, a, f] if j == 0 else sin[:, a, f]
    return ident, pband, rope, perm


def _fm(a):
    a = np.asarray(a, np.float32)
    lead = a.shape[:-1]
    C = a.shape[-1] // 128
    a = a.reshape(lead + (C, 128))
    a = np.moveaxis(a, -1, 0)
    return np.ascontiguousarray(a.reshape(128, -1))


def kernel(x_prompt, x_sample, cache_mla_ckv, cache_mla_krope, state_lru, c, c_ctx,
           w_ada, b_ada, ln_g, ln_b, w_ffn_in, w_ffn_out, w_pool, pool_scale,
           w_dq, g_q, w_uq, w_dkv, g_kv, w_uk, w_uv, w_mla_o,
           w_lru_in, lru_conv_w, lru_conv_b, w_lru_a, b_lru_a, w_lru_i, b_lru_i,
           lru_lambda, w_lru_out, _dbg=False):
    f = lambda a: np.ascontiguousarray(np.asarray(a, dtype=np.float32))
    ident, pband, rope, perm = _host_consts()
    if ("nc", _dbg) not in _CACHE:
        _CACHE[("nc", _dbg)] = build_program(_dbg)
    nc = _CACHE[("nc", _dbg)]
    x_prompt, x_sample = f(x_prompt), f(x_sample)
    w_uq0 = f(w_uq)[0]
    w_dkv0 = f(w_dkv)[0]
    shared = {
        "ident": ident, "pband": pband, "rope": rope,
        "w_ada": f(w_ada), "w_ffn_in": f(w_ffn_in), "w_ffn_out": f(w_ffn_out), "w_pool": f(w_pool),
        "w_dq": f(w_dq)[0], "w_uq": np.ascontiguousarray(w_uq0.reshape(384, 1536)),
        "w_uq_rp": np.ascontiguousarray(w_uq0[:, :, 128:192][:, :, perm].reshape(384, 512)),
        "w_dkv": w_dkv0, "w_dkv_rp": np.ascontiguousarray(w_dkv0[:, 256:320][:, perm]),
        "w_uk": np.ascontiguousarray(f(w_uk)[0].reshape(256, 1024)), "w_uv": np.ascontiguousarray(f(w_uv)[0].reshape(256, 1024)),
        "w_mla_o": f(w_mla_o)[0], "w_lru_in": f(w_lru_in)[0], "w_lru_a": f(w_lru_a)[0], "w_lru_i": f(w_lru_i)[0],
        "w_lru_out": f(w_lru_out)[0],
    }
    one = np.ones((128, 1), np.float32)
    common = [_fm(f(b_ada)), _fm(f(ln_g)), _fm(f(ln_b)), _fm(f(pool_scale)), _fm(f(g_q)[0]), _fm(f(g_kv)[0]),
              _fm(f(lru_conv_w)[0]), _fm(f(lru_conv_b)[0]), _fm(f(b_lru_a)[0]), _fm(f(b_lru_i)[0]), _fm(f(lru_lambda)[0])]
    tail = [one * np.float32(EPS), one * np.float32(EPS_LN), one, one * np.float32(0.25)]
    in_maps = []
    for i in range(8):
        cond = np.stack([f(c_ctx), f(c)[i]], axis=0)
        condT = np.ascontiguousarray(cond.reshape(2, 8, 128).transpose(2, 1, 0).reshape(128, 16))
        vec = np.concatenate(common + [_fm(f(state_lru)[i, 0])] + tail, axis=1).astype(np.float32)
        assert vec.shape == (128, NV), vec.shape
        m = dict(shared)
        m.update({
            "xin": np.ascontiguousarray(np.concatenate([x_prompt[2 * i], x_prompt[2 * i + 1], x_sample[i]], axis=0)),
            "condT": condT, "vecs": np.ascontiguousarray(vec),
            "cache_ckv": f(cache_mla_ckv)[i, 0], "cache_kr": f(cache_mla_krope)[i, 0],
        })
        in_maps.append(m)
    res = run_bass_kernel_spmd(nc, in_maps, core_ids=list(range(8)))
    y_prompt = np.zeros((16, 256, D), np.float32)
    y_sample = np.zeros((8, 1024, D), np.float32)
    n_ckv = np.zeros((16, 1, 256, 256), np.float32)
    n_kr = np.zeros((16, 1, 256, 64), np.float32)
    n_lru = np.zeros((16, 1, 2, D), np.float32)
    for i in range(8):
        r = res.results[i]
        y_prompt[2 * i] = r["yout"][0:256]
        y_prompt[2 * i + 1] = r["yout"][256:512]
        y_sample[i] = r["yout"][512:]
        for s_ in range(2):
            n_ckv[2 * i + s_, 0] = r["o_ckv"][s_ * 256:(s_ + 1) * 256]
            n_kr[2 * i + s_, 0] = r["o_kr"][s_ * 256:(s_ + 1) * 256]
            n_lru[2 * i + s_, 0] = r["o_lru"][s_ * 16:(s_ + 1) * 16].reshape(2, D)
    if _dbg:
        kernel.dbg = [[res.results[i]["dbg%d" % k] for k in range(4)] for i in range(8)]
    return (y_prompt, y_sample, n_ckv, n_kr, n_lru)
```
